# Optimizing a Trainium2 kernel written in Bass

```python
import jax, jax.numpy as jnp
from jax import lax
import numpy as np

D_MODEL = 1024
BATCH = 2
SEQ = 8192
DEPTH = 2

CTX_LEN = 256
GRID_W = 64
D_BRANCH = D_MODEL // 2
N_EVEN = (DEPTH + 1) // 2
N_ODD = DEPTH // 2
POOL_WINDOWS = (2, 4, 8, 16)
POOL_GROUP = D_BRANCH // len(POOL_WINDOWS)
SHORT_CONV_W = 3
RWKV_HEAD = 64
RWKV_HEADS = D_BRANCH // RWKV_HEAD
D_DECAY_LORA = 32
D_AAA_LORA = 32
D_GATE_LORA = 96
CONF_CONV_W = 31
EVEN_IN = 6 * D_BRANCH
ODD_IN = 8 * D_BRANCH
NORM_EPS = 1e-6
LNX_EPS = 64e-5
CONF_LN_EPS = 1e-5

kernel_name = "hybrid_pool_shortconv_rwkv7_conformer_dit"


def rmsnorm(x, g):
    xf = x.astype(jnp.float32)
    y = xf * lax.rsqrt(jnp.mean(xf * xf, axis=-1, keepdims=True) + NORM_EPS)
    return (y * g.astype(jnp.float32)).astype(x.dtype)


def layernorm(x, g, b, eps):
    xf = x.astype(jnp.float32)
    mu = jnp.mean(xf, axis=-1, keepdims=True)
    var = jnp.mean(jnp.square(xf - mu), axis=-1, keepdims=True)
    y = (xf - mu) * lax.rsqrt(var + eps) * g.astype(jnp.float32) + b.astype(jnp.float32)
    return y.astype(x.dtype)


def adaln(cvec, w, b):
    m = jax.nn.silu(cvec) @ w + b
    return jnp.split(m[..., None, :], 3, axis=-1)


def modulate(x, g, shift, scale):
    return rmsnorm(x, g) * (1.0 + scale) + shift


def to_lines(h, layout):
    if layout == "seq":
        return h
    b, t, ch = h.shape
    rows = t // GRID_W
    g = h.reshape(b, rows, GRID_W, ch)
    if layout == "rows":
        return g.reshape(b * rows, GRID_W, ch)
    return g.transpose(0, 2, 1, 3).reshape(b * GRID_W, rows, ch)


def from_lines(y, layout, b):
    if layout == "seq":
        return y
    ch = y.shape[-1]
    if layout == "rows":
        return y.reshape(b, -1, ch)
    rows = y.shape[1]
    return y.reshape(b, GRID_W, rows, ch).transpose(0, 2, 1, 3).reshape(b, rows * GRID_W, ch)


def dwconv(u, w):
    k = w.shape[0]
    return lax.conv_general_dilated(
        u, w[:, None, :].astype(u.dtype), window_strides=(1,), padding=[(k // 2, k // 2)],
        dimension_numbers=("NWC", "WIO", "NWC"), feature_group_count=u.shape[-1])


def centred_mean(u, window):
    l = u.shape[1]
    cs = jnp.pad(jnp.cumsum(u.astype(jnp.float32), axis=1), ((0, 0), (1, 0), (0, 0)))
    t = jnp.arange(l)
    lo = jnp.clip(t - window // 2, 0, l)
    hi = jnp.clip(t - window // 2 + window, 0, l)
    s = jnp.take(cs, hi, axis=1) - jnp.take(cs, lo, axis=1)
    return (s / (hi - lo).astype(jnp.float32)[None, :, None]).astype(u.dtype)


def even_mix(h, w_in, w_out, pool_w, pool_scale, sconv_w, grid):
    b = h.shape[0]
    line = "rows" if grid else "seq"
    u_a, z_a, v_b, g_b, g_c, z_b = jnp.split(h @ w_in, 6, axis=-1)
    ua = to_lines(u_a, line)
    n, l, _ = ua.shape
    ug = ua.reshape(n, l, len(POOL_WINDOWS), POOL_GROUP)
    pooled = jnp.stack([centred_mean(ug[:, :, i], w) for i, w in enumerate(POOL_WINDOWS)], axis=2) - ug
    y_a = jnp.einsum("nlgc,gcd->nlgd", pooled, pool_w).reshape(n, l, D_BRANCH)
    y_a = from_lines(y_a, line, b) * pool_scale
    y_b = g_b * from_lines(dwconv(to_lines(g_c * v_b, line), sconv_w), line, b)
    y = jnp.concatenate([y_a * jax.nn.silu(z_a), y_b * jax.nn.silu(z_b)], axis=-1)
    return y @ w_out


def token_shift_mix(z, mu):
    zp = jnp.pad(z, ((0, 0), (1, 1), (0, 0)))
    nb = 0.5 * (zp[:, :-2] + zp[:, 2:])
    return z + (nb - z) * mu


def wkv_scan(r, w, k, v, a, bb, s0, reverse):
    def step(s, inp):
        r_t, w_t, k_t, v_t, a_t, b_t = inp
        sa = jnp.einsum("bhvk,bhk->bhv", s, a_t)
        s = s * w_t[:, :, None, :] + sa[..., None] * b_t[:, :, None, :] + v_t[..., None] * k_t[:, :, None, :]
        return s, jnp.einsum("bhvk,bhk->bhv", s, r_t)
    xs = tuple(jnp.moveaxis(z.astype(jnp.float32), 1, 0) for z in (r, w, k, v, a, bb))
    s_fin, o = lax.scan(step, s0, xs, reverse=reverse)
    return s_fin, jnp.moveaxis(o, 0, 1)


def rwkv_branch(u, r, k, v, mu, w0, w1, w2, a0, a1, a2, g1, g2, k_k, k_a, r_k, lnx_g, lnx_b, s0):
    b, l, _ = u.shape
    heads = lambda z: z.reshape(b, l, RWKV_HEADS, RWKV_HEAD)
    r = token_shift_mix(r, mu[0])
    k = token_shift_mix(k, mu[1])
    v = token_shift_mix(v, mu[2])
    uw = token_shift_mix(u, mu[3])
    ua = token_shift_mix(u, mu[4])
    ugt = token_shift_mix(u, mu[5])
    g = jax.nn.sigmoid(ugt @ g1) @ g2
    kk = heads(k * k_k).astype(jnp.float32)
    kk = kk / jnp.maximum(jnp.sqrt(jnp.sum(kk * kk, axis=-1, keepdims=True)), 1e-12)
    r_h = heads(r).astype(jnp.float32)
    v_h = heads(v).astype(jnp.float32)
    outs, finals, k_dirs = [], [], []
    for d in range(2):
        logw = -jax.nn.softplus(-(w0[d] + jnp.tanh(uw @ w1[d]) @ w2[d])) - 0.5
        decay = jnp.exp(-jnp.exp(logw.astype(jnp.float32)))
        a = jax.nn.sigmoid(a0[d] + (ua @ a1[d]) @ a2[d])
        k_d = heads(k * (1.0 + (a - 1.0) * k_a)).astype(jnp.float32)
        a_h = heads(a).astype(jnp.float32)
        init = jnp.zeros((b, RWKV_HEADS, RWKV_HEAD, RWKV_HEAD), jnp.float32) if s0 is None else s0[d]
        s_fin, o = wkv_scan(r_h, heads(decay), k_d, v_h, -kk, kk * a_h, init, reverse=(d == 1))
        outs.append(o)
        finals.append(s_fin)
        k_dirs.append(k_d)
    o = outs[0] + outs[1]
    mu_o = jnp.mean(o, axis=-1, keepdims=True)
    var_o = jnp.mean(jnp.square(o - mu_o), axis=-1, keepdims=True)
    o = ((o - mu_o) * lax.rsqrt(var_o + LNX_EPS)).reshape(b, l, D_BRANCH)
    o = o * lnx_g.astype(jnp.float32) + lnx_b.astype(jnp.float32)
    bonus = jnp.sum(r_h * (k_dirs[0] + k_dirs[1]) * r_k.astype(jnp.float32), axis=-1, keepdims=True) * v_h
    y = (o + bonus.reshape(b, l, D_BRANCH)) * g.astype(jnp.float32)
    return y.astype(u.dtype), (finals[0], finals[1])


def conformer_branch(p1, p2, dw_w, dw_b, ln_g, ln_b, line):
    b = p1.shape[0]
    h = p1 * jax.nn.sigmoid(p2)
    h = from_lines(dwconv(to_lines(h, line), dw_w), line, b) + dw_b
    return jax.nn.silu(layernorm(h, ln_g, ln_b, CONF_LN_EPS))


def odd_merge(y_c, z_c, p1, p2, z_d, w_out, dw_w, dw_b, ln_g, ln_b, grid):
    y_d = conformer_branch(p1, p2, dw_w, dw_b, ln_g, ln_b, "cols" if grid else "seq")
    y = jnp.concatenate([y_c * jax.nn.silu(z_c), y_d * jax.nn.silu(z_d)], axis=-1)
    return y @ w_out


def setup_inputs(seed: int = 0) -> dict:
    key = jax.random.key(seed)
    ks = iter(jax.random.split(key, 48))
    nrm = lambda shape, s: jax.random.normal(next(ks), shape, jnp.float32) * s
    uni = lambda shape, lo, hi: jax.random.uniform(next(ks), shape, jnp.float32, lo, hi)
    D, DB, ne, no = D_MODEL, D_BRANCH, N_EVEN, N_ODD
    return {
        "x": nrm((BATCH, SEQ, D), 1.0),
        "c": nrm((BATCH, D), 1.0),
        "ctx": nrm((BATCH, CTX_LEN, D), 1.0),
        "c_ctx": nrm((D,), 1.0),
        "ada_w_e": nrm((ne, D, 3 * D), 0.5 * D ** -0.5),
        "ada_b_e": nrm((ne, 3 * D), 0.01),
        "norm_e": 1.0 + nrm((ne, D), 0.02),
        "in_e": nrm((ne, D, EVEN_IN), D ** -0.5),
        "out_e": nrm((ne, 2 * DB, D), (2 * DB) ** -0.5),
        "pool_w": nrm((ne, len(POOL_WINDOWS), POOL_GROUP, POOL_GROUP), POOL_GROUP ** -0.5),
        "pool_scale": 1.0 + nrm((ne, DB), 0.02),
        "sconv_w": nrm((ne, SHORT_CONV_W, DB), SHORT_CONV_W ** -0.5),
        "ada_w_o": nrm((no, D, 3 * D), 0.5 * D ** -0.5),
        "ada_b_o": nrm((no, 3 * D), 0.01),
        "norm_o": 1.0 + nrm((no, D), 0.02),
        "in_o": nrm((no, D, ODD_IN), D ** -0.5),
        "out_o": nrm((no, 2 * DB, D), (2 * DB) ** -0.5),
        "rwkv_mu": uni((no, 6, DB), 0.0, 1.0),
        "w0": uni((no, 2, DB), -6.5, -1.5),
        "w1": nrm((no, 2, DB, D_DECAY_LORA), DB ** -0.5),
        "w2": nrm((no, 2, D_DECAY_LORA, DB), 0.1 * D_DECAY_LORA ** -0.5),
        "a0": nrm((no, 2, DB), 0.1),
        "a1": nrm((no, 2, DB, D_AAA_LORA), DB ** -0.5),
        "a2": nrm((no, 2, D_AAA_LORA, DB), 0.1 * D_AAA_LORA ** -0.5),
        "g1": nrm((no, DB, D_GATE_LORA), DB ** -0.5),
        "g2": nrm((no, D_GATE_LORA, DB), D_GATE_LORA ** -0.5),
        "k_k": 0.85 + nrm((no, DB), 0.05),
        "k_a": 1.0 + nrm((no, DB), 0.05),
        "r_k": nrm((no, RWKV_HEADS, RWKV_HEAD), 0.1),
        "lnx_g": 1.0 + nrm((no, DB), 0.02),
        "lnx_b": nrm((no, DB), 0.01),
        "conf_dw_w": nrm((no, CONF_CONV_W, DB), CONF_CONV_W ** -0.5),
        "conf_dw_b": nrm((no, DB), 0.01),
        "conf_ln_g": 1.0 + nrm((no, DB), 0.02),
        "conf_ln_b": nrm((no, DB), 0.01),
        "final_g": 1.0 + nrm((D,), 0.02),
    }


def reference(x, c, ctx, c_ctx,
              ada_w_e, ada_b_e, norm_e, in_e, out_e, pool_w, pool_scale, sconv_w,
              ada_w_o, ada_b_o, norm_o, in_o, out_o, rwkv_mu, w0, w1, w2, a0, a1, a2, g1, g2,
              k_k, k_a, r_k, lnx_g, lnx_b, conf_dw_w, conf_dw_b, conf_ln_g, conf_ln_b,
              final_g):
    xc = ctx
    for i in range(DEPTH):
        j = i // 2
        need_ctx = i < DEPTH - 1
        if i % 2 == 0:
            sh, sc, gt = adaln(c, ada_w_e[j], ada_b_e[j])
            h = modulate(x, norm_e[j], sh, sc)
            y = even_mix(h, in_e[j], out_e[j], pool_w[j], pool_scale[j], sconv_w[j], grid=True)
            if need_ctx:
                shc, scc, gtc = adaln(c_ctx, ada_w_e[j], ada_b_e[j])
                hc = modulate(xc, norm_e[j], shc, scc)
                xc = xc + gtc * even_mix(hc, in_e[j], out_e[j], pool_w[j], pool_scale[j], sconv_w[j], grid=False)
            x = x + gt * y
        else:
            rp = (rwkv_mu[j], w0[j], w1[j], w2[j], a0[j], a1[j], a2[j], g1[j], g2[j],
                  k_k[j], k_a[j], r_k[j], lnx_g[j], lnx_b[j])
            cp = (out_o[j], conf_dw_w[j], conf_dw_b[j], conf_ln_g[j], conf_ln_b[j])
            shc, scc, gtc = adaln(c_ctx, ada_w_o[j], ada_b_o[j])
            hc = modulate(xc, norm_o[j], shc, scc)
            uc, rc, kc, vc, zcc, p1c, p2c, zdc = jnp.split(hc @ in_o[j], 8, axis=-1)
            yc_rwkv, ctx_states = rwkv_branch(uc, rc, kc, vc, *rp, None)
            sh, sc, gt = adaln(c, ada_w_o[j], ada_b_o[j])
            h = modulate(x, norm_o[j], sh, sc)
            u, r, k, v, zc, p1, p2, zd = jnp.split(h @ in_o[j], 8, axis=-1)
            y_rwkv, _ = rwkv_branch(u, r, k, v, *rp, ctx_states)
            y = odd_merge(y_rwkv, zc, p1, p2, zd, *cp, grid=True)
            if need_ctx:
                xc = xc + gtc * odd_merge(yc_rwkv, zcc, p1c, p2c, zdc, *cp, grid=False)
            x = x + gt * y
    return rmsnorm(x, final_g)
```

```python
import numpy as np
from contextlib import ExitStack
import concourse.bass as bass
import concourse.mybir as mybir
from concourse.bass_utils import run_bass_kernel_spmd

F32 = mybir.dt.float32
BF16 = mybir.dt.bfloat16
ALU = mybir.AluOpType
AF = mybir.ActivationFunctionType

D = 1024
DB = 512
CTX = 256
CH = 128
CTX0 = 2
LAT0 = CTX0 + CTX + 2
GPAD = 960
EXPM05 = float(np.exp(-0.5))


class KB:
    def __init__(self, nc, es):
        self.nc = nc
        self.es = es
        self.eng = {"pe": nc.tensor, "act": nc.scalar, "dve": nc.vector, "pool": nc.gpsimd, "sp": nc.sync}
        self.sem = {}
        self.cnt = {}
        for e in self.eng:
            self.sem[e] = es.enter_context(nc.semaphore("sem_" + e))
            self.cnt[e] = 0
        self.dsem = {}
        self.dcnt = {}
        self.waited = {e: {} for e in self.eng}
        self.lastw = {}
        self.reads = {}
        self.psf = [es.enter_context(nc.psum_tensor(f"psf{i}", [128, 512], F32)) for i in range(8)]
        self.psf_i = 0
        self.psb_i = 0
        self.ndsem = 0

    def sb(self, es, name, shape, dt):
        self.nsb = getattr(self, "nsb", 0) + 1
        return es.enter_context(self.nc.sbuf_tensor("sb%d_%s" % (self.nsb, name), shape, dt))

    def pf(self):
        i = self.psf_i % 6
        self.psf_i += 1
        return self.psf[i], f"psf{i}"

    def pfx(self, i):
        return self.psf[i], f"psf{i}"

    def pb(self):
        i = 6 + self.psb_i % 2
        self.psb_i += 1
        return self.psf[i][:].bitcast(BF16), f"psf{i}"

    def _wait(self, e, ev):
        if ev is None:
            return
        semkey, val = ev
        if self.waited[e].get(semkey, 0) >= val:
            return
        sem = self.sem[semkey] if semkey in self.sem else self.dsem[semkey]
        self.eng[e].wait_ge(sem, val)
        self.waited[e][semkey] = val

    def _deps(self, e, reads, writes, sync_same):
        for k in reads:
            for sk, v in self.lastw.get(k, {}).items():
                if sync_same or sk != e:
                    self._wait(e, (sk, v))
            if k.startswith("ps"):
                for sk, v in self.reads.get(k, {}).items():
                    if sk != e:
                        self._wait(e, (sk, v))
        for k in writes:
            for sk, v in self.lastw.get(k, {}).items():
                if sync_same or sk != e:
                    self._wait(e, (sk, v))
            for sk, v in self.reads.get(k, {}).items():
                if sync_same or sk != e:
                    self._wait(e, (sk, v))

    def _record(self, ev, reads, writes):
        sk, v = ev
        for k in writes:
            self.lastw.setdefault(k, {})[sk] = v
        for k in reads:
            self.reads.setdefault(k, {})[sk] = v

    budget = None
    nops = 0

    def op(self, e, fn, reads=(), writes=()):
        self.nops += 1
        if self.budget is not None and self.nops > self.budget:
            return
        self._deps(e, reads, writes, e != "pe")
        inst = fn(self.eng[e])
        self.cnt[e] += 1
        inst.then_inc(self.sem[e], 1)
        self._record((e, self.cnt[e]), reads, writes)

    def dma(self, q, out, in_, reads=(), writes=(), key=None):
        self.nops += 1
        if self.budget is not None and self.nops > self.budget and not str(key).startswith("dbg"):
            return
        key = key or (list(writes) + list(reads))[0]
        skey = "d_" + str(key)
        if skey not in self.dsem:
            self.dsem[skey] = self.es.enter_context(self.nc.semaphore("ds%d" % self.ndsem))
            self.ndsem += 1
            self.dcnt[skey] = 0
        self._deps(q, reads, writes, True)
        self.dcnt[skey] += 16
        self.eng[q].dma_start(out=out, in_=in_).then_inc(self.dsem[skey], 16)
        self._record((skey, self.dcnt[skey]), reads, writes)

    def barrier(self):
        evs = [(e, self.cnt[e]) for e in self.eng if self.cnt[e] > 0] + [(k, v) for k, v in self.dcnt.items() if v > 0]
        for e in self.eng:
            for ev in evs:
                if ev[0] != e:
                    self._wait(e, ev)

    def finish(self, keys):
        for k in keys:
            for sk, v in self.lastw.get(k, {}).items():
                self._wait("sp", (sk, v))
            for sk, v in self.reads.get(k, {}).items():
                self._wait("sp", (sk, v))


CP = {}
_off = 0
for _n, _w in [("pool_scale", 4), ("sconv_w", 12), ("mu", 24), ("k_k", 4), ("k_a", 4), ("r_k", 4),
               ("lnx_g", 4), ("lnx_b", 4), ("conf_dw_b", 4), ("conf_ln_g", 4), ("conf_ln_b", 4),
               ("w0", 8), ("a0", 8), ("conf_dw_w", 124)]:
    CP[_n] = _off
    _off += _w
NCP = _off


def _cols(p):
    p = np.asarray(p, np.float32).reshape(-1, 4, 128)
    return np.ascontiguousarray(p.transpose(2, 0, 1).reshape(128, -1))


def host_consts():
    s = np.arange(128)[:, None]
    t = np.arange(128)[None, :]
    b32 = (s // 32) == (t // 32)
    b64 = (s // 64) == (t // 64)
    ml = []
    for st, inc in (((s < t), (s <= t)), ((s > t), (s >= t))):
        ml += [st & b32, st & b64 & ~b32, st & ~b64, inc, st, inc]
    masks = np.concatenate(ml, axis=1).astype(np.float32)
    blk = ((s // 64) == (t // 64)).astype(np.float32)

    def invc(l):
        tt = np.arange(l)
        rows = []
        for w in (2, 4, 8, 16):
            lo = np.clip(tt - w // 2, 0, l)
            hi = np.clip(tt - w // 2 + w, 0, l)
            rows.append(1.0 / (hi - lo).astype(np.float32))
        r = np.concatenate(rows).astype(np.float32)
        return np.ascontiguousarray(np.broadcast_to(r[None, :], (128, r.size)))

    return {
        "ident": np.eye(128, dtype=np.float32),
        "masks": masks,
        "blk": blk,
        "invc_x": invc(64),
        "invc_c": invc(256),
    }


def build(SEQ, dump=()):
    assert SEQ % 512 == 0
    NT = SEQ // 512
    NTAU = CTX + SEQ
    NCOL = LAT0 + SEQ + 2
    nc = bass.Bass("TRN2", target_bir_lowering=False)

    def din(name, shape, dt=F32):
        return nc.dram_tensor(name, shape, dt, kind="ExternalInput").ap()

    def dscr(name, shape, dt):
        kind = "ExternalOutput" if name in dump else "Internal"
        return nc.dram_tensor(name, shape, dt, kind=kind).ap()

    x_in = din("x", [SEQ, D])
    ctx_in = din("ctx", [CTX, D])
    ccol_in = din("ccol", [128, 16])
    ada_w = [din("ada_w_e", [D, 3 * D]), din("ada_w_o", [D, 3 * D])]
    ada_b = [din("ada_b_e", [1, 3 * D]), din("ada_b_o", [1, 3 * D])]
    norm_g = [din("norm_e", [1, D]), din("norm_o", [1, D])]
    in_e = din("in_e", [D, 3072])
    out_e = din("out_e", [D, D])
    in_o = din("in_o", [D, 4096])
    out_o = din("out_o", [D, D])
    pool_w = din("pool_w", [128, 512])
    cp_in = din("cp", [128, NCP])
    w1l = din("w1l", [128, 256])
    a1l = din("a1l", [128, 256])
    g1l = din("g1l", [128, 384])
    w2l = din("w2l", [64, 512])
    a2l = din("a2l", [64, 512])
    g2l = din("g2l", [96, 512])
    final_g = din("final_g", [1, D])
    ident_in = din("ident", [128, 128])
    masks_in = din("masks", [128, 1536])
    blk_in = din("blk", [128, 128])
    invcx_in = din("invc_x", [128, 256])
    invcc_in = din("invc_c", [128, 1024])
    out = nc.dram_tensor("out", [SEQ, D], F32, kind="ExternalOutput").ap()

    X1 = dscr("X1", [SEQ, D], F32)
    XC1 = dscr("XC1", [CTX, D], F32)
    S_u = dscr("S_u", [DB, NCOL], BF16)
    S_r = dscr("S_r", [DB, NCOL], BF16)
    S_k = dscr("S_k", [DB, NCOL], BF16)
    S_v = dscr("S_v", [DB, NCOL], BF16)
    S_zc = dscr("S_zc", [DB, SEQ], BF16)
    S_zd = dscr("S_zd", [DB, SEQ], BF16)
    S_glu = dscr("S_glu", [DB, SEQ + 2 * GPAD], BF16)
    S_sw = [dscr("S_sw0", [DB, NTAU], F32), dscr("S_sw1", [DB, NTAU], F32)]
    S_a = [dscr("S_a0", [DB, NTAU], BF16), dscr("S_a1", [DB, NTAU], BF16)]
    S_g = dscr("S_g", [DB, NTAU], BF16)
    S_of = dscr("S_of", [SEQ, DB], F32)
    S_y = dscr("S_y", [D, SEQ], BF16)

    with ExitStack() as es:
        kb = KB(nc, es)
        op = kb.op

        identf = kb.sb(es, "identf", [128, 128], F32)
        identb = kb.sb(es, "identb", [128, 128], BF16)
        blk = kb.sb(es, "blk", [128, 128], F32)
        onesf = kb.sb(es, "onesf", [128, 128], F32)
        onesb = kb.sb(es, "onesb", [128, 128], BF16)
        cp = kb.sb(es, "cp", [128, NCP], F32)
        ccol = kb.sb(es, "ccol", [128, 16], F32)
        zb = kb.sb(es, "zb", [128, GPAD], BF16)
        kb.dma("sp", identf[:], ident_in, writes=["identf"])
        kb.dma("sp", blk[:], blk_in, writes=["blk"])
        kb.dma("sp", cp[:], cp_in, writes=["cp"])
        kb.dma("sp", ccol[:], ccol_in, writes=["ccol"])
        op("dve", lambda e: e.tensor_copy(out=identb[:], in_=identf[:]), reads=["identf"], writes=["identb"])
        op("dve", lambda e: e.memset(onesf[:], 1.0), writes=["onesf"])
        op("dve", lambda e: e.memset(onesb[:], 1.0), writes=["onesb"])
        op("dve", lambda e: e.memset(zb[:], 0.0), writes=["zb"])
        for S in (S_u, S_r, S_k, S_v):
            for c0 in (0, LAT0 - 2, NCOL - 2):
                kb.dma("pool", S[:, c0:c0 + 2].rearrange("(c p) n -> p c n", p=128),
                       zb[:, 0:8].rearrange("p (c n) -> p c n", c=4), reads=["zb"], writes=["Shalo"], key="Shalo")
        for c0 in (0, GPAD + SEQ):
            for c in range(4):
                kb.dma("pool", S_glu[c * 128:(c + 1) * 128, c0:c0 + GPAD], zb[:, :], reads=["zb"],
                       writes=["Sgluhalo"], key="Shalo")

        def cpc(name, i=0):
            o = CP[name] + i
            return cp[:, o:o + 1]

        def load_cast(es_, name, src, rows, cols_list, dst_shape, view):
            dst = kb.sb(es_, name, dst_shape, BF16)
            return dst

        cm = {}

        def alloc_common(stk):
            cm["stage"] = [kb.sb(stk, f"stage{i}", [128, 1024], F32) for i in range(2)]
            cm["tmpm"] = [kb.sb(stk, f"tmpm{i}", [128, 1024], F32) for i in range(2)]
            cm["junk"] = kb.sb(stk, "junk", [128, 1024], BF16)

        stage_i = [0]

        def load_w_bf16(dst, src_ap, kchunks, ncols, dkey):
            step = 1024
            for k in range(kchunks):
                for n0 in range(0, ncols, step):
                    n1 = min(ncols, n0 + step)
                    i = stage_i[0] % 2
                    stage_i[0] += 1
                    st = cm["stage"][i]
                    kb.dma("sp", st[:, :n1 - n0], src_ap[k * 128:(k + 1) * 128, n0:n1], writes=[f"stage{i}"])
                    eng = "act" if (stage_i[0] % 2) else "dve"
                    if eng == "act":
                        op("act", lambda e: e.activation(out=dst[:, k, n0:n1], in_=st[:, :n1 - n0], func=AF.Copy),
                           reads=[f"stage{i}"], writes=[dkey])
                    else:
                        op("dve", lambda e: e.tensor_copy(out=dst[:, k, n0:n1], in_=st[:, :n1 - n0]),
                           reads=[f"stage{i}"], writes=[dkey])

        def load_small_bf16(dst2d, src_ap, rows, ncols, dkey):
            i = stage_i[0] % 2
            stage_i[0] += 1
            st = cm["stage"][i]
            kb.dma("sp", st[:rows, :ncols], src_ap, writes=[f"stage{i}"])
            op("dve", lambda e: e.tensor_copy(out=dst2d, in_=st[:rows, :ncols]), reads=[f"stage{i}"], writes=[dkey])

        ss = kb.sb(es, "ss", [128, 4], F32)
        rs = kb.sb(es, "rs", [128, 4], F32)
        tmpm_i = [0]

        def norm_mod_T(xt, xkey, nsub, A, B, akeys, hb, hT, tag):
            junk, tmpm = cm["junk"], cm["tmpm"]
            for j in range(nsub):
                op("act", lambda e: e.activation(out=junk[:], in_=xt[:, j, :], func=AF.Square,
                                                 accum_out=ss[:, j:j + 1]),
                   reads=[xkey], writes=["junk", "ss"])
            op("dve", lambda e: e.tensor_scalar(out=rs[:, :nsub], in0=ss[:, :nsub], scalar1=1.0 / D, scalar2=1e-6,
                                                op0=ALU.mult, op1=ALU.add), reads=["ss"], writes=["rs"])
            op("act", lambda e: e.activation(out=rs[:, :nsub], in_=rs[:, :nsub], func=AF.Sqrt), reads=["rs"],
               writes=["rs"])
            op("dve", lambda e: e.reciprocal(out=rs[:, :nsub], in_=rs[:, :nsub]), reads=["rs"], writes=["rs"])
            for j in range(nsub):
                i = tmpm_i[0] % 2
                tmpm_i[0] += 1
                tm = tmpm[i]
                op("dve", lambda e: e.scalar_tensor_tensor(out=tm[:], in0=xt[:, j, :], scalar=rs[:, j:j + 1], in1=A[:],
                                                           op0=ALU.mult, op1=ALU.mult),
                   reads=[xkey, "rs"] + akeys, writes=[f"tmpm{i}"])
                op("dve", lambda e: e.tensor_tensor(out=hb[:, j, :], in0=tm[:], in1=B[:], op=ALU.add),
                   reads=[f"tmpm{i}"] + akeys, writes=[f"hb{j}"])
            for c in range(8):
                pt, pk = kb.pb()
                for j in range(nsub):
                    op("pe", lambda e: e.transpose(out=pt[:, j * 128:(j + 1) * 128], in_=hb[:, j, c * 128:(c + 1) * 128],
                                                   identity=identb[:]),
                       reads=[f"hb{j}", "identb"], writes=[pk])
                eng = "act" if c % 2 == 0 else "dve"
                if eng == "act":
                    op("act", lambda e: e.activation(out=hT[:, c, :nsub * 128], in_=pt[:, :nsub * 128], func=AF.Copy),
                       reads=[pk], writes=[f"{tag}{c}"])
                else:
                    op("dve", lambda e: e.tensor_copy(out=hT[:, c, :nsub * 128], in_=pt[:, :nsub * 128]),
                       reads=[pk], writes=[f"{tag}{c}"])

        def mm(out_ap, lhsT, rhs, start, stop, reads, writes):
            op("pe", lambda e: e.matmul(out_ap, lhsT=lhsT, rhs=rhs, start=start, stop=stop), reads=reads, writes=writes)

        def adaln(es_, layer, want_gate_ctx, es_ctx=None, pre=None):
            t = dict(pre or {})
            for nme in ("A", "B", "G", "Ac", "Bc") + (("Gc",) if want_gate_ctx else ()):
                if nme in t:
                    continue
                stk = es_ctx if (es_ctx is not None and nme != "G") else es_
                t[nme] = kb.sb(stk, f"mod{layer}{nme}", [128, D], F32)
            with ExitStack() as sub:
                sil = kb.sb(sub, "sil", [128, 16], F32)
                sbc = kb.sb(sub, "sbc", [128, 16, 128], F32)
                brow = kb.sb(sub, "brow", [1, 3 * D], F32)
                gbc = kb.sb(sub, "gbc", [128, 1, D], F32)
                wts = [kb.sb(sub, f"adaw{i}", [128, 512], F32) for i in range(3)]
                op("act", lambda e: e.activation(out=sil[:], in_=ccol[:], func=AF.Silu), reads=["ccol"], writes=["sil"])
                for i in range(16):
                    op("dve", lambda e: e.tensor_scalar(out=sbc[:, i, :], in0=onesf[:], scalar1=sil[:, i:i + 1],
                                                        scalar2=None, op0=ALU.mult),
                       reads=["onesf", "sil"], writes=["sbc"])
                kb.dma("sp", brow[:], ada_b[layer], writes=["brow"])
                kb.dma("sp", gbc[:], norm_g[layer].partition_broadcast(128), writes=["gbc"])
                wi = 0
                for n in range(6):
                    px, pxk = kb.pf()
                    pc, pck = kb.pf()
                    for k in range(8):
                        w = wts[wi % 3]
                        wk = f"adaw{wi % 3}"
                        wi += 1
                        kb.dma("sp", w[:], ada_w[layer][k * 128:(k + 1) * 128, n * 512:(n + 1) * 512], writes=[wk])
                        mm(px[:], sbc[:, k, :], w[:], k == 0, False, ["sbc", wk], [pxk])
                        mm(pc[:], sbc[:, 8 + k, :], w[:], k == 0, False, ["sbc", wk], [pck])
                    mm(px[:], onesf[0:1, :], brow[0:1, n * 512:(n + 1) * 512], False, True, ["onesf", "brow"], [pxk])
                    mm(pc[:], onesf[0:1, :], brow[0:1, n * 512:(n + 1) * 512], False, True, ["onesf", "brow"], [pck])
                    part, half = n // 2, n % 2
                    hs = slice(half * 512, (half + 1) * 512)
                    for (p_, pk_, sfx) in ((px, pxk, ""), (pc, pck, "c")):
                        if part == 0:
                            dst = t["B" + sfx]
                            op("act", lambda e: e.activation(out=dst[:, hs], in_=p_[:], func=AF.Copy), reads=[pk_],
                               writes=[f"mod{layer}B{sfx}"])
                        elif part == 1:
                            dst = t["A" + sfx]
                            op("dve", lambda e: e.scalar_tensor_tensor(out=dst[:, hs], in0=p_[:], scalar=1.0,
                                                                       in1=gbc[:, 0, hs], op0=ALU.add, op1=ALU.mult),
                               reads=[pk_, "gbc"], writes=[f"mod{layer}A{sfx}"])
                        else:
                            if ("G" + sfx) in t:
                                dst = t["G" + sfx]
                                op("act", lambda e: e.activation(out=dst[:, hs], in_=p_[:], func=AF.Copy), reads=[pk_],
                                   writes=[f"mod{layer}G{sfx}"])
                            else:
                                op("act", lambda e: e.activation(out=sil[:, 0:8], in_=p_[:, 0:8], func=AF.Copy),
                                   reads=[pk_], writes=["sil"])
                kb.barrier()
            return t

        with ExitStack() as L0:
            alloc_common(L0)
            mod0 = adaln(L0, 0, True)
            ine = kb.sb(L0, "ine", [128, 8, 3072], BF16)
            oute = kb.sb(L0, "oute", [128, 8, D], BF16)
            poolw = kb.sb(L0, "poolw", [128, 4, 128], BF16)
            load_w_bf16(ine, in_e, 8, 3072, "ine")
            load_w_bf16(oute, out_e, 8, D, "oute")
            load_small_bf16(poolw[:].rearrange("p g d -> p (g d)"), pool_w, 128, 512, "poolw")
            invcx = kb.sb(L0, "invcx", [128, 4, 1, 64], F32)
            invcc = kb.sb(L0, "invcc", [128, 4, 1, 256], F32)
            kb.dma("sp", invcx[:, :, 0, :], invcx_in.rearrange("p (g t) -> p g t", g=4), writes=["invcx"])
            kb.dma("sp", invcc[:, :, 0, :], invcc_in.rearrange("p (g t) -> p g t", g=4), writes=["invcc"])
            xt = kb.sb(L0, "xt", [128, 4, D], F32)
            hb = kb.sb(L0, "hb", [128, 4, D], BF16)
            hTs = [kb.sb(L0, f"hT{i}", [128, 8, 512], BF16) for i in range(2)]
            ybuf = kb.sb(L0, "ybuf", [128, 8, 512], BF16)
            xn = [kb.sb(L0, f"xn{i}", [128, D], F32) for i in range(2)]
            t2 = [kb.sb(L0, f"t2_{i}", [128, 512], F32) for i in range(2)]
            sza = kb.sb(L0, "sza", [128, 512], BF16)
            szb = kb.sb(L0, "szb", [128, 512], BF16)
            vb = kb.sb(L0, "vb", [128, 512], F32)
            gb = kb.sb(L0, "gb", [128, 512], F32)
            pm = kb.sb(L0, "pm", [128, 512], BF16)
            dw = kb.sb(L0, "dw", [128, 512], F32)
            dw2 = kb.sb(L0, "dw2", [128, 512], F32)
            geo = {}
            for gname, nrows, rowlen in (("x", 8, 64), ("c", 1, 256)):
                W = rowlen + 32
                g_ = {"nrows": nrows, "rowlen": rowlen, "W": W}
                g_["upad"] = kb.sb(L0, f"upad{gname}", [128, nrows, W], F32)
                g_["sA"] = kb.sb(L0, f"sA{gname}", [128, nrows, W], F32)
                g_["sB"] = kb.sb(L0, f"sB{gname}", [128, nrows, W], F32)
                g_["cvpad"] = kb.sb(L0, f"cvpad{gname}", [128, nrows, rowlen + 2], F32)
                g_["invc"] = invcx if gname == "x" else invcc
                g_["invk"] = "invcx" if gname == "x" else "invcc"
                g_["k"] = gname
                op("dve", lambda e: e.memset(g_["upad"][:], 0.0), writes=[f"upad{gname}"])
                op("dve", lambda e: e.memset(g_["cvpad"][:], 0.0), writes=[f"cvpad{gname}"])
                op("dve", lambda e: e.memset(g_["sA"][:], 0.0), writes=[f"sA{gname}"])
                op("dve", lambda e: e.memset(g_["sB"][:], 0.0), writes=[f"sB{gname}"])
                geo[gname] = g_
            xn_i = [0]

            def l0_front(src, ntok, A, B, akeys, par):
                nsub = ntok // 128
                kb.dma("sp", xt[:, :nsub, :], src.rearrange("(j p) d -> p j d", p=128), writes=["xt"])
                norm_mod_T(xt, "xt", nsub, A, B, akeys, hb, hTs[par], f"hT{par}_")

            def l0_back(src, dst, ntok, g_, G, akeys, par):
                nsub = ntok // 128
                hT = hTs[par]
                hkey = lambda c: f"hT{par}_{c}"
                nrows, rowlen, W, gk = g_["nrows"], g_["rowlen"], g_["W"], g_["k"]
                upad, sA, sB, cvpad = g_["upad"], g_["sA"], g_["sB"], g_["cvpad"]

                def proj(m):
                    ps, pk = kb.pf()
                    for k in range(8):
                        mm(ps[:, :ntok], ine[:, k, m * 128:(m + 1) * 128], hT[:, k, :ntok], k == 0, k == 7,
                           ["ine", hkey(k)], [pk])
                    return ps, pk

                def rows(ap2d):
                    return ap2d.rearrange("p (r l) -> p r l", r=nrows)

                for g in range(4):
                    ps, pk = proj(4 + g)
                    op("act", lambda e: e.activation(out=sza[:, :ntok], in_=ps[:, :ntok], func=AF.Silu), reads=[pk],
                       writes=["sza"])
                    ps, pk = proj(g)
                    op("act", lambda e: e.activation(out=upad[:, :, 16:16 + rowlen], in_=rows(ps[:, :ntok]),
                                                     func=AF.Copy), reads=[pk], writes=[f"upad{gk}"])
                    op("dve", lambda e: e.tensor_tensor(out=sA[:, :, 1:W], in0=upad[:, :, 0:W - 1], in1=upad[:, :, 1:W],
                                                        op=ALU.add), reads=[f"upad{gk}"], writes=[f"sA{gk}"])
                    cur, curk, oth, othk = sA, f"sA{gk}", sB, f"sB{gk}"
                    lo, hi = 1, W
                    for lvl in range(g):
                        sh = 1 << lvl
                        nlo, nhi = lo + sh, hi - sh
                        op("dve", lambda e: e.tensor_tensor(out=oth[:, :, nlo:nhi], in0=cur[:, :, nlo - sh:nhi - sh],
                                                            in1=cur[:, :, nlo + sh:nhi + sh], op=ALU.add),
                           reads=[curk], writes=[othk])
                        cur, curk, oth, othk = oth, othk, cur, curk
                        lo, hi = nlo, nhi
                    op("dve", lambda e: e.tensor_tensor(out=rows(dw[:, :ntok]), in0=cur[:, :, 16:16 + rowlen],
                                                        in1=g_["invc"][:, g, :, :].broadcast_to([128, nrows, rowlen]),
                                                        op=ALU.mult), reads=[curk, g_["invk"]], writes=["dw"])
                    op("dve", lambda e: e.tensor_tensor(out=rows(pm[:, :ntok]), in0=rows(dw[:, :ntok]),
                                                        in1=upad[:, :, 16:16 + rowlen], op=ALU.subtract),
                       reads=["dw", f"upad{gk}"], writes=["pm"])
                    ps, pk = kb.pf()
                    mm(ps[:, :ntok], poolw[:, g, :], pm[:, :ntok], True, True, ["poolw", "pm"], [pk])
                    op("dve", lambda e: e.scalar_tensor_tensor(out=ybuf[:, g, :ntok], in0=ps[:, :ntok],
                                                               scalar=cpc("pool_scale", g), in1=sza[:, :ntok],
                                                               op0=ALU.mult, op1=ALU.mult),
                       reads=[pk, "cp", "sza"], writes=[f"y{g}"])
                for c in range(4):
                    ps, pk = proj(8 + c)
                    op("act", lambda e: e.activation(out=vb[:, :ntok], in_=ps[:, :ntok], func=AF.Copy), reads=[pk],
                       writes=["vb"])
                    ps, pk = proj(12 + c)
                    op("act", lambda e: e.activation(out=gb[:, :ntok], in_=ps[:, :ntok], func=AF.Copy), reads=[pk],
                       writes=["gb"])
                    ps, pk = proj(20 + c)
                    op("act", lambda e: e.activation(out=szb[:, :ntok], in_=ps[:, :ntok], func=AF.Silu), reads=[pk],
                       writes=["szb"])
                    ps, pk = proj(16 + c)
                    op("dve", lambda e: e.tensor_tensor(out=cvpad[:, :, 1:1 + rowlen], in0=rows(ps[:, :ntok]),
                                                        in1=rows(vb[:, :ntok]), op=ALU.mult),
                       reads=[pk, "vb"], writes=[f"cvpad{gk}"])
                    op("dve", lambda e: e.tensor_scalar(out=rows(dw[:, :ntok]), in0=cvpad[:, :, 0:rowlen],
                                                        scalar1=cpc("sconv_w", 0 * 4 + c), scalar2=None, op0=ALU.mult),
                       reads=[f"cvpad{gk}", "cp"], writes=["dw"])
                    op("dve", lambda e: e.scalar_tensor_tensor(out=rows(dw2[:, :ntok]), in0=cvpad[:, :, 1:1 + rowlen],
                                                               scalar=cpc("sconv_w", 1 * 4 + c), in1=rows(dw[:, :ntok]),
                                                               op0=ALU.mult, op1=ALU.add),
                       reads=[f"cvpad{gk}", "cp", "dw"], writes=["dw2"])
                    op("dve", lambda e: e.scalar_tensor_tensor(out=rows(dw[:, :ntok]), in0=cvpad[:, :, 2:2 + rowlen],
                                                               scalar=cpc("sconv_w", 2 * 4 + c), in1=rows(dw2[:, :ntok]),
                                                               op0=ALU.mult, op1=ALU.add),
                       reads=[f"cvpad{gk}", "cp", "dw2"], writes=["dw"])
                    op("dve", lambda e: e.tensor_tensor(out=dw2[:, :ntok], in0=dw[:, :ntok], in1=gb[:, :ntok],
                                                         op=ALU.mult), reads=["dw", "gb"], writes=["dw2"])
                    op("dve", lambda e: e.tensor_tensor(out=ybuf[:, 4 + c, :ntok], in0=dw2[:, :ntok], in1=szb[:, :ntok],
                                                         op=ALU.mult), reads=["dw2", "szb"], writes=[f"y{4 + c}"])
                ykeys = [f"y{c}" for c in range(8)]
                for j in range(nsub):
                    i = xn_i[0] % 2
                    xn_i[0] += 1
                    kb.dma("pool", xn[i][:], src[j * 128:(j + 1) * 128, :], writes=[f"xn{i}"], key=f"xnld{i}")
                    for half in range(2):
                        hs = slice(half * 512, (half + 1) * 512)
                        ps, pk = kb.pf()
                        for c in range(8):
                            mm(ps[:], ybuf[:, c, j * 128:(j + 1) * 128], oute[:, c, hs], c == 0, c == 7,
                               [f"y{c}", "oute"], [pk])
                        tt = t2[half]
                        op("dve", lambda e: e.tensor_tensor(out=tt[:], in0=ps[:], in1=G[:, hs], op=ALU.mult),
                           reads=[pk] + akeys, writes=[f"t2_{half}"])
                        op("dve", lambda e: e.tensor_tensor(out=xn[i][:, hs], in0=tt[:], in1=xn[i][:, hs], op=ALU.add),
                           reads=[f"t2_{half}", f"xn{i}"], writes=[f"xn{i}"])
                    kb.dma("pool", dst[j * 128:(j + 1) * 128, :], xn[i][:], reads=[f"xn{i}"], writes=["X1scr"],
                           key="xnst")

            tiles0 = [(ctx_in, XC1, CTX, geo["c"], mod0["Ac"], mod0["Bc"], mod0["Gc"], ["mod0Ac", "mod0Bc", "mod0Gc"])]
            for it in range(NT):
                tiles0.append((x_in[it * 512:(it + 1) * 512, :], X1[it * 512:(it + 1) * 512, :], 512, geo["x"],
                               mod0["A"], mod0["B"], mod0["G"], ["mod0A", "mod0B", "mod0G"]))
            for ti, (src, dst, ntok, g_, A_, B_, G_, ak_) in enumerate(tiles0):
                if ti == 0:
                    l0_front(src, ntok, A_, B_, ak_, 0)
                if ti + 1 < len(tiles0):
                    nx = tiles0[ti + 1]
                    l0_front(nx[0], nx[2], nx[4], nx[5], nx[7], (ti + 1) % 2)
                l0_back(src, dst, ntok, g_, G_, ak_, ti % 2)
            kb.barrier()

        if "STOP_L0" in dump:
            kb.finish(["X1scr"])
            return nc

        with ExitStack() as L1:
            G1pre = kb.sb(L1, "mod1G", [128, D], F32)
            with ExitStack() as L1a:
                alloc_common(L1a)
                mod1 = adaln(L1, 1, False, es_ctx=L1a, pre={"G": G1pre})
                ino = kb.sb(L1a, "ino", [128, 8, 4096], BF16)
                load_w_bf16(ino, in_o, 8, 4096, "ino")
                xt = kb.sb(L1a, "xt1", [128, 4, D], F32)
                hb = kb.sb(L1a, "hb1", [128, 4, D], BF16)
                hT1s = [kb.sb(L1a, f"hT1_{i}", [128, 8, 512], BF16) for i in range(2)]
                obuf = [kb.sb(L1a, f"obuf{s}", [128, 4, 512], BF16) for s in range(7)]
                p1t = kb.sb(L1a, "p1t", [128, 512], F32)
                sgt = kb.sb(L1a, "sgt", [128, 512], F32)

                def l1_front(src, ntok, A, B, akeys, par):
                    nsub = ntok // 128
                    kb.dma("sp", xt[:, :nsub, :], src.rearrange("(j p) d -> p j d", p=128), reads=["X1scr"],
                           writes=["xt1"])
                    norm_mod_T(xt, "xt1", nsub, A, B, akeys, hb, hT1s[par], f"hU{par}_")

                def l1_back(ntok, col0, lat0, is_ctx, par):
                    hT = hT1s[par]

                    def proj(m):
                        ps, pk = kb.pf()
                        for k in range(8):
                            mm(ps[:, :ntok], ino[:, k, m * 128:(m + 1) * 128], hT[:, k, :ntok], k == 0, k == 7,
                               ["ino", f"hU{par}_{k}"], [pk])
                        return ps, pk

                    for s, S in enumerate((S_u, S_r, S_k, S_v)):
                        for c in range(4):
                            ps, pk = proj(s * 4 + c)
                            if c % 2 == 0:
                                op("act", lambda e: e.activation(out=obuf[s][:, c, :ntok], in_=ps[:, :ntok],
                                                                 func=AF.Copy), reads=[pk], writes=[f"obuf{s}"])
                            else:
                                op("dve", lambda e: e.tensor_copy(out=obuf[s][:, c, :ntok], in_=ps[:, :ntok]),
                                   reads=[pk], writes=[f"obuf{s}"])
                        kb.dma("pool", S[:, col0:col0 + ntok].rearrange("(c p) n -> p c n", p=128),
                               obuf[s][:, :, :ntok], reads=[f"obuf{s}"], writes=["Sstreams"], key=f"obst{s}")
                    if is_ctx:
                        return
                    for c in range(4):
                        ps, pk = proj(16 + c)
                        op("act", lambda e: e.activation(out=obuf[4][:, c, :], in_=ps[:], func=AF.Silu), reads=[pk],
                           writes=["obuf4"])
                        ps, pk = proj(28 + c)
                        op("act", lambda e: e.activation(out=obuf[5][:, c, :], in_=ps[:], func=AF.Silu), reads=[pk],
                           writes=["obuf5"])
                        ps, pk = proj(20 + c)
                        op("act", lambda e: e.activation(out=p1t[:], in_=ps[:], func=AF.Copy), reads=[pk],
                           writes=["p1t"])
                        ps, pk = proj(24 + c)
                        op("act", lambda e: e.activation(out=sgt[:], in_=ps[:], func=AF.Sigmoid), reads=[pk],
                           writes=["sgt"])
                        op("dve", lambda e: e.tensor_tensor(out=obuf[6][:, c, :], in0=p1t[:], in1=sgt[:], op=ALU.mult),
                           reads=["p1t", "sgt"], writes=["obuf6"])
                    kb.dma("pool", S_zc[:, lat0:lat0 + 512].rearrange("(c p) n -> p c n", p=128), obuf[4][:],
                           reads=["obuf4"], writes=["Sstreams"], key="obst4")
                    kb.dma("pool", S_zd[:, lat0:lat0 + 512].rearrange("(c p) n -> p c n", p=128), obuf[5][:],
                           reads=["obuf5"], writes=["Sstreams"], key="obst5")
                    kb.dma("pool", S_glu[:, GPAD + lat0:GPAD + lat0 + 512].rearrange("(c p) n -> p c n", p=128),
                           obuf[6][:], reads=["obuf6"], writes=["Sstreams"], key="obst6")

                tiles1 = [(XC1, CTX, CTX0, 0, mod1["Ac"], mod1["Bc"], ["mod1Ac", "mod1Bc"], True)]
                for it in range(NT):
                    tiles1.append((X1[it * 512:(it + 1) * 512, :], 512, LAT0 + it * 512, it * 512, mod1["A"], mod1["B"],
                                   ["mod1A", "mod1B"], False))
                for ti, (src, ntok, col0, lat0, A_, B_, ak_, isc) in enumerate(tiles1):
                    if ti == 0:
                        l1_front(src, ntok, A_, B_, ak_, 0)
                    if ti + 1 < len(tiles1):
                        nx = tiles1[ti + 1]
                        l1_front(nx[0], nx[1], nx[4], nx[5], nx[6], (ti + 1) % 2)
                    l1_back(ntok, col0, lat0, isc, ti % 2)
                kb.barrier()

            if "STOP_L1A" in dump:
                kb.finish(["Sstreams"])
                return nc

            with ExitStack() as L1b:
                alloc_common(L1b)
                w1b = kb.sb(L1b, "w1b", [128, 4, 64], BF16)
                a1b = kb.sb(L1b, "a1b", [128, 4, 64], BF16)
                g1b = kb.sb(L1b, "g1b", [128, 4, 96], BF16)
                w2b = kb.sb(L1b, "w2b", [64, 512], BF16)
                a2b = kb.sb(L1b, "a2b", [64, 512], BF16)
                g2b = kb.sb(L1b, "g2b", [96, 512], BF16)
                load_small_bf16(w1b[:].rearrange("p c r -> p (c r)"), w1l, 128, 256, "w1b")
                load_small_bf16(a1b[:].rearrange("p c r -> p (c r)"), a1l, 128, 256, "a1b")
                load_small_bf16(g1b[:].rearrange("p c r -> p (c r)"), g1l, 128, 384, "g1b")
                load_small_bf16(w2b[:], w2l, 64, 512, "w2b")
                load_small_bf16(a2b[:], a2l, 64, 512, "a2b")
                load_small_bf16(g2b[:], g2l, 96, 512, "g2b")
                LS = []
                for p_ in range(2):
                    d_ = {"ut": kb.sb(L1b, f"ut{p_}", [128, 4, 514], BF16),
                          "nbt": kb.sb(L1b, f"nbt{p_}", [128, 4, 512], F32),
                          "um": [kb.sb(L1b, f"um{p_}_{j}", [128, 4, 512], BF16) for j in range(3)],
                          "hw": kb.sb(L1b, f"hw{p_}", [64, 512], BF16),
                          "ha": kb.sb(L1b, f"ha{p_}", [64, 512], BF16),
                          "hg": kb.sb(L1b, f"hg{p_}", [96, 512], BF16),
                          "swo": [kb.sb(L1b, f"swo{p_}_{d}", [128, 4, 512], F32) for d in range(2)],
                          "ao": [kb.sb(L1b, f"ao{p_}_{d}", [128, 4, 512], BF16) for d in range(2)],
                          "go": kb.sb(L1b, f"go{p_}", [128, 4, 512], BF16)}
                    LS.append(d_)
                dw_l = kb.sb(L1b, "dw_l", [128, 512], F32)

                def lora_front(col0, ntok, p_):
                    d_ = LS[p_]
                    u, nbt, um = d_["ut"], d_["nbt"], d_["um"]
                    uk, nk = f"ut{p_}", f"nbt{p_}"
                    kb.dma("sp", u[:, :, :ntok + 2], S_u[:, col0 - 1:col0 + ntok + 1].rearrange("(c p) n -> p c n", p=128),
                           reads=["Sstreams", "Shalo"], writes=[uk])
                    op("dve", lambda e: e.tensor_tensor(out=nbt[:, :, :ntok], in0=u[:, :, 0:ntok], in1=u[:, :, 2:ntok + 2],
                                                        op=ALU.add), reads=[uk], writes=[nk])
                    op("dve", lambda e: e.scalar_tensor_tensor(out=nbt[:, :, :ntok], in0=nbt[:, :, :ntok], scalar=0.5,
                                                               in1=u[:, :, 1:ntok + 1], op0=ALU.mult, op1=ALU.subtract),
                       reads=[nk, uk], writes=[nk])
                    for j in range(3):
                        for c in range(4):
                            if c != 3:
                                op("dve", lambda e: e.scalar_tensor_tensor(out=um[j][:, c, :ntok], in0=nbt[:, c, :ntok],
                                                                           scalar=cpc("mu", (3 + j) * 4 + c),
                                                                           in1=u[:, c, 1:ntok + 1], op0=ALU.mult,
                                                                           op1=ALU.add),
                                   reads=[nk, uk, "cp"], writes=[f"um{p_}_{j}"])
                            else:
                                op("dve", lambda e: e.tensor_scalar(out=dw_l[:, :ntok], in0=nbt[:, c, :ntok],
                                                                     scalar1=cpc("mu", (3 + j) * 4 + c), scalar2=None,
                                                                     op0=ALU.mult),
                                   reads=[nk, "cp"], writes=["dw_l"])
                                op("dve", lambda e: e.tensor_tensor(out=um[j][:, c, :ntok], in0=dw_l[:, :ntok],
                                                                     in1=u[:, c, 1:ntok + 1], op=ALU.add),
                                   reads=["dw_l", uk], writes=[f"um{p_}_{j}"])

                def lora_back(tau0, ntok, p_):
                    d_ = LS[p_]
                    um, hw, ha, hg, swo, ao, go = d_["um"], d_["hw"], d_["ha"], d_["hg"], d_["swo"], d_["ao"], d_["go"]
                    for j, (wb_, hid, nh, fn) in enumerate(((w1b, hw, 64, AF.Tanh), (a1b, ha, 64, AF.Copy),
                                                             (g1b, hg, 96, AF.Sigmoid))):
                        ps, pk = kb.pf()
                        for c in range(4):
                            mm(ps[:nh, :ntok], wb_[:, c, :], um[j][:, c, :ntok], c == 0, c == 3,
                               [("w1b", "a1b", "g1b")[j], f"um{p_}_{j}"], [pk])
                        op("act", lambda e: e.activation(out=hid[:, :ntok], in_=ps[:nh, :ntok], func=fn), reads=[pk],
                           writes=[("hw", "ha", "hg")[j] + str(p_)])
                    for d in range(2):
                        for c in range(4):
                            ps, pk = kb.pf()
                            mm(ps[:, :ntok], w2b[32 * d:32 * d + 32, c * 128:(c + 1) * 128], hw[32 * d:32 * d + 32, :ntok],
                               True, True, ["w2b", f"hw{p_}"], [pk])
                            op("act", lambda e: e.activation(out=swo[d][:, c, :ntok], in_=ps[:, :ntok], func=AF.Sigmoid,
                                                             bias=cpc("w0", d * 4 + c), scale=1.0),
                               reads=[pk, "cp"], writes=[f"swo{p_}_{d}"])
                            ps, pk = kb.pf()
                            mm(ps[:, :ntok], a2b[32 * d:32 * d + 32, c * 128:(c + 1) * 128], ha[32 * d:32 * d + 32, :ntok],
                               True, True, ["a2b", f"ha{p_}"], [pk])
                            op("act", lambda e: e.activation(out=ao[d][:, c, :ntok], in_=ps[:, :ntok], func=AF.Sigmoid,
                                                             bias=cpc("a0", d * 4 + c), scale=1.0),
                               reads=[pk, "cp"], writes=[f"ao{p_}_{d}"])
                        kb.dma("pool", S_sw[d][:, tau0:tau0 + ntok].rearrange("(c p) n -> p c n", p=128),
                               swo[d][:, :, :ntok], reads=[f"swo{p_}_{d}"], writes=["Slora"], key=f"swst{d}")
                        kb.dma("pool", S_a[d][:, tau0:tau0 + ntok].rearrange("(c p) n -> p c n", p=128),
                               ao[d][:, :, :ntok], reads=[f"ao{p_}_{d}"], writes=["Slora"], key=f"aost{d}")
                    for c in range(4):
                        ps, pk = kb.pf()
                        mm(ps[:, :ntok], g2b[:, c * 128:(c + 1) * 128], hg[:, :ntok], True, True, ["g2b", f"hg{p_}"], [pk])
                        op("dve", lambda e: e.tensor_copy(out=go[:, c, :ntok], in_=ps[:, :ntok]), reads=[pk],
                           writes=[f"go{p_}"])
                    kb.dma("pool", S_g[:, tau0:tau0 + ntok].rearrange("(c p) n -> p c n", p=128), go[:, :, :ntok],
                           reads=[f"go{p_}"], writes=["Slora"], key="gost")

                ltiles = [(CTX0, 0, CTX)] + [(LAT0 + it * 512, CTX + it * 512, 512) for it in range(NT)]
                lora_front(ltiles[0][0], ltiles[0][2], 0)
                for ti, (col0, tau0, ntok) in enumerate(ltiles):
                    if ti + 1 < len(ltiles):
                        lora_front(ltiles[ti + 1][0], ltiles[ti + 1][2], (ti + 1) % 2)
                    lora_back(tau0, ntok, ti % 2)
                kb.barrier()

            if "STOP_L1B" in dump:
                kb.finish(["Slora", "Sstreams"])
                return nc

            with ExitStack() as SC:
                scan_phase(nc, kb, SC, SEQ, NT, cp, cpc, identf, identb, masks_in, blk, onesf,
                           S_r, S_k, S_v, S_sw, S_a, S_g, S_zc, S_of, S_y, mm, dump=dump)
                kb.barrier()

            if "STOP_SCAN" in dump:
                kb.finish(["Sy", "Sof"])
                return nc

            with ExitStack() as CF:
                cdiag = kb.sb(CF, "cdiag", [128, 124, 128], BF16)
                for j in range(31):
                    for c in range(4):
                        eng = "dve"
                        op(eng, lambda e: e.tensor_scalar(out=cdiag[:, j * 4 + c, :], in0=identf[:],
                                                          scalar1=cpc("conf_dw_w", j * 4 + c), scalar2=None,
                                                          op0=ALU.mult), reads=["identf", "cp"], writes=["cdiag"])
                gl = [kb.sb(CF, f"gl{i}", [128, 512 + 2 * GPAD], BF16) for i in range(3)]
                CS = []
                for p_ in range(2):
                    d_ = {}
                    d_["hc"] = kb.sb(CF, f"hc{p_}", [128, 4, 512], F32)
                    d_["hcb"] = kb.sb(CF, f"hcb{p_}", [128, 4, 512], BF16)
                    d_["hsq"] = kb.sb(CF, f"hsq{p_}", [128, 4, 512], BF16)
                    d_["szd"] = kb.sb(CF, f"szd{p_}", [128, 4, 512], BF16)
                    d_["yc"] = kb.sb(CF, f"yc{p_}", [128, 4, 512], BF16)
                    CS.append(d_)
                mean = kb.sb(CF, "mean", [128, 512], F32)
                msq = kb.sb(CF, "msq", [128, 512], F32)
                rstd = kb.sb(CF, "rstd", [128, 512], F32)
                t1s = [kb.sb(CF, f"t1_{i}", [128, 512], F32) for i in range(2)]
                t3s = [kb.sb(CF, f"t3_{i}", [128, 512], F32) for i in range(2)]
                gi = [0]
                cacc = [kb.sb(CF, f"cacc{i}", [128, 512], F32) for i in range(2)]
                cacc_i = [0]

                def conf_A(it):
                    p_ = it % 2
                    d_ = CS[p_]
                    t0 = it * 512
                    kb.dma("sp", d_["szd"][:], S_zd[:, t0:t0 + 512].rearrange("(c p) n -> p c n", p=128),
                           reads=["Sstreams"], writes=[f"szd{p_}"])
                    for c in range(4):
                        g_ = gl[gi[0] % 3]
                        gk = f"gl{gi[0] % 3}"
                        gi[0] += 1
                        kb.dma("sp", g_[:], S_glu[c * 128:(c + 1) * 128, t0:t0 + 512 + 2 * GPAD],
                               reads=["Sstreams", "Sgluhalo"], writes=[gk])
                        NPE = 21
                        ps, pk = kb.pf()
                        for j in range(NPE):
                            mm(ps[:], cdiag[:, j * 4 + c, :], g_[:, 64 * j:64 * j + 512], j == 0, j == NPE - 1,
                               ["cdiag", gk], [pk])
                        ac = cacc[cacc_i[0] % 2]
                        ack = f"cacc{cacc_i[0] % 2}"
                        cacc_i[0] += 1
                        for j in range(NPE, 31):
                            if j == NPE:
                                op("dve", lambda e: e.tensor_scalar(out=ac[:], in0=g_[:, 64 * j:64 * j + 512],
                                                                    scalar1=cpc("conf_dw_w", j * 4 + c), scalar2=None,
                                                                    op0=ALU.mult), reads=[gk, "cp"], writes=[ack])
                            else:
                                op("dve", lambda e: e.scalar_tensor_tensor(out=ac[:], in0=g_[:, 64 * j:64 * j + 512],
                                                                           scalar=cpc("conf_dw_w", j * 4 + c), in1=ac[:],
                                                                           op0=ALU.mult, op1=ALU.add),
                                   reads=[gk, "cp", ack], writes=[ack])
                        op("dve", lambda e: e.scalar_tensor_tensor(out=d_["hc"][:, c, :], in0=ps[:],
                                                                   scalar=cpc("conf_dw_b", c), in1=ac[:], op0=ALU.add,
                                                                   op1=ALU.add),
                           reads=[pk, "cp", ack], writes=[f"hc{p_}_{c}"])
                        op("act", lambda e: e.activation(out=d_["hsq"][:, c, :], in_=d_["hc"][:, c, :], func=AF.Square),
                           reads=[f"hc{p_}_{c}"], writes=[f"hsq{p_}_{c}"])
                        op("act", lambda e: e.activation(out=d_["hcb"][:, c, :], in_=d_["hc"][:, c, :], func=AF.Copy),
                           reads=[f"hc{p_}_{c}"], writes=[f"hcb{p_}_{c}"])

                def conf_B(it):
                    p_ = it % 2
                    d_ = CS[p_]
                    t0 = it * 512
                    pm_, pmk = kb.pf()
                    pq_, pqk = kb.pf()
                    for c in range(4):
                        mm(pm_[:], onesb[:], d_["hcb"][:, c, :], c == 0, c == 3, ["onesb", f"hcb{p_}_{c}"], [pmk])
                    for c in range(4):
                        mm(pq_[:], onesb[:], d_["hsq"][:, c, :], c == 0, c == 3, ["onesb", f"hsq{p_}_{c}"], [pqk])
                    op("dve", lambda e: e.tensor_scalar(out=mean[:], in0=pm_[:], scalar1=1.0 / DB, scalar2=None,
                                                        op0=ALU.mult), reads=[pmk], writes=["mean"])
                    op("dve", lambda e: e.tensor_tensor(out=msq[:], in0=mean[:], in1=mean[:], op=ALU.mult),
                       reads=["mean"], writes=["msq"])
                    op("dve", lambda e: e.scalar_tensor_tensor(out=rstd[:], in0=pq_[:], scalar=1.0 / DB, in1=msq[:],
                                                               op0=ALU.mult, op1=ALU.subtract),
                       reads=[pqk, "msq"], writes=["rstd"])
                    op("dve", lambda e: e.tensor_scalar(out=rstd[:], in0=rstd[:], scalar1=1e-5, scalar2=None,
                                                        op0=ALU.add), reads=["rstd"], writes=["rstd"])
                    op("act", lambda e: e.activation(out=rstd[:], in_=rstd[:], func=AF.Sqrt), reads=["rstd"],
                       writes=["rstd"])
                    op("dve", lambda e: e.reciprocal(out=rstd[:], in_=rstd[:]), reads=["rstd"], writes=["rstd"])
                    for c in range(4):
                        t1, t3 = t1s[c % 2], t3s[c % 2]
                        t1k, t3k = f"t1_{c % 2}", f"t3_{c % 2}"
                        op("dve", lambda e: e.tensor_tensor(out=t1[:], in0=d_["hc"][:, c, :], in1=mean[:], op=ALU.subtract),
                           reads=[f"hc{p_}_{c}", "mean"], writes=[t1k])
                        op("dve", lambda e: e.tensor_tensor(out=t3[:], in0=t1[:], in1=rstd[:], op=ALU.mult),
                           reads=[t1k, "rstd"], writes=[t3k])
                        op("act", lambda e: e.activation(out=t1[:], in_=t3[:], func=AF.Silu,
                                                         bias=cpc("conf_ln_b", c), scale=cpc("conf_ln_g", c)),
                           reads=[t3k, "cp"], writes=[t1k])
                        op("dve", lambda e: e.tensor_tensor(out=d_["yc"][:, c, :], in0=t1[:], in1=d_["szd"][:, c, :],
                                                            op=ALU.mult), reads=[t1k, f"szd{p_}"], writes=[f"yc{p_}"])
                    kb.dma("pool", S_y[DB:D, t0:t0 + 512].rearrange("(c p) n -> p c n", p=128), d_["yc"][:],
                           reads=[f"yc{p_}"], writes=[f"Syc{p_}"], key=f"ycst{p_}")

                fuse_post = "STOP_CONF" not in dump
                if fuse_post:
                    cm["stage"] = [kb.sb(CF, f"stageP{i}", [128, 1024], F32) for i in range(2)]
                    cm["junk"] = kb.sb(CF, "junkP", [128, 1024], BF16)
                    outo = kb.sb(CF, "outo", [128, 8, D], BF16)
                    load_w_bf16(outo, out_o, 8, D, "outo")
                    fg = kb.sb(CF, "fg", [128, 1, D], F32)
                    kb.dma("sp", fg[:], final_g.partition_broadcast(128), writes=["fg"])
                    G1 = mod1["G"]
                    yt = [kb.sb(CF, f"yt{i}", [128, 8, 512], BF16) for i in range(2)]
                    x1t = kb.sb(CF, "x1t", [128, 4, D], F32)
                    x2 = [kb.sb(CF, f"x2_{i}", [128, D], F32) for i in range(2)]
                    ot = [kb.sb(CF, f"ot{i}", [128, D], F32) for i in range(2)]
                    t2 = [kb.sb(CF, f"t2p{i}", [128, 512], F32) for i in range(2)]
                    ssp = kb.sb(CF, "ssp", [128, 2], F32)
                    rsp = kb.sb(CF, "rsp", [128, 2], F32)
                xi = [0]

                def post_tile(it):
                    t0 = it * 512
                    y_ = yt[it % 2]
                    kb.dma("sp", y_[:], S_y[:, t0:t0 + 512].rearrange("(c p) n -> p c n", p=128),
                           reads=["Sy", f"Syc{it % 2}"], writes=[f"yt{it % 2}"])
                    kb.dma("sp", x1t[:], X1[t0:t0 + 512, :].rearrange("(j p) d -> p j d", p=128), reads=["X1scr"],
                           writes=["x1t"])
                    for j in range(4):
                        i = xi[0] % 2
                        xi[0] += 1
                        for half in range(2):
                            hs = slice(half * 512, (half + 1) * 512)
                            ps, pk = kb.pf()
                            for c in range(8):
                                mm(ps[:], y_[:, c, j * 128:(j + 1) * 128], outo[:, c, hs], c == 0, c == 7,
                                   [f"yt{it % 2}", "outo"], [pk])
                            op("dve", lambda e: e.tensor_tensor(out=t2[half][:], in0=ps[:], in1=G1[:, hs], op=ALU.mult),
                               reads=[pk, "mod1G"], writes=[f"t2p{half}"])
                            op("dve", lambda e: e.tensor_tensor(out=x2[i][:, hs], in0=t2[half][:], in1=x1t[:, j, hs],
                                                                op=ALU.add),
                               reads=[f"t2p{half}", "x1t"], writes=[f"x2_{i}"])
                        op("act", lambda e: e.activation(out=cm["junk"][:], in_=x2[i][:], func=AF.Square,
                                                         accum_out=ssp[:, i:i + 1]),
                           reads=[f"x2_{i}"], writes=["junk", f"ssp{i}"])
                        op("dve", lambda e: e.tensor_scalar(out=rsp[:, i:i + 1], in0=ssp[:, i:i + 1], scalar1=1.0 / D,
                                                            scalar2=1e-6, op0=ALU.mult, op1=ALU.add),
                           reads=[f"ssp{i}"], writes=[f"rsp{i}"])
                        op("act", lambda e: e.activation(out=rsp[:, i:i + 1], in_=rsp[:, i:i + 1], func=AF.Sqrt),
                           reads=[f"rsp{i}"], writes=[f"rsp{i}"])
                        op("dve", lambda e: e.reciprocal(out=rsp[:, i:i + 1], in_=rsp[:, i:i + 1]), reads=[f"rsp{i}"],
                           writes=[f"rsp{i}"])
                        op("dve", lambda e: e.scalar_tensor_tensor(out=ot[i][:], in0=x2[i][:], scalar=rsp[:, i:i + 1],
                                                                   in1=fg[:, 0, :], op0=ALU.mult, op1=ALU.mult),
                           reads=[f"x2_{i}", f"rsp{i}", "fg"], writes=[f"ot{i}"])
                        kb.dma("pool", out[t0 + j * 128:t0 + (j + 1) * 128, :], ot[i][:], reads=[f"ot{i}"],
                               writes=["OUT"], key=f"otst{i}")

                conf_A(0)
                for it in range(NT):
                    if it + 1 < NT:
                        conf_A(it + 1)
                    conf_B(it)
                    if fuse_post and it >= 1:
                        post_tile(it - 1)
                if fuse_post:
                    post_tile(NT - 1)
                kb.barrier()

            if "STOP_CONF" in dump:
                kb.finish(["Sy", "Syc0", "Syc1"])
                return nc
        kb.finish(["OUT"])
    return nc


def scan_phase(nc, kb, SC, SEQ, NT, cp, cpc, identf, identb, masks_in, blk, onesf,
               S_r, S_k, S_v, S_sw, S_a, S_g, S_zc, S_of, S_y, mm, dump=()):
    op = kb.op
    masks = kb.sb(SC, "masks", [128, 2, 6, 128], F32)
    kb.dma("sp", masks[:], masks_in.rearrange("p (d m t) -> p d m t", d=2, m=6), writes=["masks"])
    blkb = kb.sb(SC, "blkb", [128, 128], BF16)
    op("dve", lambda e: e.tensor_copy(out=blkb[:], in_=blk[:]), reads=["blk"], writes=["blkb"])
    sqb = kb.sb(SC, "sqb", [128, 512], BF16)
    omka = kb.sb(SC, "omka", [128, 4], F32)
    kac = cp[:, CP["k_a"]:CP["k_a"] + 4]
    op("dve", lambda e: e.tensor_scalar(out=omka[:], in0=kac, scalar1=-1.0, scalar2=1.0, op0=ALU.mult, op1=ALU.add),
       reads=["cp"], writes=["omka"])
    ones128 = kb.sb(SC, "ones128", [128, 128], F32)
    op("dve", lambda e: e.memset(ones128[:], 1.0), writes=["ones128"])
    NSLOT = 6

    class NS:
        pass

    TMP = NS()
    for n in ("rt", "kt", "vt"):
        setattr(TMP, n, kb.sb(SC, f"{n}T", [128, 514], BF16))
    for n in ("atd", "at0", "vb16", "Af"):
        setattr(TMP, n, kb.sb(SC, f"{n}T", [128, 512], BF16))
    for n in ("swt", "rp", "kp", "vp", "kkr", "kkn", "kd", "bb", "lw", "cs", "csr", "ig", "gp", "tA", "tB", "ksum"):
        setattr(TMP, n, kb.sb(SC, f"{n}T", [128, 512], F32))
    TMPN = set(vars(TMP).keys())
    PB = []
    for b in range(2):
        P = NS()
        P.b = b
        P.k = lambda n, b=b: (f"{n}_pT" if n in TMPN else f"{n}_p{b}")
        for n in TMPN:
            setattr(P, n, getattr(TMP, n))
        for n in ("gt_", "zct", "Bf", "Kf", "yo"):
            setattr(P, n, kb.sb(SC, f"{n}{b}", [128, 512], BF16))
        for n in ("gam", "Rf32", "bonus"):
            setattr(P, n, kb.sb(SC, f"{n}{b}", [128, 512], F32))
        P.AR = kb.sb(SC, f"AR{b}", [128, 4, 2, 128], BF16)
        for n in ("AT", "BT", "KT", "VT"):
            setattr(P, n, kb.sb(SC, f"{n}{b}", [128, 4, 128], BF16))
        PB.append(P)
    SL = []
    for s_ in range(NSLOT):
        L = NS()
        L.s = s_
        L.k = lambda n, s_=s_: f"{n}_s{s_}"
        L.smx = [kb.sb(SC, f"smx{s_}_{h}", [128, 3, 128], BF16) for h in range(2)]
        L.smo = [kb.sb(SC, f"smo{s_}_{h}", [128, 3, 128], BF16) for h in range(2)]
        L.smxT = kb.sb(SC, f"smxT{s_}", [128, 2, 3, 128], BF16)
        L.XX = [kb.sb(SC, f"XX{s_}_{i}", [128, 2, 2, 128], BF16) for i in range(2)]
        L.PPb = kb.sb(SC, f"PPb{s_}", [128, 2, 2, 128], BF16)
        L.Yb = kb.sb(SC, f"Yb{s_}", [128, 2, 2, 128], BF16)
        L.YBb = kb.sb(SC, f"YBb{s_}", [128, 2, 128], BF16)
        L.T128b = kb.sb(SC, f"T128b{s_}", [128, 2, 128], BF16)
        L.mkv = kb.sb(SC, f"mkv{s_}", [128, 128], BF16)
        L.WTc = kb.sb(SC, f"WTc{s_}", [128, 128], BF16)
        L.UlTc = kb.sb(SC, f"UlTc{s_}", [128, 128], BF16)
        L.P0b = kb.sb(SC, f"P0b{s_}", [128, 128], F32)
        L.Gt = kb.sb(SC, f"Gt{s_}", [128, 128], F32)
        L.ofs = kb.sb(SC, f"ofs{s_}", [128, 128], F32)
        L.ofl = kb.sb(SC, f"ofl{s_}", [128, 128], F32)
        L.osum = kb.sb(SC, f"osum{s_}", [128, 128], F32)
        L.onrm = kb.sb(SC, f"onrm{s_}", [128, 128], F32)
        L.st6 = kb.sb(SC, f"st6{s_}", [128, 2, 6], F32)
        L.mv = kb.sb(SC, f"mv{s_}", [128, 2, 2], F32)
        L.rsd = kb.sb(SC, f"rsd{s_}", [128, 2], F32)
        L.yl = kb.sb(SC, f"yl{s_}", [128, 128], F32)
        L.yl2 = kb.sb(SC, f"yl2{s_}", [128, 128], F32)
        SL.append(L)
    ST = [[kb.sb(SC, f"ST{p}_{i}", [128, 128], F32) for i in range(2)] for p in range(2)]
    shared_i = [0]

    def shared_bank():
        return kb.pfx(7)

    def prep_bank():
        return kb.pfx(6)

    coef = kb.sb(SC, "coef", [128, 24], F32)
    coefh = kb.sb(SC, "coefh", [128, 24], BF16)
    coefl = kb.sb(SC, "coefl", [128, 24], F32)
    mu12 = cp[:, CP["mu"]:CP["mu"] + 12]
    op("dve", lambda e: e.tensor_scalar(out=coef[:, 0:12], in0=mu12, scalar1=0.5, scalar2=None, op0=ALU.mult),
       reads=["cp"], writes=["coef"])
    op("dve", lambda e: e.tensor_scalar(out=coef[:, 12:24], in0=mu12, scalar1=-1.0, scalar2=1.0, op0=ALU.mult,
                                        op1=ALU.add), reads=["cp"], writes=["coef"])
    op("dve", lambda e: e.tensor_copy(out=coefh[:], in_=coef[:]), reads=["coef"], writes=["coefh"])
    op("dve", lambda e: e.tensor_copy(out=coefl[:], in_=coefh[:]), reads=["coefh"], writes=["coefl"])
    op("dve", lambda e: e.tensor_tensor(out=coefl[:], in0=coef[:], in1=coefl[:], op=ALU.subtract),
       reads=["coef", "coefl"], writes=["coefl"])
    DGh = kb.sb(SC, "DGh", [128, 24, 128], BF16)
    DGl = kb.sb(SC, "DGl", [128, 24, 128], BF16)
    for i_ in range(24):
        op("dve", lambda e: e.tensor_scalar(out=DGh[:, i_, :], in0=identf[:], scalar1=coef[:, i_:i_ + 1], scalar2=None,
                                            op0=ALU.mult), reads=["identf", "coef"], writes=["DGh"])
        op("dve", lambda e: e.tensor_scalar(out=DGl[:, i_, :], in0=identf[:], scalar1=coefl[:, i_:i_ + 1], scalar2=None,
                                            op0=ALU.mult), reads=["identf", "coefl"], writes=["DGl"])

    def prep_loads(P, hp, d, tau0, n, col0, lat0, bwd):
        k = P.k
        chs = slice(hp * 128, (hp + 1) * 128)
        kb.dma("sp", P.rt[:, :n + 2], S_r[chs, col0 - 1:col0 + n + 1], reads=["Sstreams", "Shalo"], writes=[k("rt")])
        kb.dma("sp", P.kt[:, :n + 2], S_k[chs, col0 - 1:col0 + n + 1], reads=["Sstreams", "Shalo"], writes=[k("kt")])
        kb.dma("sp", P.vt[:, :n + 2], S_v[chs, col0 - 1:col0 + n + 1], reads=["Sstreams", "Shalo"], writes=[k("vt")])
        kb.dma("sp", P.swt[:, :n], S_sw[d][chs, tau0:tau0 + n], reads=["Slora"], writes=[k("swt")])
        kb.dma("sp", P.atd[:, :n], S_a[d][chs, tau0:tau0 + n], reads=["Slora"], writes=[k("atd")])
        if bwd and lat0 is not None:
            kb.dma("sp", P.at0[:, :n], S_a[0][chs, tau0:tau0 + n], reads=["Slora"], writes=[k("at0")])

    def prep_gen(P, hp, d, tau0, n, col0, lat0, bwd):
        k = P.k
        chs = slice(hp * 128, (hp + 1) * 128)
        nch = n // 128
        fin = bwd and lat0 is not None
        if fin:
            kb.dma("sp", P.gt_[:, :n], S_g[chs, tau0:tau0 + n], reads=["Slora"], writes=[k("gt_")])
            kb.dma("sp", P.zct[:, :n], S_zc[chs, lat0:lat0 + n], reads=["Sstreams"], writes=[k("zct")])
        yield
        tA, tB = P.tA, P.tB
        for (xt_, xk, mi) in ((P.rt, "rt", 0), (P.kt, "kt", 1), (P.vt, "vt", 2)):
            ps, pk = prep_bank()
            ih, io = mi * 4 + hp, 12 + mi * 4 + hp
            seq = [(DGh, ih, 0), (DGh, io, 1), (DGh, ih, 2)]
            for qi, (dg, ii, sh) in enumerate(seq):
                mm(ps[:, :n], dg[:, ii, :], xt_[:, sh:sh + n], qi == 0, qi == len(seq) - 1, ["DGh", "DGl", k(xk)], [pk])
            if mi == 0:
                op("act", lambda e: e.activation(out=P.rp[:, :n], in_=ps[:, :n], func=AF.Copy), reads=[pk], writes=[k("rp")])
            elif mi == 1:
                op("act", lambda e: e.activation(out=P.kp[:, :n], in_=ps[:, :n], func=AF.Copy), reads=[pk], writes=[k("kp")])
                op("act", lambda e: e.activation(out=P.kkr[:, :n], in_=ps[:, :n], func=AF.Copy, scale=cpc("k_k", hp)),
                   reads=[pk, "cp"], writes=[k("kkr")])
                op("act", lambda e: e.activation(out=sqb[:, :n], in_=ps[:, :n], func=AF.Square, scale=cpc("k_k", hp)),
                   reads=[pk, "cp"], writes=["sqb"])
            else:
                op("act", lambda e: e.activation(out=P.vp[:, :n], in_=ps[:, :n], func=AF.Copy), reads=[pk], writes=[k("vp")])
                op("act", lambda e: e.activation(out=P.vb16[:, :n], in_=ps[:, :n], func=AF.Copy), reads=[pk],
                   writes=[k("vb16")])
            yield
        ps, pk = prep_bank()
        mm(ps[:, :n], blkb[:], sqb[:, :n], True, True, ["blkb", "sqb"], [pk])
        op("act", lambda e: e.activation(out=tA[:, :n], in_=ps[:, :n], func=AF.Sqrt), reads=[pk], writes=[k("tA")])
        yield
        op("dve", lambda e: e.tensor_scalar(out=tA[:, :n], in0=tA[:, :n], scalar1=1e-12, scalar2=None, op0=ALU.max),
           reads=[k("tA")], writes=[k("tA")])
        op("dve", lambda e: e.reciprocal(out=tA[:, :n], in_=tA[:, :n]), reads=[k("tA")], writes=[k("tA")])
        op("dve", lambda e: e.tensor_tensor(out=P.kkn[:, :n], in0=P.kkr[:, :n], in1=tA[:, :n], op=ALU.mult),
           reads=[k("kkr"), k("tA")], writes=[k("kkn")])
        yield
        op("act", lambda e: e.activation(out=tB[:, :n], in_=P.atd[:, :n], func=AF.Identity, scale=cpc("k_a", hp),
                                         bias=omka[:, hp:hp + 1]), reads=[k("atd"), "cp", "omka"], writes=[k("tB")])
        op("dve", lambda e: e.tensor_tensor(out=P.kd[:, :n], in0=tB[:, :n], in1=P.kp[:, :n], op=ALU.mult),
           reads=[k("tB"), k("kp")], writes=[k("kd")])
        op("dve", lambda e: e.tensor_tensor(out=P.bb[:, :n], in0=P.kkn[:, :n], in1=P.atd[:, :n], op=ALU.mult),
           reads=[k("kkn"), k("atd")], writes=[k("bb")])
        op("act", lambda e: e.activation(out=P.lw[:, :n], in_=P.swt[:, :n], func=AF.Copy, scale=-EXPM05),
           reads=[k("swt")], writes=[k("lw")])
        yield
        for j in range(nch):
            js = slice(j * 128, (j + 1) * 128)
            op("dve", lambda e: e.tensor_tensor_scan(out=P.cs[:, js], data0=ones128[:], data1=P.lw[:, js], initial=0.0,
                                                     op0=ALU.mult, op1=ALU.add), reads=["ones128", k("lw")],
               writes=[k("cs")])
        cse, csk = P.cs, k("cs")
        if bwd:
            op("dve", lambda e: e.tensor_tensor(out=tB[:, :n], in0=P.lw[:, :n], in1=P.cs[:, :n], op=ALU.subtract),
               reads=[k("lw"), k("cs")], writes=[k("tB")])
            for j in range(nch):
                js = slice(j * 128, (j + 1) * 128)
                op("dve", lambda e: e.tensor_scalar(out=P.csr[:, js], in0=tB[:, js],
                                                    scalar1=P.cs[:, j * 128 + 127:j * 128 + 128], scalar2=None,
                                                    op0=ALU.add), reads=[k("tB"), k("cs")], writes=[k("csr")])
            cse, csk = P.csr, k("csr")
        yield
        op("act", lambda e: e.activation(out=P.gam[:, :n], in_=cse[:, :n], func=AF.Exp), reads=[csk], writes=[k("gam")])
        op("act", lambda e: e.activation(out=P.ig[:, :n], in_=cse[:, :n], func=AF.Exp, scale=-1.0), reads=[csk],
           writes=[k("ig")])
        g3 = lambda t_: t_[:, :n].rearrange("p (j t) -> p j t", t=128)
        if not bwd:
            op("act", lambda e: e.activation(out=g3(P.gp)[:, :, 1:128], in_=g3(P.gam)[:, :, 0:127], func=AF.Copy),
               reads=[k("gam")], writes=[k("gp")])
            op("act", lambda e: e.activation(out=g3(P.gp)[:, :, 0:1], in_=g3(ones128)[:, :nch, 0:1] if False else
                                             ones128[:, 0:nch].unsqueeze(2), func=AF.Copy),
               reads=["ones128"], writes=[k("gp")])
        else:
            op("act", lambda e: e.activation(out=g3(P.gp)[:, :, 0:127], in_=g3(P.gam)[:, :, 1:128], func=AF.Copy),
               reads=[k("gam")], writes=[k("gp")])
            op("act", lambda e: e.activation(out=g3(P.gp)[:, :, 127:128], in_=ones128[:, 0:nch].unsqueeze(2),
                                             func=AF.Copy), reads=["ones128"], writes=[k("gp")])
        yield
        v3 = lambda t_: t_[:, :n].rearrange("p (j t) -> p j t", t=128)
        op("dve", lambda e: e.scalar_tensor_tensor(out=P.Af[:, :n], in0=P.kkn[:, :n], scalar=-1.0, in1=P.gp[:, :n],
                                                   op0=ALU.mult, op1=ALU.mult), reads=[k("kkn"), k("gp")], writes=[k("Af")])
        op("act", lambda e: e.activation(out=P.AR[:, :nch, 0, :], in_=v3(P.Af), func=AF.Copy), reads=[k("Af")],
           writes=[k("AR")])
        op("dve", lambda e: e.tensor_tensor(out=P.Rf32[:, :n], in0=P.rp[:, :n], in1=P.gam[:, :n], op=ALU.mult),
           reads=[k("rp"), k("gam")], writes=[k("Rf32")])
        op("act", lambda e: e.activation(out=P.AR[:, :nch, 1, :], in_=v3(P.Rf32), func=AF.Copy), reads=[k("Rf32")],
           writes=[k("AR")])
        yield
        op("dve", lambda e: e.tensor_tensor(out=P.Bf[:, :n], in0=P.bb[:, :n], in1=P.ig[:, :n], op=ALU.mult),
           reads=[k("bb"), k("ig")], writes=[k("Bf")])
        op("dve", lambda e: e.tensor_tensor(out=P.Kf[:, :n], in0=P.kd[:, :n], in1=P.ig[:, :n], op=ALU.mult),
           reads=[k("kd"), k("ig")], writes=[k("Kf")])
        yield
        for (src, sk, dstT, dk) in ((P.Af, "Af", P.AT, "AT"), (P.Bf, "Bf", P.BT, "BT"), (P.Kf, "Kf", P.KT, "KT"),
                                    (P.vb16, "vb16", P.VT, "VT")):
            pf_, pk = prep_bank()
            pt = pf_[:].bitcast(BF16)
            for j in range(nch):
                op("pe", lambda e: e.transpose(out=pt[:, j * 128:(j + 1) * 128], in_=src[:, j * 128:(j + 1) * 128],
                                               identity=identb[:]), reads=[k(sk), "identb"], writes=[pk])
            op("act", lambda e: e.activation(out=dstT[:, :nch, :].rearrange("p j t -> p (j t)"), in_=pt[:, :n],
                                             func=AF.Copy), reads=[pk], writes=[k(dk)])
            yield
        if fin:
            op("dve", lambda e: e.tensor_tensor(out=tB[:, :n], in0=P.at0[:, :n], in1=P.atd[:, :n], op=ALU.add),
               reads=[k("at0"), k("atd")], writes=[k("tB")])
            op("dve", lambda e: e.tensor_scalar(out=tB[:, :n], in0=tB[:, :n], scalar1=-2.0, scalar2=cpc("k_a", hp),
                                                op0=ALU.add, op1=ALU.mult), reads=[k("tB"), "cp"], writes=[k("tB")])
            op("dve", lambda e: e.scalar_tensor_tensor(out=P.ksum[:, :n], in0=tB[:, :n], scalar=2.0, in1=P.kp[:, :n],
                                                       op0=ALU.add, op1=ALU.mult), reads=[k("tB"), k("kp")],
               writes=[k("ksum")])
            op("dve", lambda e: e.scalar_tensor_tensor(out=sqb[:, :n], in0=P.rp[:, :n], scalar=cpc("r_k", hp),
                                                       in1=P.ksum[:, :n], op0=ALU.mult, op1=ALU.mult),
               reads=[k("rp"), "cp", k("ksum")], writes=["sqb"])
            ps, pk = prep_bank()
            mm(ps[:, :n], blkb[:], sqb[:, :n], True, True, ["blkb", "sqb"], [pk])
            op("dve", lambda e: e.tensor_tensor(out=P.bonus[:, :n], in0=ps[:, :n], in1=P.vp[:, :n], op=ALU.mult),
               reads=[pk, k("vp")], writes=[k("bonus")])
            yield

    chain_turn = [0]

    def chunk_gen(L, P, inst, hp, d, j, lat_row, bwd, fin, want_out, stp, first_of_pass, sc_state):
        k = L.k
        pk_ = P.k
        js = slice(j * 128, (j + 1) * 128)
        bank, bk = kb.pfx(L.s)
        if fin:
            kb.dma("sp", L.ofl[:], S_of[lat_row:lat_row + 128, hp * 128:(hp + 1) * 128], reads=["Sof"], writes=[k("ofl")])
        smx, smo, smxT, PPb = L.smx, L.smo, L.smxT, L.PPb
        flat = lambda t_: t_[:].rearrange("p h a t -> p (h a t)")
        for h in range(2):
            hs = slice(64 * h, 64 * h + 64)
            arh = P.AR[hs, j, :, :].rearrange("p a t -> p (a t)")
            mm(bank[:, 0:256], P.Bf[hs, js], arh, True, False, [pk_("Bf"), pk_("AR")], [bk])
            mm(bank[:, 256:512], P.Kf[hs, js], arh, False, True, [pk_("Kf"), pk_("AR")], [bk])
            op("dve", lambda e: e.tensor_tensor(out=smx[h][:], in0=bank[:, 0:128].unsqueeze(1).broadcast_to([128, 3, 128]),
                                                in1=masks[:, d, 0:3, :], op=ALU.mult), reads=[bk, "masks"],
               writes=[k(f"smx{h}")])
            op("dve", lambda e: e.tensor_tensor(out=smo[h][:], in0=bank[:, 128:512].rearrange("p (m t) -> p m t", m=3),
                                                in1=masks[:, d, 3:6, :], op=ALU.mult), reads=[bk, "masks"],
               writes=[k(f"smo{h}")])
            yield
        pt = bank[:].bitcast(BF16)
        for h in range(2):
            for m in range(3):
                c0 = (h * 3 + m) * 128
                op("pe", lambda e: e.transpose(out=pt[:, c0:c0 + 128], in_=smx[h][:, m, :], identity=identb[:]),
                   reads=[k(f"smx{h}"), "identb"], writes=[bk])
        op("act", lambda e: e.activation(out=smxT[:].rearrange("p h m t -> p (h m t)"), in_=pt[:, 0:768], func=AF.Copy),
           reads=[bk], writes=[k("smxT")])
        for h in range(2):
            op("dve", lambda e: e.tensor_tensor(out=PPb[:, h, 0, :], in0=smx[h][:, 0, :], in1=identb[:], op=ALU.add),
               reads=[k(f"smx{h}"), "identb"], writes=[k("PPb")])
        op("dve", lambda e: e.tensor_tensor(out=PPb[:, :, 1, :], in0=smxT[:, :, 0, :],
                                            in1=identb[:].unsqueeze(1).broadcast_to([128, 2, 128]), op=ALU.add),
           reads=[k("smxT"), "identb"], writes=[k("PPb")])
        yield
        Xc = [smx[0][:, 0, :], smx[1][:, 0, :]]
        XTc = [smxT[:, 0, 0, :], smxT[:, 1, 0, :]]
        xkeys = [k("smx0"), k("smx1"), k("smxT")]

        evi = [L.s]

        def evac_copy(dst_ap, src_ap, dkey):
            if L.s >= 2:
                op("act", lambda e: e.activation(out=dst_ap, in_=src_ap, func=AF.Copy), reads=[bk], writes=[dkey])
            else:
                op("dve", lambda e: e.tensor_copy(out=dst_ap, in_=src_ap), reads=[bk], writes=[dkey])

        def pp_seed(h, first):
            mm(bank[:, h * 256:(h + 1) * 256], identb[:], PPb[:, h, :, :].rearrange("p a t -> p (a t)"), first, False,
               ["identb", k("PPb")], [bk])

        def pp_update():
            evac_copy(flat(PPb), bank[:], k("PPb"))

        def pp_seed_all():
            mm(bank[:], identb[:], flat(PPb), True, False, ["identb", k("PPb")], [bk])

        for i in range(1, 5):
            last = i == 4
            xx = L.XX[i % 2]
            xk = k(f"XX{i % 2}")
            if not last:
                for h in range(2):
                    mm(bank[:, h * 256:h * 256 + 128], XTc[h], Xc[h], h == 0, False, xkeys, [bk])
                    mm(bank[:, h * 256 + 128:(h + 1) * 256], Xc[h], XTc[h], False, h == 1, xkeys, [bk])
                evac_copy(flat(xx), bank[:], xk)
            else:
                for h in range(2):
                    mm(bank[:, h * 256:h * 256 + 128], XTc[h], Xc[h], h == 0, h == 1, xkeys, [bk])
                evac_copy(xx[:, :, 0, :], bank[:].rearrange("p (h a t) -> p h a t", h=2, a=2)[:, :, 0, :], xk)
            Xc = [xx[:, 0, 0, :], xx[:, 1, 0, :]]
            XTc = [xx[:, 0, 1, :], xx[:, 1, 1, :]]
            xkeys = [xk]
            yield
            pp_seed_all()
            for h in range(2):
                mm(bank[:, h * 256:h * 256 + 128], PPb[:, h, 1, :], Xc[h], False, False, [k("PPb"), xk], [bk])
                mm(bank[:, h * 256 + 128:(h + 1) * 256], Xc[h], PPb[:, h, 1, :], False, h == 1, [k("PPb"), xk], [bk])
            pp_update()
            yield
        for h in range(2):
            mm(bank[:, h * 256:h * 256 + 128], smxT[:, h, 1, :], PPb[:, h, 0, :], h == 0, False, [k("smxT"), k("PPb")], [bk])
            mm(bank[:, h * 256 + 128:(h + 1) * 256], smx[h][:, 1, :], PPb[:, h, 1, :], False, h == 1,
               [k(f"smx{h}"), k("PPb")], [bk])
        evac_copy(flat(L.Yb), bank[:], k("Yb"))
        yield
        pp_seed_all()
        for h in range(2):
            mm(bank[:, h * 256:h * 256 + 128], PPb[:, h, 1, :], L.Yb[:, h, 0, :], False, False, [k("PPb"), k("Yb")], [bk])
            mm(bank[:, h * 256 + 128:(h + 1) * 256], PPb[:, h, 0, :], L.Yb[:, h, 1, :], False, h == 1,
               [k("PPb"), k("Yb")], [bk])
        pp_update()
        yield
        for h in range(2):
            mm(bank[:, h * 128:(h + 1) * 128], smxT[:, h, 2, :], PPb[:, h, 0, :], h == 0, False, [k("smxT"), k("PPb")], [bk])
        for h in range(2):
            hs = slice(64 * h, 64 * h + 64)
            mm(bank[:, 256 + 64 * h:256 + 64 * h + 64], smo[h][:, 1, :], P.VT[:, j, hs], False, h == 1,
               [k(f"smo{h}"), pk_("VT")], [bk])
        evac_copy(L.YBb[:].rearrange("p h t -> p (h t)"), bank[:, 0:256], k("YBb"))
        evac_copy(L.mkv[:], bank[:, 256:384], k("mkv"))
        yield
        for h in range(2):
            mm(bank[:, h * 128:(h + 1) * 128], identb[:], PPb[:, h, 0, :], h == 0, False, ["identb", k("PPb")], [bk])
            mm(bank[:, h * 128:(h + 1) * 128], PPb[:, h, 1, :], L.YBb[:, h, :], False, h == 1, [k("PPb"), k("YBb")], [bk])
        evac_copy(L.T128b[:].rearrange("p h t -> p (h t)"), bank[:, 0:256], k("T128b"))
        yield
        for h in range(2):
            hs = slice(64 * h, 64 * h + 64)
            mm(bank[:, hs], L.T128b[:, h, :], P.AT[:, j, hs], h == 0, False, [k("T128b"), pk_("AT")], [bk])
            mm(bank[:, 128 + 64 * h:128 + 64 * h + 64], L.T128b[:, h, :], L.mkv[:, hs], False, h == 1,
               [k("T128b"), k("mkv")], [bk])
        op("act", lambda e: e.activation(out=L.WTc[:], in_=bank[:, 0:128], func=AF.Copy), reads=[bk], writes=[k("WTc")])
        op("act", lambda e: e.activation(out=L.UlTc[:], in_=bank[:, 128:256], func=AF.Copy), reads=[bk],
           writes=[k("UlTc")])
        yield
        mm(bank[:, 0:128], L.WTc[:], P.BT[:, j, :], True, not want_out, [k("WTc"), pk_("BT")], [bk])
        if want_out:
            for h in range(2):
                mm(bank[64 * h:64 * h + 64, 128:256], L.WTc[:, 64 * h:64 * h + 64], smo[h][:, 0, :], False, True,
                   [k("WTc"), k(f"smo{h}")], [bk])
        op("dve", lambda e: e.tensor_tensor(out=L.P0b[:], in0=bank[:, 0:128], in1=blk[:], op=ALU.mult), reads=[bk, "blk"],
           writes=[k("P0b")])
        if want_out:
            op("dve", lambda e: e.tensor_tensor(out=L.Gt[:], in0=bank[:, 128:256], in1=P.Rf32[:, js], op=ALU.add),
               reads=[bk, pk_("Rf32")], writes=[k("Gt")])
        yield
        while chain_turn[0] != inst:
            yield
        if first_of_pass:
            op("dve", lambda e: e.memset(ST[stp][0][:], 0.0), writes=[f"ST{stp}_0"])
            sc_state[0] = 0
        cur = sc_state[0]
        stc, stck = ST[stp][cur], f"ST{stp}_{cur}"
        stn, stnk = ST[stp][1 - cur], f"ST{stp}_{1 - cur}"
        if want_out:
            po, pok = shared_bank()
            mm(po[:, 0:128], L.Gt[:], stc[:], True, False, [k("Gt"), stck], [pok])
            for h in range(2):
                hs = slice(64 * h, 64 * h + 64)
                mm(po[:, hs], smo[h][:, 0, :], L.UlTc[:, hs], False, False, [k(f"smo{h}"), k("UlTc")], [pok])
                mm(po[:, hs], smo[h][:, 2, :], P.VT[:, j, hs], False, h == 1, [k(f"smo{h}"), pk_("VT")], [pok])
        mm(bank[:, 0:128], P.BT[:, j, :], L.UlTc[:], True, False, [pk_("BT"), k("UlTc")], [bk])
        mm(bank[:, 0:128], P.KT[:, j, :], P.VT[:, j, :], False, False, [pk_("KT"), pk_("VT")], [bk])
        mm(bank[:, 0:128], identf[:], stc[:], False, False, ["identf", stck], [bk])
        mm(bank[:, 0:128], L.P0b[:], stc[:], False, True, [k("P0b"), stck], [bk])
        gcol = j * 128 if bwd else j * 128 + 127
        op("dve", lambda e: e.scalar_tensor_tensor(out=stn[:], in0=bank[:, 0:128], scalar=P.gam[:, gcol:gcol + 1],
                                                   in1=blk[:], op0=ALU.mult, op1=ALU.mult),
           reads=[bk, pk_("gam"), "blk"], writes=[stnk])
        sc_state[0] = 1 - cur
        chain_turn[0] += 1
        if not want_out:
            return
        if not fin:
            op("act", lambda e: e.activation(out=L.ofs[:], in_=po[:, 0:128], func=AF.Copy), reads=[pok],
               writes=[k("ofs")])
            kb.dma("sp", S_of[lat_row:lat_row + 128, hp * 128:(hp + 1) * 128], L.ofs[:], reads=[k("ofs")],
                   writes=["Sof"], key="ofst")
            return
        op("dve", lambda e: e.tensor_tensor(out=L.osum[:], in0=po[:, 0:128], in1=L.ofl[:], op=ALU.add),
           reads=[pok, k("ofl")], writes=[k("osum")])
        yield
        for h in range(2):
            hs = slice(64 * h, 64 * h + 64)
            op("dve", lambda e: e.bn_stats(out=L.st6[:, h, :], in_=L.osum[:, hs]), reads=[k("osum")], writes=[k("st6")])
            op("dve", lambda e: e.bn_aggr(out=L.mv[:, h, :], in_=L.st6[:, h, :]), reads=[k("st6")], writes=[k("mv")])
        op("dve", lambda e: e.tensor_scalar(out=L.rsd[:], in0=L.mv[:, :, 1], scalar1=64e-5, scalar2=None, op0=ALU.add),
           reads=[k("mv")], writes=[k("rsd")])
        op("act", lambda e: e.activation(out=L.rsd[:], in_=L.rsd[:], func=AF.Sqrt), reads=[k("rsd")], writes=[k("rsd")])
        yield
        op("dve", lambda e: e.reciprocal(out=L.rsd[:], in_=L.rsd[:]), reads=[k("rsd")], writes=[k("rsd")])
        for h in range(2):
            hs = slice(64 * h, 64 * h + 64)
            op("dve", lambda e: e.tensor_scalar(out=L.onrm[:, hs], in0=L.osum[:, hs], scalar1=L.mv[:, h, 0:1],
                                                scalar2=L.rsd[:, h:h + 1], op0=ALU.subtract, op1=ALU.mult),
               reads=[k("osum"), k("mv"), k("rsd")], writes=[k("onrm")])
        yield
        op("pe", lambda e: e.transpose(out=bank[:, 0:128], in_=L.onrm[:], identity=identf[:]),
           reads=[k("onrm"), "identf"], writes=[bk])
        op("dve", lambda e: e.tensor_scalar(out=L.yl[:], in0=bank[:, 0:128], scalar1=cpc("lnx_g", hp),
                                            scalar2=cpc("lnx_b", hp), op0=ALU.mult, op1=ALU.add),
           reads=[bk, "cp"], writes=[k("yl")])
        yield
        op("dve", lambda e: e.tensor_tensor(out=L.yl2[:], in0=L.yl[:], in1=P.bonus[:, js], op=ALU.add),
           reads=[k("yl"), pk_("bonus")], writes=[k("yl2")])
        op("dve", lambda e: e.tensor_tensor(out=L.yl[:], in0=L.yl2[:], in1=P.gt_[:, js], op=ALU.mult),
           reads=[k("yl2"), pk_("gt_")], writes=[k("yl")])
        op("dve", lambda e: e.tensor_tensor(out=P.yo[:, js], in0=L.yl[:], in1=P.zct[:, js], op=ALU.mult),
           reads=[k("yl"), pk_("zct")], writes=[pk_("yo")])

    sups = []
    npass = 0
    for hp in range(4):
        for d in range(2):
            bwd = d == 1
            lst = [(0, CTX, CTX0, None)] + [(CTX + it * 512, 512, LAT0 + it * 512, it * 512) for it in range(NT)]
            if bwd:
                lst = [lst[0]] + lst[1:][::-1]
            for qi, (tau0, n, col0, lat0) in enumerate(lst):
                sups.append((hp, d, tau0, n, col0, lat0, bwd, qi == 0, npass % 2))
            npass += 1
    lim = [int(x[8:]) for x in dump if x.startswith("SCANLIM_")]
    if lim:
        sups = sups[:lim[0]]
    nsup = len(sups)
    pass_state = {}
    prep_done = [False] * nsup
    prep_started = [False] * nsup
    chunks_left = [0] * nsup
    inst_list = []
    for q, (hp, d, tau0, n, col0, lat0, bwd, first, stp) in enumerate(sups):
        nch = n // 128
        order = list(range(nch))[::-1] if bwd else list(range(nch))
        chunks_left[q] = nch
        for oi, j in enumerate(order):
            inst_list.append((q, j, first and oi == 0))
    loads_issued = [False] * nsup

    def issue_loads(q):
        if q < nsup and not loads_issued[q]:
            (hp, d, tau0, n, col0, lat0, bwd, first, stp) = sups[q]
            prep_loads(PB[q % 2], hp, d, tau0, n, col0, lat0, bwd)
            loads_issued[q] = True

    active = []
    free_slots = list(range(NSLOT))
    next_inst = 0
    sc_states = {}
    while next_inst < len(inst_list) or active:
        for q in range(nsup):
            if prep_started[q]:
                continue
            if q >= 2 and chunks_left[q - 2] > 0:
                break
            if q >= 1 and not prep_done[q - 1]:
                break
            (hp, d, tau0, n, col0, lat0, bwd, first, stp) = sups[q]
            issue_loads(q)
            active.append([prep_gen(PB[q % 2], hp, d, tau0, n, col0, lat0, bwd), "prep", q, None])
            prep_started[q] = True
            break
        if next_inst < len(inst_list) and free_slots:
            q, j, first = inst_list[next_inst]
            if prep_done[q]:
                (hp, d, tau0, n, col0, lat0, bwd, firstq, stp) = sups[q]
                slot = free_slots.pop(0)
                fin = bwd and lat0 is not None
                lat_row = None if lat0 is None else lat0 + j * 128
                key = (hp, d)
                if key not in sc_states:
                    sc_states[key] = [0]
                g = chunk_gen(SL[slot], PB[q % 2], next_inst, hp, d, j, lat_row, bwd, fin, lat0 is not None, stp, first,
                              sc_states[key])
                active.append([g, "chunk", q, slot])
                next_inst += 1
        still = []
        for item in active:
            g, kind, q, slot = item
            try:
                next(g)
                still.append(item)
            except StopIteration:
                if kind == "prep":
                    prep_done[q] = True
                    issue_loads(q + 1)
                else:
                    free_slots.append(slot)
                    chunks_left[q] -= 1
                    if chunks_left[q] == 0:
                        (hp, d, tau0, n, col0, lat0, bwd, firstq, stp) = sups[q]
                        if bwd and lat0 is not None:
                            P = PB[q % 2]
                            kb.dma("sp", S_y[hp * 128:(hp + 1) * 128, lat0:lat0 + 512], P.yo[:], reads=[P.k("yo")],
                                   writes=["Sy"], key="yost")
        active = still


def prep_inputs(inp, SEQ):
    f = lambda a: np.ascontiguousarray(np.asarray(a, np.float32))
    shared = {
        "ada_w_e": f(inp["ada_w_e"][0]), "ada_w_o": f(inp["ada_w_o"][0]),
        "ada_b_e": f(inp["ada_b_e"][0][None]), "ada_b_o": f(inp["ada_b_o"][0][None]),
        "norm_e": f(inp["norm_e"][0][None]), "norm_o": f(inp["norm_o"][0][None]),
        "in_e": f(inp["in_e"][0]), "out_e": f(inp["out_e"][0]), "in_o": f(inp["in_o"][0]), "out_o": f(inp["out_o"][0]),
        "pool_w": f(np.asarray(inp["pool_w"][0]).transpose(1, 0, 2).reshape(128, 512)),
        "final_g": f(np.asarray(inp["final_g"])[None]),
    }
    cols = [_cols(inp["pool_scale"][0][None]), _cols(inp["sconv_w"][0]), _cols(inp["rwkv_mu"][0]),
            _cols(inp["k_k"][0][None]), _cols(inp["k_a"][0][None]), _cols(np.asarray(inp["r_k"][0]).reshape(1, 512)),
            _cols(inp["lnx_g"][0][None]), _cols(inp["lnx_b"][0][None]), _cols(inp["conf_dw_b"][0][None]),
            _cols(inp["conf_ln_g"][0][None]), _cols(inp["conf_ln_b"][0][None]), _cols(inp["w0"][0]),
            _cols(inp["a0"][0]), _cols(inp["conf_dw_w"][0])]
    shared["cp"] = np.ascontiguousarray(np.concatenate(cols, axis=1))
    assert shared["cp"].shape == (128, NCP)

    def l1(w):
        w = np.asarray(w, np.float32).transpose(1, 0, 2).reshape(4, 128, -1).transpose(1, 0, 2)
        return np.ascontiguousarray(w.reshape(128, -1))

    shared["w1l"] = l1(inp["w1"][0])
    shared["a1l"] = l1(inp["a1"][0])
    shared["g1l"] = np.ascontiguousarray(
        np.asarray(inp["g1"][0], np.float32).reshape(4, 128, 96).transpose(1, 0, 2).reshape(128, 384))
    shared["w2l"] = f(np.asarray(inp["w2"][0]).reshape(64, 512))
    shared["a2l"] = f(np.asarray(inp["a2"][0]).reshape(64, 512))
    shared["g2l"] = f(inp["g2"][0])
    shared.update(host_consts())
    maps = []
    for b in range(2):
        m = dict(shared)
        m["x"] = f(inp["x"][b][:SEQ])
        m["ctx"] = f(inp["ctx"][b])
        m["ccol"] = np.ascontiguousarray(np.concatenate(
            [np.asarray(inp["c"][b], np.float32).reshape(8, 128).T, np.asarray(inp["c_ctx"], np.float32).reshape(8, 128).T],
            axis=1))
        maps.append(m)
    return maps


_NC_CACHE = {}


def kernel(**inputs):
    SEQ = 8192
    if SEQ not in _NC_CACHE:
        _NC_CACHE[SEQ] = build(SEQ)
    nc = _NC_CACHE[SEQ]
    maps = prep_inputs(inputs, SEQ)
    in_maps = [maps[c // 4] for c in range(8)]
    res = run_bass_kernel_spmd(nc, in_maps, core_ids=list(range(8)))
    return np.stack([res.results[0]["out"], res.results[4]["out"]], axis=0).astype(np.float32)
```

```python
import numpy as np
from contextlib import ExitStack
import concourse.bass as bass
import concourse.mybir as mybir
from concourse.bass_utils import run_bass_kernel_spmd

F32 = mybir.dt.float32
BF16 = mybir.dt.bfloat16
ALU = mybir.AluOpType
AF = mybir.ActivationFunctionType

D = 1024
DB = 512
CTX = 256
CH = 128
CTX0 = 2
LAT0 = CTX0 + CTX + 2
GPAD = 960
EXPM05 = float(np.exp(-0.5))


class KB:
    def __init__(self, nc, es):
        self.nc = nc
        self.es = es
        self.eng = {"pe": nc.tensor, "act": nc.scalar, "dve": nc.vector, "pool": nc.gpsimd, "sp": nc.sync}
        self.sem = {}
        self.cnt = {}
        for e in self.eng:
            self.sem[e] = es.enter_context(nc.semaphore("sem_" + e))
            self.cnt[e] = 0
        self.dsem = {}
        self.dcnt = {}
        self.waited = {e: {} for e in self.eng}
        self.lastw = {}
        self.reads = {}
        self.psf = [es.enter_context(nc.psum_tensor(f"psf{i}", [128, 512], F32)) for i in range(8)]
        self.psf_i = 0
        self.psb_i = 0
        self.ndsem = 0

    def sb(self, es, name, shape, dt):
        self.nsb = getattr(self, "nsb", 0) + 1
        return es.enter_context(self.nc.sbuf_tensor("sb%d_%s" % (self.nsb, name), shape, dt))

    def pf(self):
        i = self.psf_i % 6
        self.psf_i += 1
        return self.psf[i], f"psf{i}"

    def pfx(self, i):
        return self.psf[i], f"psf{i}"

    def pb(self):
        i = 6 + self.psb_i % 2
        self.psb_i += 1
        return self.psf[i][:].bitcast(BF16), f"psf{i}"

    def _wait(self, e, ev):
        if ev is None:
            return
        semkey, val = ev
        if self.waited[e].get(semkey, 0) >= val:
            return
        sem = self.sem[semkey] if semkey in self.sem else self.dsem[semkey]
        self.eng[e].wait_ge(sem, val)
        self.waited[e][semkey] = val

    def _deps(self, e, reads, writes, sync_same):
        for k in reads:
            for sk, v in self.lastw.get(k, {}).items():
                if sync_same or sk != e:
                    self._wait(e, (sk, v))
            if k.startswith("ps"):
                for sk, v in self.reads.get(k, {}).items():
                    if sk != e:
                        self._wait(e, (sk, v))
        for k in writes:
            for sk, v in self.lastw.get(k, {}).items():
                if sync_same or sk != e:
                    self._wait(e, (sk, v))
            for sk, v in self.reads.get(k, {}).items():
                if sync_same or sk != e:
                    self._wait(e, (sk, v))

    def _record(self, ev, reads, writes):
        sk, v = ev
        for k in writes:
            self.lastw.setdefault(k, {})[sk] = v
        for k in reads:
            self.reads.setdefault(k, {})[sk] = v

    budget = None
    nops = 0

    def op(self, e, fn, reads=(), writes=()):
        self.nops += 1
        if self.budget is not None and self.nops > self.budget:
            return
        self._deps(e, reads, writes, e != "pe")
        inst = fn(self.eng[e])
        self.cnt[e] += 1
        inst.then_inc(self.sem[e], 1)
        self._record((e, self.cnt[e]), reads, writes)

    def dma(self, q, out, in_, reads=(), writes=(), key=None):
        self.nops += 1
        if self.budget is not None and self.nops > self.budget and not str(key).startswith("dbg"):
            return
        key = key or (list(writes) + list(reads))[0]
        skey = "d_" + str(key)
        if skey not in self.dsem:
            self.dsem[skey] = self.es.enter_context(self.nc.semaphore("ds%d" % self.ndsem))
            self.ndsem += 1
            self.dcnt[skey] = 0
        self._deps(q, reads, writes, True)
        self.dcnt[skey] += 16
        self.eng[q].dma_start(out=out, in_=in_).then_inc(self.dsem[skey], 16)
        self._record((skey, self.dcnt[skey]), reads, writes)

    def barrier(self):
        evs = [(e, self.cnt[e]) for e in self.eng if self.cnt[e] > 0] + [(k, v) for k, v in self.dcnt.items() if v > 0]
        for e in self.eng:
            for ev in evs:
                if ev[0] != e:
                    self._wait(e, ev)

    def finish(self, keys):
        for k in keys:
            for sk, v in self.lastw.get(k, {}).items():
                self._wait("sp", (sk, v))
            for sk, v in self.reads.get(k, {}).items():
                self._wait("sp", (sk, v))


CP = {}
_off = 0
for _n, _w in [("pool_scale", 4), ("sconv_w", 12), ("mu", 24), ("k_k", 4), ("k_a", 4), ("r_k", 4),
               ("lnx_g", 4), ("lnx_b", 4), ("conf_dw_b", 4), ("conf_ln_g", 4), ("conf_ln_b", 4),
               ("w0", 8), ("a0", 8), ("conf_dw_w", 124)]:
    CP[_n] = _off
    _off += _w
NCP = _off


def _cols(p):
    p = np.asarray(p, np.float32).reshape(-1, 4, 128)
    return np.ascontiguousarray(p.transpose(2, 0, 1).reshape(128, -1))


def host_consts():
    s = np.arange(128)[:, None]
    t = np.arange(128)[None, :]
    b32 = (s // 32) == (t // 32)
    b64 = (s // 64) == (t // 64)
    ml = []
    for st, inc in (((s < t), (s <= t)), ((s > t), (s >= t))):
        ml += [st & b32, st & b64 & ~b32, st & ~b64, inc, st, inc]
    masks = np.concatenate(ml, axis=1).astype(np.float32)
    blk = ((s // 64) == (t // 64)).astype(np.float32)

    def invc(l):
        tt = np.arange(l)
        rows = []
        for w in (2, 4, 8, 16):
            lo = np.clip(tt - w // 2, 0, l)
            hi = np.clip(tt - w // 2 + w, 0, l)
            rows.append(1.0 / (hi - lo).astype(np.float32))
        r = np.concatenate(rows).astype(np.float32)
        return np.ascontiguousarray(np.broadcast_to(r[None, :], (128, r.size)))

    return {
        "ident": np.eye(128, dtype=np.float32),
        "masks": masks,
        "blk": blk,
        "invc_x": invc(64),
        "invc_c": invc(256),
    }


def build(SEQ, dump=()):
    assert SEQ % 512 == 0
    NT = SEQ // 512
    NTAU = CTX + SEQ
    NCOL = LAT0 + SEQ + 2
    nc = bass.Bass("TRN2", target_bir_lowering=False)

    def din(name, shape, dt=F32):
        return nc.dram_tensor(name, shape, dt, kind="ExternalInput").ap()

    def dscr(name, shape, dt):
        kind = "ExternalOutput" if name in dump else "Internal"
        return nc.dram_tensor(name, shape, dt, kind=kind).ap()

    x_in = din("x", [SEQ, D])
    ctx_in = din("ctx", [CTX, D])
    ccol_in = din("ccol", [128, 16])
    ada_w = [din("ada_w_e", [D, 3 * D]), din("ada_w_o", [D, 3 * D])]
    ada_b = [din("ada_b_e", [1, 3 * D]), din("ada_b_o", [1, 3 * D])]
    norm_g = [din("norm_e", [1, D]), din("norm_o", [1, D])]
    in_e = din("in_e", [D, 3072])
    out_e = din("out_e", [D, D])
    in_o = din("in_o", [D, 4096])
    out_o = din("out_o", [D, D])
    pool_w = din("pool_w", [128, 512])
    cp_in = din("cp", [128, NCP])
    w1l = din("w1l", [128, 256])
    a1l = din("a1l", [128, 256])
    g1l = din("g1l", [128, 384])
    w2l = din("w2l", [64, 512])
    a2l = din("a2l", [64, 512])
    g2l = din("g2l", [96, 512])
    final_g = din("final_g", [1, D])
    ident_in = din("ident", [128, 128])
    masks_in = din("masks", [128, 1536])
    blk_in = din("blk", [128, 128])
    invcx_in = din("invc_x", [128, 256])
    invcc_in = din("invc_c", [128, 1024])
    out = nc.dram_tensor("out", [SEQ, D], F32, kind="ExternalOutput").ap()

    X1 = dscr("X1", [SEQ, D], F32)
    XC1 = dscr("XC1", [CTX, D], F32)
    S_u = dscr("S_u", [DB, NCOL], BF16)
    S_r = dscr("S_r", [DB, NCOL], BF16)
    S_k = dscr("S_k", [DB, NCOL], BF16)
    S_v = dscr("S_v", [DB, NCOL], BF16)
    S_zc = dscr("S_zc", [DB, SEQ], BF16)
    S_zd = dscr("S_zd", [DB, SEQ], BF16)
    S_glu = dscr("S_glu", [DB, SEQ + 2 * GPAD], BF16)
    S_sw = [dscr("S_sw0", [DB, NTAU], F32), dscr("S_sw1", [DB, NTAU], F32)]
    S_a = [dscr("S_a0", [DB, NTAU], BF16), dscr("S_a1", [DB, NTAU], BF16)]
    S_g = dscr("S_g", [DB, NTAU], BF16)
    S_of = dscr("S_of", [SEQ, DB], F32)
    S_y = dscr("S_y", [D, SEQ], BF16)

    with ExitStack() as es:
        kb = KB(nc, es)
        op = kb.op

        identf = kb.sb(es, "identf", [128, 128], F32)
        identb = kb.sb(es, "identb", [128, 128], BF16)
        blk = kb.sb(es, "blk", [128, 128], F32)
        onesf = kb.sb(es, "onesf", [128, 128], F32)
        onesb = kb.sb(es, "onesb", [128, 128], BF16)
        cp = kb.sb(es, "cp", [128, NCP], F32)
        ccol = kb.sb(es, "ccol", [128, 16], F32)
        zb = kb.sb(es, "zb", [128, GPAD], BF16)
        kb.dma("sp", identf[:], ident_in, writes=["identf"])
        kb.dma("sp", blk[:], blk_in, writes=["blk"])
        kb.dma("sp", cp[:], cp_in, writes=["cp"])
        kb.dma("sp", ccol[:], ccol_in, writes=["ccol"])
        op("dve", lambda e: e.tensor_copy(out=identb[:], in_=identf[:]), reads=["identf"], writes=["identb"])
        op("dve", lambda e: e.memset(onesf[:], 1.0), writes=["onesf"])
        op("dve", lambda e: e.memset(onesb[:], 1.0), writes=["onesb"])
        op("dve", lambda e: e.memset(zb[:], 0.0), writes=["zb"])
        for S in (S_u, S_r, S_k, S_v):
            for c0 in (0, LAT0 - 2, NCOL - 2):
                kb.dma("pool", S[:, c0:c0 + 2].rearrange("(c p) n -> p c n", p=128),
                       zb[:, 0:8].rearrange("p (c n) -> p c n", c=4), reads=["zb"], writes=["Shalo"], key="Shalo")
        for c0 in (0, GPAD + SEQ):
            for c in range(4):
                kb.dma("pool", S_glu[c * 128:(c + 1) * 128, c0:c0 + GPAD], zb[:, :], reads=["zb"],
                       writes=["Sgluhalo"], key="Shalo")

        def cpc(name, i=0):
            o = CP[name] + i
            return cp[:, o:o + 1]

        def load_cast(es_, name, src, rows, cols_list, dst_shape, view):
            dst = kb.sb(es_, name, dst_shape, BF16)
            return dst

        cm = {}

        def alloc_common(stk):
            cm["stage"] = [kb.sb(stk, f"stage{i}", [128, 1024], F32) for i in range(2)]
            cm["tmpm"] = [kb.sb(stk, f"tmpm{i}", [128, 1024], F32) for i in range(2)]
            cm["junk"] = kb.sb(stk, "junk", [128, 1024], BF16)

        stage_i = [0]

        def load_w_bf16(dst, src_ap, kchunks, ncols, dkey):
            step = 1024
            for k in range(kchunks):
                for n0 in range(0, ncols, step):
                    n1 = min(ncols, n0 + step)
                    i = stage_i[0] % 2
                    stage_i[0] += 1
                    st = cm["stage"][i]
                    kb.dma("sp", st[:, :n1 - n0], src_ap[k * 128:(k + 1) * 128, n0:n1], writes=[f"stage{i}"])
                    eng = "act" if (stage_i[0] % 2) else "dve"
                    if eng == "act":
                        op("act", lambda e: e.activation(out=dst[:, k, n0:n1], in_=st[:, :n1 - n0], func=AF.Copy),
                           reads=[f"stage{i}"], writes=[dkey])
                    else:
                        op("dve", lambda e: e.tensor_copy(out=dst[:, k, n0:n1], in_=st[:, :n1 - n0]),
                           reads=[f"stage{i}"], writes=[dkey])

        def load_small_bf16(dst2d, src_ap, rows, ncols, dkey):
            i = stage_i[0] % 2
            stage_i[0] += 1
            st = cm["stage"][i]
            kb.dma("sp", st[:rows, :ncols], src_ap, writes=[f"stage{i}"])
            op("dve", lambda e: e.tensor_copy(out=dst2d, in_=st[:rows, :ncols]), reads=[f"stage{i}"], writes=[dkey])

        ss = kb.sb(es, "ss", [128, 4], F32)
        rs = kb.sb(es, "rs", [128, 4], F32)
        tmpm_i = [0]

        def norm_mod_T(xt, xkey, nsub, A, B, akeys, hb, hT, tag):
            junk, tmpm = cm["junk"], cm["tmpm"]
            for j in range(nsub):
                op("act", lambda e: e.activation(out=junk[:], in_=xt[:, j, :], func=AF.Square,
                                                 accum_out=ss[:, j:j + 1]),
                   reads=[xkey], writes=["junk", "ss"])
            op("dve", lambda e: e.tensor_scalar(out=rs[:, :nsub], in0=ss[:, :nsub], scalar1=1.0 / D, scalar2=1e-6,
                                                op0=ALU.mult, op1=ALU.add), reads=["ss"], writes=["rs"])
            op("act", lambda e: e.activation(out=rs[:, :nsub], in_=rs[:, :nsub], func=AF.Sqrt), reads=["rs"],
               writes=["rs"])
            op("dve", lambda e: e.reciprocal(out=rs[:, :nsub], in_=rs[:, :nsub]), reads=["rs"], writes=["rs"])
            for j in range(nsub):
                i = tmpm_i[0] % 2
                tmpm_i[0] += 1
                tm = tmpm[i]
                op("dve", lambda e: e.scalar_tensor_tensor(out=tm[:], in0=xt[:, j, :], scalar=rs[:, j:j + 1], in1=A[:],
                                                           op0=ALU.mult, op1=ALU.mult),
                   reads=[xkey, "rs"] + akeys, writes=[f"tmpm{i}"])
                op("dve", lambda e: e.tensor_tensor(out=hb[:, j, :], in0=tm[:], in1=B[:], op=ALU.add),
                   reads=[f"tmpm{i}"] + akeys, writes=[f"hb{j}"])
            for c in range(8):
                pt, pk = kb.pb()
                for j in range(nsub):
                    op("pe", lambda e: e.transpose(out=pt[:, j * 128:(j + 1) * 128], in_=hb[:, j, c * 128:(c + 1) * 128],
                                                   identity=identb[:]),
                       reads=[f"hb{j}", "identb"], writes=[pk])
                eng = "act" if c % 2 == 0 else "dve"
                if eng == "act":
                    op("act", lambda e: e.activation(out=hT[:, c, :nsub * 128], in_=pt[:, :nsub * 128], func=AF.Copy),
                       reads=[pk], writes=[f"{tag}{c}"])
                else:
                    op("dve", lambda e: e.tensor_copy(out=hT[:, c, :nsub * 128], in_=pt[:, :nsub * 128]),
                       reads=[pk], writes=[f"{tag}{c}"])

        def mm(out_ap, lhsT, rhs, start, stop, reads, writes):
            op("pe", lambda e: e.matmul(out_ap, lhsT=lhsT, rhs=rhs, start=start, stop=stop), reads=reads, writes=writes)

        def adaln(es_, layer, want_gate_ctx, es_ctx=None, pre=None):
            t = dict(pre or {})
            for nme in ("A", "B", "G", "Ac", "Bc") + (("Gc",) if want_gate_ctx else ()):
                if nme in t:
                    continue
                stk = es_ctx if (es_ctx is not None and nme != "G") else es_
                t[nme] = kb.sb(stk, f"mod{layer}{nme}", [128, D], F32)
            with ExitStack() as sub:
                sil = kb.sb(sub, "sil", [128, 16], F32)
                sbc = kb.sb(sub, "sbc", [128, 16, 128], F32)
                brow = kb.sb(sub, "brow", [1, 3 * D], F32)
                gbc = kb.sb(sub, "gbc", [128, 1, D], F32)
                wts = [kb.sb(sub, f"adaw{i}", [128, 512], F32) for i in range(3)]
                op("act", lambda e: e.activation(out=sil[:], in_=ccol[:], func=AF.Silu), reads=["ccol"], writes=["sil"])
                for i in range(16):
                    op("dve", lambda e: e.tensor_scalar(out=sbc[:, i, :], in0=onesf[:], scalar1=sil[:, i:i + 1],
                                                        scalar2=None, op0=ALU.mult),
                       reads=["onesf", "sil"], writes=["sbc"])
                kb.dma("sp", brow[:], ada_b[layer], writes=["brow"])
                kb.dma("sp", gbc[:], norm_g[layer].partition_broadcast(128), writes=["gbc"])
                wi = 0
                for n in range(6):
                    px, pxk = kb.pf()
                    pc, pck = kb.pf()
                    for k in range(8):
                        w = wts[wi % 3]
                        wk = f"adaw{wi % 3}"
                        wi += 1
                        kb.dma("sp", w[:], ada_w[layer][k * 128:(k + 1) * 128, n * 512:(n + 1) * 512], writes=[wk])
                        mm(px[:], sbc[:, k, :], w[:], k == 0, False, ["sbc", wk], [pxk])
                        mm(pc[:], sbc[:, 8 + k, :], w[:], k == 0, False, ["sbc", wk], [pck])
                    mm(px[:], onesf[0:1, :], brow[0:1, n * 512:(n + 1) * 512], False, True, ["onesf", "brow"], [pxk])
                    mm(pc[:], onesf[0:1, :], brow[0:1, n * 512:(n + 1) * 512], False, True, ["onesf", "brow"], [pck])
                    part, half = n // 2, n % 2
                    hs = slice(half * 512, (half + 1) * 512)
                    for (p_, pk_, sfx) in ((px, pxk, ""), (pc, pck, "c")):
                        if part == 0:
                            dst = t["B" + sfx]
                            op("act", lambda e: e.activation(out=dst[:, hs], in_=p_[:], func=AF.Copy), reads=[pk_],
                               writes=[f"mod{layer}B{sfx}"])
                        elif part == 1:
                            dst = t["A" + sfx]
                            op("dve", lambda e: e.scalar_tensor_tensor(out=dst[:, hs], in0=p_[:], scalar=1.0,
                                                                       in1=gbc[:, 0, hs], op0=ALU.add, op1=ALU.mult),
                               reads=[pk_, "gbc"], writes=[f"mod{layer}A{sfx}"])
                        else:
                            if ("G" + sfx) in t:
                                dst = t["G" + sfx]
                                op("act", lambda e: e.activation(out=dst[:, hs], in_=p_[:], func=AF.Copy), reads=[pk_],
                                   writes=[f"mod{layer}G{sfx}"])
                            else:
                                op("act", lambda e: e.activation(out=sil[:, 0:8], in_=p_[:, 0:8], func=AF.Copy),
                                   reads=[pk_], writes=["sil"])
                kb.barrier()
            return t

        with ExitStack() as L0:
            alloc_common(L0)
            mod0 = adaln(L0, 0, True)
            ine = kb.sb(L0, "ine", [128, 8, 3072], BF16)
            oute = kb.sb(L0, "oute", [128, 8, D], BF16)
            poolw = kb.sb(L0, "poolw", [128, 4, 128], BF16)
            load_w_bf16(ine, in_e, 8, 3072, "ine")
            load_w_bf16(oute, out_e, 8, D, "oute")
            load_small_bf16(poolw[:].rearrange("p g d -> p (g d)"), pool_w, 128, 512, "poolw")
            invcx = kb.sb(L0, "invcx", [128, 4, 1, 64], F32)
            invcc = kb.sb(L0, "invcc", [128, 4, 1, 256], F32)
            kb.dma("sp", invcx[:, :, 0, :], invcx_in.rearrange("p (g t) -> p g t", g=4), writes=["invcx"])
            kb.dma("sp", invcc[:, :, 0, :], invcc_in.rearrange("p (g t) -> p g t", g=4), writes=["invcc"])
            xt = kb.sb(L0, "xt", [128, 4, D], F32)
            hb = kb.sb(L0, "hb", [128, 4, D], BF16)
            hTs = [kb.sb(L0, f"hT{i}", [128, 8, 512], BF16) for i in range(2)]
            ybuf = kb.sb(L0, "ybuf", [128, 8, 512], BF16)
            xn = [kb.sb(L0, f"xn{i}", [128, D], F32) for i in range(2)]
            t2 = [kb.sb(L0, f"t2_{i}", [128, 512], F32) for i in range(2)]
            sza = kb.sb(L0, "sza", [128, 512], BF16)
            szb = kb.sb(L0, "szb", [128, 512], BF16)
            vb = kb.sb(L0, "vb", [128, 512], F32)
            gb = kb.sb(L0, "gb", [128, 512], F32)
            pm = kb.sb(L0, "pm", [128, 512], BF16)
            dw = kb.sb(L0, "dw", [128, 512], F32)
            dw2 = kb.sb(L0, "dw2", [128, 512], F32)
            geo = {}
            for gname, nrows, rowlen in (("x", 8, 64), ("c", 1, 256)):
                W = rowlen + 32
                g_ = {"nrows": nrows, "rowlen": rowlen, "W": W}
                g_["upad"] = kb.sb(L0, f"upad{gname}", [128, nrows, W], F32)
                g_["sA"] = kb.sb(L0, f"sA{gname}", [128, nrows, W], F32)
                g_["sB"] = kb.sb(L0, f"sB{gname}", [128, nrows, W], F32)
                g_["cvpad"] = kb.sb(L0, f"cvpad{gname}", [128, nrows, rowlen + 2], F32)
                g_["invc"] = invcx if gname == "x" else invcc
                g_["invk"] = "invcx" if gname == "x" else "invcc"
                g_["k"] = gname
                op("dve", lambda e: e.memset(g_["upad"][:], 0.0), writes=[f"upad{gname}"])
                op("dve", lambda e: e.memset(g_["cvpad"][:], 0.0), writes=[f"cvpad{gname}"])
                op("dve", lambda e: e.memset(g_["sA"][:], 0.0), writes=[f"sA{gname}"])
                op("dve", lambda e: e.memset(g_["sB"][:], 0.0), writes=[f"sB{gname}"])
                geo[gname] = g_
            xn_i = [0]

            def l0_front(src, ntok, A, B, akeys, par):
                nsub = ntok // 128
                kb.dma("sp", xt[:, :nsub, :], src.rearrange("(j p) d -> p j d", p=128), writes=["xt"])
                norm_mod_T(xt, "xt", nsub, A, B, akeys, hb, hTs[par], f"hT{par}_")

            def l0_back(src, dst, ntok, g_, G, akeys, par):
                nsub = ntok // 128
                hT = hTs[par]
                hkey = lambda c: f"hT{par}_{c}"
                nrows, rowlen, W, gk = g_["nrows"], g_["rowlen"], g_["W"], g_["k"]
                upad, sA, sB, cvpad = g_["upad"], g_["sA"], g_["sB"], g_["cvpad"]

                def proj(m):
                    ps, pk = kb.pf()
                    for k in range(8):
                        mm(ps[:, :ntok], ine[:, k, m * 128:(m + 1) * 128], hT[:, k, :ntok], k == 0, k == 7,
                           ["ine", hkey(k)], [pk])
                    return ps, pk

                def rows(ap2d):
                    return ap2d.rearrange("p (r l) -> p r l", r=nrows)

                for g in range(4):
                    ps, pk = proj(4 + g)
                    op("act", lambda e: e.activation(out=sza[:, :ntok], in_=ps[:, :ntok], func=AF.Silu), reads=[pk],
                       writes=["sza"])
                    ps, pk = proj(g)
                    op("act", lambda e: e.activation(out=upad[:, :, 16:16 + rowlen], in_=rows(ps[:, :ntok]),
                                                     func=AF.Copy), reads=[pk], writes=[f"upad{gk}"])
                    op("dve", lambda e: e.tensor_tensor(out=sA[:, :, 1:W], in0=upad[:, :, 0:W - 1], in1=upad[:, :, 1:W],
                                                        op=ALU.add), reads=[f"upad{gk}"], writes=[f"sA{gk}"])
                    cur, curk, oth, othk = sA, f"sA{gk}", sB, f"sB{gk}"
                    lo, hi = 1, W
                    for lvl in range(g):
                        sh = 1 << lvl
                        nlo, nhi = lo + sh, hi - sh
                        op("dve", lambda e: e.tensor_tensor(out=oth[:, :, nlo:nhi], in0=cur[:, :, nlo - sh:nhi - sh],
                                                            in1=cur[:, :, nlo + sh:nhi + sh], op=ALU.add),
                           reads=[curk], writes=[othk])
                        cur, curk, oth, othk = oth, othk, cur, curk
                        lo, hi = nlo, nhi
                    op("dve", lambda e: e.tensor_tensor(out=rows(dw[:, :ntok]), in0=cur[:, :, 16:16 + rowlen],
                                                        in1=g_["invc"][:, g, :, :].broadcast_to([128, nrows, rowlen]),
                                                        op=ALU.mult), reads=[curk, g_["invk"]], writes=["dw"])
                    op("dve", lambda e: e.tensor_tensor(out=rows(pm[:, :ntok]), in0=rows(dw[:, :ntok]),
                                                        in1=upad[:, :, 16:16 + rowlen], op=ALU.subtract),
                       reads=["dw", f"upad{gk}"], writes=["pm"])
                    ps, pk = kb.pf()
                    mm(ps[:, :ntok], poolw[:, g, :], pm[:, :ntok], True, True, ["poolw", "pm"], [pk])
                    op("dve", lambda e: e.scalar_tensor_tensor(out=ybuf[:, g, :ntok], in0=ps[:, :ntok],
                                                               scalar=cpc("pool_scale", g), in1=sza[:, :ntok],
                                                               op0=ALU.mult, op1=ALU.mult),
                       reads=[pk, "cp", "sza"], writes=[f"y{g}"])
                for c in range(4):
                    ps, pk = proj(8 + c)
                    op("act", lambda e: e.activation(out=vb[:, :ntok], in_=ps[:, :ntok], func=AF.Copy), reads=[pk],
                       writes=["vb"])
                    ps, pk = proj(12 + c)
                    op("act", lambda e: e.activation(out=gb[:, :ntok], in_=ps[:, :ntok], func=AF.Copy), reads=[pk],
                       writes=["gb"])
                    ps, pk = proj(20 + c)
                    op("act", lambda e: e.activation(out=szb[:, :ntok], in_=ps[:, :ntok], func=AF.Silu), reads=[pk],
                       writes=["szb"])
                    ps, pk = proj(16 + c)
                    op("dve", lambda e: e.tensor_tensor(out=cvpad[:, :, 1:1 + rowlen], in0=rows(ps[:, :ntok]),
                                                        in1=rows(vb[:, :ntok]), op=ALU.mult),
                       reads=[pk, "vb"], writes=[f"cvpad{gk}"])
                    op("dve", lambda e: e.tensor_scalar(out=rows(dw[:, :ntok]), in0=cvpad[:, :, 0:rowlen],
                                                        scalar1=cpc("sconv_w", 0 * 4 + c), scalar2=None, op0=ALU.mult),
                       reads=[f"cvpad{gk}", "cp"], writes=["dw"])
                    op("dve", lambda e: e.scalar_tensor_tensor(out=rows(dw2[:, :ntok]), in0=cvpad[:, :, 1:1 + rowlen],
                                                               scalar=cpc("sconv_w", 1 * 4 + c), in1=rows(dw[:, :ntok]),
                                                               op0=ALU.mult, op1=ALU.add),
                       reads=[f"cvpad{gk}", "cp", "dw"], writes=["dw2"])
                    op("dve", lambda e: e.scalar_tensor_tensor(out=rows(dw[:, :ntok]), in0=cvpad[:, :, 2:2 + rowlen],
                                                               scalar=cpc("sconv_w", 2 * 4 + c), in1=rows(dw2[:, :ntok]),
                                                               op0=ALU.mult, op1=ALU.add),
                       reads=[f"cvpad{gk}", "cp", "dw2"], writes=["dw"])
                    op("dve", lambda e: e.tensor_tensor(out=dw2[:, :ntok], in0=dw[:, :ntok], in1=gb[:, :ntok],
                                                         op=ALU.mult), reads=["dw", "gb"], writes=["dw2"])
                    op("dve", lambda e: e.tensor_tensor(out=ybuf[:, 4 + c, :ntok], in0=dw2[:, :ntok], in1=szb[:, :ntok],
                                                         op=ALU.mult), reads=["dw2", "szb"], writes=[f"y{4 + c}"])
                ykeys = [f"y{c}" for c in range(8)]
                for j in range(nsub):
                    i = xn_i[0] % 2
                    xn_i[0] += 1
                    kb.dma("pool", xn[i][:], src[j * 128:(j + 1) * 128, :], writes=[f"xn{i}"], key=f"xnld{i}")
                    for half in range(2):
                        hs = slice(half * 512, (half + 1) * 512)
                        ps, pk = kb.pf()
                        for c in range(8):
                            mm(ps[:], ybuf[:, c, j * 128:(j + 1) * 128], oute[:, c, hs], c == 0, c == 7,
                               [f"y{c}", "oute"], [pk])
                        tt = t2[half]
                        op("dve", lambda e: e.tensor_tensor(out=tt[:], in0=ps[:], in1=G[:, hs], op=ALU.mult),
                           reads=[pk] + akeys, writes=[f"t2_{half}"])
                        op("dve", lambda e: e.tensor_tensor(out=xn[i][:, hs], in0=tt[:], in1=xn[i][:, hs], op=ALU.add),
                           reads=[f"t2_{half}", f"xn{i}"], writes=[f"xn{i}"])
                    kb.dma("pool", dst[j * 128:(j + 1) * 128, :], xn[i][:], reads=[f"xn{i}"], writes=["X1scr"],
                           key="xnst")

            tiles0 = [(ctx_in, XC1, CTX, geo["c"], mod0["Ac"], mod0["Bc"], mod0["Gc"], ["mod0Ac", "mod0Bc", "mod0Gc"])]
            for it in range(NT):
                tiles0.append((x_in[it * 512:(it + 1) * 512, :], X1[it * 512:(it + 1) * 512, :], 512, geo["x"],
                               mod0["A"], mod0["B"], mod0["G"], ["mod0A", "mod0B", "mod0G"]))
            for ti, (src, dst, ntok, g_, A_, B_, G_, ak_) in enumerate(tiles0):
                if ti == 0:
                    l0_front(src, ntok, A_, B_, ak_, 0)
                if ti + 1 < len(tiles0):
                    nx = tiles0[ti + 1]
                    l0_front(nx[0], nx[2], nx[4], nx[5], nx[7], (ti + 1) % 2)
                l0_back(src, dst, ntok, g_, G_, ak_, ti % 2)
            kb.barrier()

        if "STOP_L0" in dump:
            kb.finish(["X1scr"])
            return nc

        with ExitStack() as L1:
            G1pre = kb.sb(L1, "mod1G", [128, D], F32)
            with ExitStack() as L1a:
                alloc_common(L1a)
                mod1 = adaln(L1, 1, False, es_ctx=L1a, pre={"G": G1pre})
                ino = kb.sb(L1a, "ino", [128, 8, 4096], BF16)
                load_w_bf16(ino, in_o, 8, 4096, "ino")
                xt = kb.sb(L1a, "xt1", [128, 4, D], F32)
                hb = kb.sb(L1a, "hb1", [128, 4, D], BF16)
                hT1s = [kb.sb(L1a, f"hT1_{i}", [128, 8, 512], BF16) for i in range(2)]
                obuf = [kb.sb(L1a, f"obuf{s}", [128, 4, 512], BF16) for s in range(7)]
                p1t = kb.sb(L1a, "p1t", [128, 512], F32)
                sgt = kb.sb(L1a, "sgt", [128, 512], F32)

                def l1_front(src, ntok, A, B, akeys, par):
                    nsub = ntok // 128
                    kb.dma("sp", xt[:, :nsub, :], src.rearrange("(j p) d -> p j d", p=128), reads=["X1scr"],
                           writes=["xt1"])
                    norm_mod_T(xt, "xt1", nsub, A, B, akeys, hb, hT1s[par], f"hU{par}_")

                def l1_back(ntok, col0, lat0, is_ctx, par):
                    hT = hT1s[par]

                    def proj(m):
                        ps, pk = kb.pf()
                        for k in range(8):
                            mm(ps[:, :ntok], ino[:, k, m * 128:(m + 1) * 128], hT[:, k, :ntok], k == 0, k == 7,
                               ["ino", f"hU{par}_{k}"], [pk])
                        return ps, pk

                    for s, S in enumerate((S_u, S_r, S_k, S_v)):
                        for c in range(4):
                            ps, pk = proj(s * 4 + c)
                            if c % 2 == 0:
                                op("act", lambda e: e.activation(out=obuf[s][:, c, :ntok], in_=ps[:, :ntok],
                                                                 func=AF.Copy), reads=[pk], writes=[f"obuf{s}"])
                            else:
                                op("dve", lambda e: e.tensor_copy(out=obuf[s][:, c, :ntok], in_=ps[:, :ntok]),
                                   reads=[pk], writes=[f"obuf{s}"])
                        kb.dma("pool", S[:, col0:col0 + ntok].rearrange("(c p) n -> p c n", p=128),
                               obuf[s][:, :, :ntok], reads=[f"obuf{s}"], writes=["Sstreams"], key=f"obst{s}")
                    if is_ctx:
                        return
                    for c in range(4):
                        ps, pk = proj(16 + c)
                        op("act", lambda e: e.activation(out=obuf[4][:, c, :], in_=ps[:], func=AF.Silu), reads=[pk],
                           writes=["obuf4"])
                        ps, pk = proj(28 + c)
                        op("act", lambda e: e.activation(out=obuf[5][:, c, :], in_=ps[:], func=AF.Silu), reads=[pk],
                           writes=["obuf5"])
                        ps, pk = proj(20 + c)
                        op("act", lambda e: e.activation(out=p1t[:], in_=ps[:], func=AF.Copy), reads=[pk],
                           writes=["p1t"])
                        ps, pk = proj(24 + c)
                        op("act", lambda e: e.activation(out=sgt[:], in_=ps[:], func=AF.Sigmoid), reads=[pk],
                           writes=["sgt"])
                        op("dve", lambda e: e.tensor_tensor(out=obuf[6][:, c, :], in0=p1t[:], in1=sgt[:], op=ALU.mult),
                           reads=["p1t", "sgt"], writes=["obuf6"])
                    kb.dma("pool", S_zc[:, lat0:lat0 + 512].rearrange("(c p) n -> p c n", p=128), obuf[4][:],
                           reads=["obuf4"], writes=["Sstreams"], key="obst4")
                    kb.dma("pool", S_zd[:, lat0:lat0 + 512].rearrange("(c p) n -> p c n", p=128), obuf[5][:],
                           reads=["obuf5"], writes=["Sstreams"], key="obst5")
                    kb.dma("pool", S_glu[:, GPAD + lat0:GPAD + lat0 + 512].rearrange("(c p) n -> p c n", p=128),
                           obuf[6][:], reads=["obuf6"], writes=["Sstreams"], key="obst6")

                tiles1 = [(XC1, CTX, CTX0, 0, mod1["Ac"], mod1["Bc"], ["mod1Ac", "mod1Bc"], True)]
                for it in range(NT):
                    tiles1.append((X1[it * 512:(it + 1) * 512, :], 512, LAT0 + it * 512, it * 512, mod1["A"], mod1["B"],
                                   ["mod1A", "mod1B"], False))
                for ti, (src, ntok, col0, lat0, A_, B_, ak_, isc) in enumerate(tiles1):
                    if ti == 0:
                        l1_front(src, ntok, A_, B_, ak_, 0)
                    if ti + 1 < len(tiles1):
                        nx = tiles1[ti + 1]
                        l1_front(nx[0], nx[1], nx[4], nx[5], nx[6], (ti + 1) % 2)
                    l1_back(ntok, col0, lat0, isc, ti % 2)
                kb.barrier()

            if "STOP_L1A" in dump:
                kb.finish(["Sstreams"])
                return nc

            with ExitStack() as L1b:
                alloc_common(L1b)
                w1b = kb.sb(L1b, "w1b", [128, 4, 64], BF16)
                a1b = kb.sb(L1b, "a1b", [128, 4, 64], BF16)
                g1b = kb.sb(L1b, "g1b", [128, 4, 96], BF16)
                w2b = kb.sb(L1b, "w2b", [64, 512], BF16)
                a2b = kb.sb(L1b, "a2b", [64, 512], BF16)
                g2b = kb.sb(L1b, "g2b", [96, 512], BF16)
                load_small_bf16(w1b[:].rearrange("p c r -> p (c r)"), w1l, 128, 256, "w1b")
                load_small_bf16(a1b[:].rearrange("p c r -> p (c r)"), a1l, 128, 256, "a1b")
                load_small_bf16(g1b[:].rearrange("p c r -> p (c r)"), g1l, 128, 384, "g1b")
                load_small_bf16(w2b[:], w2l, 64, 512, "w2b")
                load_small_bf16(a2b[:], a2l, 64, 512, "a2b")
                load_small_bf16(g2b[:], g2l, 96, 512, "g2b")
                LS = []
                for p_ in range(2):
                    d_ = {"ut": kb.sb(L1b, f"ut{p_}", [128, 4, 514], BF16),
                          "nbt": kb.sb(L1b, f"nbt{p_}", [128, 4, 512], F32),
                          "um": [kb.sb(L1b, f"um{p_}_{j}", [128, 4, 512], BF16) for j in range(3)],
                          "hw": kb.sb(L1b, f"hw{p_}", [64, 512], BF16),
                          "ha": kb.sb(L1b, f"ha{p_}", [64, 512], BF16),
                          "hg": kb.sb(L1b, f"hg{p_}", [96, 512], BF16),
                          "swo": [kb.sb(L1b, f"swo{p_}_{d}", [128, 4, 512], F32) for d in range(2)],
                          "ao": [kb.sb(L1b, f"ao{p_}_{d}", [128, 4, 512], BF16) for d in range(2)],
                          "go": kb.sb(L1b, f"go{p_}", [128, 4, 512], BF16)}
                    LS.append(d_)
                dw_l = kb.sb(L1b, "dw_l", [128, 512], F32)

                def lora_front(col0, ntok, p_):
                    d_ = LS[p_]
                    u, nbt, um = d_["ut"], d_["nbt"], d_["um"]
                    uk, nk = f"ut{p_}", f"nbt{p_}"
                    kb.dma("sp", u[:, :, :ntok + 2], S_u[:, col0 - 1:col0 + ntok + 1].rearrange("(c p) n -> p c n", p=128),
                           reads=["Sstreams", "Shalo"], writes=[uk])
                    op("dve", lambda e: e.tensor_tensor(out=nbt[:, :, :ntok], in0=u[:, :, 0:ntok], in1=u[:, :, 2:ntok + 2],
                                                        op=ALU.add), reads=[uk], writes=[nk])
                    op("dve", lambda e: e.scalar_tensor_tensor(out=nbt[:, :, :ntok], in0=nbt[:, :, :ntok], scalar=0.5,
                                                               in1=u[:, :, 1:ntok + 1], op0=ALU.mult, op1=ALU.subtract),
                       reads=[nk, uk], writes=[nk])
                    for j in range(3):
                        for c in range(4):
                            if c != 3:
                                op("dve", lambda e: e.scalar_tensor_tensor(out=um[j][:, c, :ntok], in0=nbt[:, c, :ntok],
                                                                           scalar=cpc("mu", (3 + j) * 4 + c),
                                                                           in1=u[:, c, 1:ntok + 1], op0=ALU.mult,
                                                                           op1=ALU.add),
                                   reads=[nk, uk, "cp"], writes=[f"um{p_}_{j}"])
                            else:
                                op("dve", lambda e: e.tensor_scalar(out=dw_l[:, :ntok], in0=nbt[:, c, :ntok],
                                                                     scalar1=cpc("mu", (3 + j) * 4 + c), scalar2=None,
                                                                     op0=ALU.mult),
                                   reads=[nk, "cp"], writes=["dw_l"])
                                op("dve", lambda e: e.tensor_tensor(out=um[j][:, c, :ntok], in0=dw_l[:, :ntok],
                                                                     in1=u[:, c, 1:ntok + 1], op=ALU.add),
                                   reads=["dw_l", uk], writes=[f"um{p_}_{j}"])

                def lora_back(tau0, ntok, p_):
                    d_ = LS[p_]
                    um, hw, ha, hg, swo, ao, go = d_["um"], d_["hw"], d_["ha"], d_["hg"], d_["swo"], d_["ao"], d_["go"]
                    for j, (wb_, hid, nh, fn) in enumerate(((w1b, hw, 64, AF.Tanh), (a1b, ha, 64, AF.Copy),
                                                             (g1b, hg, 96, AF.Sigmoid))):
                        ps, pk = kb.pf()
                        for c in range(4):
                            mm(ps[:nh, :ntok], wb_[:, c, :], um[j][:, c, :ntok], c == 0, c == 3,
                               [("w1b", "a1b", "g1b")[j], f"um{p_}_{j}"], [pk])
                        op("act", lambda e: e.activation(out=hid[:, :ntok], in_=ps[:nh, :ntok], func=fn), reads=[pk],
                           writes=[("hw", "ha", "hg")[j] + str(p_)])
                    for d in range(2):
                        for c in range(4):
                            ps, pk = kb.pf()
                            mm(ps[:, :ntok], w2b[32 * d:32 * d + 32, c * 128:(c + 1) * 128], hw[32 * d:32 * d + 32, :ntok],
                               True, True, ["w2b", f"hw{p_}"], [pk])
                            op("act", lambda e: e.activation(out=swo[d][:, c, :ntok], in_=ps[:, :ntok], func=AF.Sigmoid,
                                                             bias=cpc("w0", d * 4 + c), scale=1.0),
                               reads=[pk, "cp"], writes=[f"swo{p_}_{d}"])
                            ps, pk = kb.pf()
                            mm(ps[:, :ntok], a2b[32 * d:32 * d + 32, c * 128:(c + 1) * 128], ha[32 * d:32 * d + 32, :ntok],
                               True, True, ["a2b", f"ha{p_}"], [pk])
                            op("act", lambda e: e.activation(out=ao[d][:, c, :ntok], in_=ps[:, :ntok], func=AF.Sigmoid,
                                                             bias=cpc("a0", d * 4 + c), scale=1.0),
                               reads=[pk, "cp"], writes=[f"ao{p_}_{d}"])
                        kb.dma("pool", S_sw[d][:, tau0:tau0 + ntok].rearrange("(c p) n -> p c n", p=128),
                               swo[d][:, :, :ntok], reads=[f"swo{p_}_{d}"], writes=["Slora"], key=f"swst{d}")
                        kb.dma("pool", S_a[d][:, tau0:tau0 + ntok].rearrange("(c p) n -> p c n", p=128),
                               ao[d][:, :, :ntok], reads=[f"ao{p_}_{d}"], writes=["Slora"], key=f"aost{d}")
                    for c in range(4):
                        ps, pk = kb.pf()
                        mm(ps[:, :ntok], g2b[:, c * 128:(c + 1) * 128], hg[:, :ntok], True, True, ["g2b", f"hg{p_}"], [pk])
                        op("dve", lambda e: e.tensor_copy(out=go[:, c, :ntok], in_=ps[:, :ntok]), reads=[pk],
                           writes=[f"go{p_}"])
                    kb.dma("pool", S_g[:, tau0:tau0 + ntok].rearrange("(c p) n -> p c n", p=128), go[:, :, :ntok],
                           reads=[f"go{p_}"], writes=["Slora"], key="gost")

                ltiles = [(CTX0, 0, CTX)] + [(LAT0 + it * 512, CTX + it * 512, 512) for it in range(NT)]
                lora_front(ltiles[0][0], ltiles[0][2], 0)
                for ti, (col0, tau0, ntok) in enumerate(ltiles):
                    if ti + 1 < len(ltiles):
                        lora_front(ltiles[ti + 1][0], ltiles[ti + 1][2], (ti + 1) % 2)
                    lora_back(tau0, ntok, ti % 2)
                kb.barrier()

            if "STOP_L1B" in dump:
                kb.finish(["Slora", "Sstreams"])
                return nc

            with ExitStack() as SC:
                scan_phase(nc, kb, SC, SEQ, NT, cp, cpc, identf, identb, masks_in, blk, onesf,
                           S_r, S_k, S_v, S_sw, S_a, S_g, S_zc, S_of, S_y, mm, dump=dump)
                kb.barrier()

            if "STOP_SCAN" in dump:
                kb.finish(["Sy", "Sof"])
                return nc

            with ExitStack() as CF:
                cdiag = kb.sb(CF, "cdiag", [128, 124, 128], BF16)
                for j in range(31):
                    for c in range(4):
                        eng = "dve"
                        op(eng, lambda e: e.tensor_scalar(out=cdiag[:, j * 4 + c, :], in0=identf[:],
                                                          scalar1=cpc("conf_dw_w", j * 4 + c), scalar2=None,
                                                          op0=ALU.mult), reads=["identf", "cp"], writes=["cdiag"])
                gl = [kb.sb(CF, f"gl{i}", [128, 512 + 2 * GPAD], BF16) for i in range(3)]
                CS = []
                for p_ in range(2):
                    d_ = {}
                    d_["hc"] = kb.sb(CF, f"hc{p_}", [128, 4, 512], F32)
                    d_["hcb"] = kb.sb(CF, f"hcb{p_}", [128, 4, 512], BF16)
                    d_["hsq"] = kb.sb(CF, f"hsq{p_}", [128, 4, 512], BF16)
                    d_["szd"] = kb.sb(CF, f"szd{p_}", [128, 4, 512], BF16)
                    d_["yc"] = kb.sb(CF, f"yc{p_}", [128, 4, 512], BF16)
                    CS.append(d_)
                mean = kb.sb(CF, "mean", [128, 512], F32)
                msq = kb.sb(CF, "msq", [128, 512], F32)
                rstd = kb.sb(CF, "rstd", [128, 512], F32)
                t1s = [kb.sb(CF, f"t1_{i}", [128, 512], F32) for i in range(2)]
                t3s = [kb.sb(CF, f"t3_{i}", [128, 512], F32) for i in range(2)]
                gi = [0]
                cacc = [kb.sb(CF, f"cacc{i}", [128, 512], F32) for i in range(2)]
                cacc_i = [0]

                def conf_A(it):
                    p_ = it % 2
                    d_ = CS[p_]
                    t0 = it * 512
                    kb.dma("sp", d_["szd"][:], S_zd[:, t0:t0 + 512].rearrange("(c p) n -> p c n", p=128),
                           reads=["Sstreams"], writes=[f"szd{p_}"])
                    for c in range(4):
                        g_ = gl[gi[0] % 3]
                        gk = f"gl{gi[0] % 3}"
                        gi[0] += 1
                        kb.dma("sp", g_[:], S_glu[c * 128:(c + 1) * 128, t0:t0 + 512 + 2 * GPAD],
                               reads=["Sstreams", "Sgluhalo"], writes=[gk])
                        NPE = 21
                        ps, pk = kb.pf()
                        for j in range(NPE):
                            mm(ps[:], cdiag[:, j * 4 + c, :], g_[:, 64 * j:64 * j + 512], j == 0, j == NPE - 1,
                               ["cdiag", gk], [pk])
                        ac = cacc[cacc_i[0] % 2]
                        ack = f"cacc{cacc_i[0] % 2}"
                        cacc_i[0] += 1
                        for j in range(NPE, 31):
                            if j == NPE:
                                op("dve", lambda e: e.tensor_scalar(out=ac[:], in0=g_[:, 64 * j:64 * j + 512],
                                                                    scalar1=cpc("conf_dw_w", j * 4 + c), scalar2=None,
                                                                    op0=ALU.mult), reads=[gk, "cp"], writes=[ack])
                            else:
                                op("dve", lambda e: e.scalar_tensor_tensor(out=ac[:], in0=g_[:, 64 * j:64 * j + 512],
                                                                           scalar=cpc("conf_dw_w", j * 4 + c), in1=ac[:],
                                                                           op0=ALU.mult, op1=ALU.add),
                                   reads=[gk, "cp", ack], writes=[ack])
                        op("dve", lambda e: e.scalar_tensor_tensor(out=d_["hc"][:, c, :], in0=ps[:],
                                                                   scalar=cpc("conf_dw_b", c), in1=ac[:], op0=ALU.add,
                                                                   op1=ALU.add),
                           reads=[pk, "cp", ack], writes=[f"hc{p_}_{c}"])
                        op("act", lambda e: e.activation(out=d_["hsq"][:, c, :], in_=d_["hc"][:, c, :], func=AF.Square),
                           reads=[f"hc{p_}_{c}"], writes=[f"hsq{p_}_{c}"])
                        op("act", lambda e: e.activation(out=d_["hcb"][:, c, :], in_=d_["hc"][:, c, :], func=AF.Copy),
                           reads=[f"hc{p_}_{c}"], writes=[f"hcb{p_}_{c}"])

                def conf_B(it):
                    p_ = it % 2
                    d_ = CS[p_]
                    t0 = it * 512
                    pm_, pmk = kb.pf()
                    pq_, pqk = kb.pf()
                    for c in range(4):
                        mm(pm_[:], onesb[:], d_["hcb"][:, c, :], c == 0, c == 3, ["onesb", f"hcb{p_}_{c}"], [pmk])
                    for c in range(4):
                        mm(pq_[:], onesb[:], d_["hsq"][:, c, :], c == 0, c == 3, ["onesb", f"hsq{p_}_{c}"], [pqk])
                    op("dve", lambda e: e.tensor_scalar(out=mean[:], in0=pm_[:], scalar1=1.0 / DB, scalar2=None,
                                                        op0=ALU.mult), reads=[pmk], writes=["mean"])
                    op("dve", lambda e: e.tensor_tensor(out=msq[:], in0=mean[:], in1=mean[:], op=ALU.mult),
                       reads=["mean"], writes=["msq"])
                    op("dve", lambda e: e.scalar_tensor_tensor(out=rstd[:], in0=pq_[:], scalar=1.0 / DB, in1=msq[:],
                                                               op0=ALU.mult, op1=ALU.subtract),
                       reads=[pqk, "msq"], writes=["rstd"])
                    op("dve", lambda e: e.tensor_scalar(out=rstd[:], in0=rstd[:], scalar1=1e-5, scalar2=None,
                                                        op0=ALU.add), reads=["rstd"], writes=["rstd"])
                    op("act", lambda e: e.activation(out=rstd[:], in_=rstd[:], func=AF.Sqrt), reads=["rstd"],
                       writes=["rstd"])
                    op("dve", lambda e: e.reciprocal(out=rstd[:], in_=rstd[:]), reads=["rstd"], writes=["rstd"])
                    for c in range(4):
                        t1, t3 = t1s[c % 2], t3s[c % 2]
                        t1k, t3k = f"t1_{c % 2}", f"t3_{c % 2}"
                        op("dve", lambda e: e.tensor_tensor(out=t1[:], in0=d_["hc"][:, c, :], in1=mean[:], op=ALU.subtract),
                           reads=[f"hc{p_}_{c}", "mean"], writes=[t1k])
                        op("dve", lambda e: e.tensor_tensor(out=t3[:], in0=t1[:], in1=rstd[:], op=ALU.mult),
                           reads=[t1k, "rstd"], writes=[t3k])
                        op("act", lambda e: e.activation(out=t1[:], in_=t3[:], func=AF.Silu,
                                                         bias=cpc("conf_ln_b", c), scale=cpc("conf_ln_g", c)),
                           reads=[t3k, "cp"], writes=[t1k])
                        op("dve", lambda e: e.tensor_tensor(out=d_["yc"][:, c, :], in0=t1[:], in1=d_["szd"][:, c, :],
                                                            op=ALU.mult), reads=[t1k, f"szd{p_}"], writes=[f"yc{p_}"])
                    kb.dma("pool", S_y[DB:D, t0:t0 + 512].rearrange("(c p) n -> p c n", p=128), d_["yc"][:],
                           reads=[f"yc{p_}"], writes=[f"Syc{p_}"], key=f"ycst{p_}")

                fuse_post = "STOP_CONF" not in dump
                if fuse_post:
                    cm["stage"] = [kb.sb(CF, f"stageP{i}", [128, 1024], F32) for i in range(2)]
                    cm["junk"] = kb.sb(CF, "junkP", [128, 1024], BF16)
                    outo = kb.sb(CF, "outo", [128, 8, D], BF16)
                    load_w_bf16(outo, out_o, 8, D, "outo")
                    fg = kb.sb(CF, "fg", [128, 1, D], F32)
                    kb.dma("sp", fg[:], final_g.partition_broadcast(128), writes=["fg"])
                    G1 = mod1["G"]
                    yt = [kb.sb(CF, f"yt{i}", [128, 8, 512], BF16) for i in range(2)]
                    x1t = kb.sb(CF, "x1t", [128, 4, D], F32)
                    x2 = [kb.sb(CF, f"x2_{i}", [128, D], F32) for i in range(2)]
                    ot = [kb.sb(CF, f"ot{i}", [128, D], F32) for i in range(2)]
                    t2 = [kb.sb(CF, f"t2p{i}", [128, 512], F32) for i in range(2)]
                    ssp = kb.sb(CF, "ssp", [128, 2], F32)
                    rsp = kb.sb(CF, "rsp", [128, 2], F32)
                xi = [0]

                def post_tile(it):
                    t0 = it * 512
                    y_ = yt[it % 2]
                    kb.dma("sp", y_[:], S_y[:, t0:t0 + 512].rearrange("(c p) n -> p c n", p=128),
                           reads=["Sy", f"Syc{it % 2}"], writes=[f"yt{it % 2}"])
                    kb.dma("sp", x1t[:], X1[t0:t0 + 512, :].rearrange("(j p) d -> p j d", p=128), reads=["X1scr"],
                           writes=["x1t"])
                    for j in range(4):
                        i = xi[0] % 2
                        xi[0] += 1
                        for half in range(2):
                            hs = slice(half * 512, (half + 1) * 512)
                            ps, pk = kb.pf()
                            for c in range(8):
                                mm(ps[:], y_[:, c, j * 128:(j + 1) * 128], outo[:, c, hs], c == 0, c == 7,
                                   [f"yt{it % 2}", "outo"], [pk])
                            op("dve", lambda e: e.tensor_tensor(out=t2[half][:], in0=ps[:], in1=G1[:, hs], op=ALU.mult),
                               reads=[pk, "mod1G"], writes=[f"t2p{half}"])
                            op("dve", lambda e: e.tensor_tensor(out=x2[i][:, hs], in0=t2[half][:], in1=x1t[:, j, hs],
                                                                op=ALU.add),
                               reads=[f"t2p{half}", "x1t"], writes=[f"x2_{i}"])
                        op("act", lambda e: e.activation(out=cm["junk"][:], in_=x2[i][:], func=AF.Square,
                                                         accum_out=ssp[:, i:i + 1]),
                           reads=[f"x2_{i}"], writes=["junk", f"ssp{i}"])
                        op("dve", lambda e: e.tensor_scalar(out=rsp[:, i:i + 1], in0=ssp[:, i:i + 1], scalar1=1.0 / D,
                                                            scalar2=1e-6, op0=ALU.mult, op1=ALU.add),
                           reads=[f"ssp{i}"], writes=[f"rsp{i}"])
                        op("act", lambda e: e.activation(out=rsp[:, i:i + 1], in_=rsp[:, i:i + 1], func=AF.Sqrt),
                           reads=[f"rsp{i}"], writes=[f"rsp{i}"])
                        op("dve", lambda e: e.reciprocal(out=rsp[:, i:i + 1], in_=rsp[:, i:i + 1]), reads=[f"rsp{i}"],
                           writes=[f"rsp{i}"])
                        op("dve", lambda e: e.scalar_tensor_tensor(out=ot[i][:], in0=x2[i][:], scalar=rsp[:, i:i + 1],
                                                                   in1=fg[:, 0, :], op0=ALU.mult, op1=ALU.mult),
                           reads=[f"x2_{i}", f"rsp{i}", "fg"], writes=[f"ot{i}"])
                        kb.dma("pool", out[t0 + j * 128:t0 + (j + 1) * 128, :], ot[i][:], reads=[f"ot{i}"],
                               writes=["OUT"], key=f"otst{i}")

                conf_A(0)
                for it in range(NT):
                    if it + 1 < NT:
                        conf_A(it + 1)
                    conf_B(it)
                    if fuse_post and it >= 1:
                        post_tile(it - 1)
                if fuse_post:
                    post_tile(NT - 1)
                kb.barrier()

            if "STOP_CONF" in dump:
                kb.finish(["Sy", "Syc0", "Syc1"])
                return nc
        kb.finish(["OUT"])
    return nc


def scan_phase(nc, kb, SC, SEQ, NT, cp, cpc, identf, identb, masks_in, blk, onesf,
               S_r, S_k, S_v, S_sw, S_a, S_g, S_zc, S_of, S_y, mm, dump=()):
    op = kb.op
    masks = kb.sb(SC, "masks", [128, 2, 6, 128], F32)
    kb.dma("sp", masks[:], masks_in.rearrange("p (d m t) -> p d m t", d=2, m=6), writes=["masks"])
    blkb = kb.sb(SC, "blkb", [128, 128], BF16)
    op("dve", lambda e: e.tensor_copy(out=blkb[:], in_=blk[:]), reads=["blk"], writes=["blkb"])
    sqb = kb.sb(SC, "sqb", [128, 512], BF16)
    omka = kb.sb(SC, "omka", [128, 4], F32)
    kac = cp[:, CP["k_a"]:CP["k_a"] + 4]
    op("dve", lambda e: e.tensor_scalar(out=omka[:], in0=kac, scalar1=-1.0, scalar2=1.0, op0=ALU.mult, op1=ALU.add),
       reads=["cp"], writes=["omka"])
    ones128 = kb.sb(SC, "ones128", [128, 128], F32)
    op("dve", lambda e: e.memset(ones128[:], 1.0), writes=["ones128"])
    NSLOT = 6

    class NS:
        pass

    TMP = NS()
    for n in ("rt", "kt", "vt"):
        setattr(TMP, n, kb.sb(SC, f"{n}T", [128, 514], BF16))
    for n in ("atd", "at0", "vb16", "Af"):
        setattr(TMP, n, kb.sb(SC, f"{n}T", [128, 512], BF16))
    for n in ("swt", "rp", "kp", "vp", "kkr", "kkn", "kd", "bb", "lw", "cs", "csr", "ig", "gp", "tA", "tB", "ksum"):
        setattr(TMP, n, kb.sb(SC, f"{n}T", [128, 512], F32))
    TMPN = set(vars(TMP).keys())
    PB = []
    for b in range(2):
        P = NS()
        P.b = b
        P.k = lambda n, b=b: (f"{n}_pT" if n in TMPN else f"{n}_p{b}")
        for n in TMPN:
            setattr(P, n, getattr(TMP, n))
        for n in ("gt_", "zct", "Bf", "Kf", "yo"):
            setattr(P, n, kb.sb(SC, f"{n}{b}", [128, 512], BF16))
        for n in ("gam", "Rf32", "bonus"):
            setattr(P, n, kb.sb(SC, f"{n}{b}", [128, 512], F32))
        P.AR = kb.sb(SC, f"AR{b}", [128, 4, 2, 128], BF16)
        for n in ("AT", "BT", "KT", "VT"):
            setattr(P, n, kb.sb(SC, f"{n}{b}", [128, 4, 128], BF16))
        PB.append(P)
    SL = []
    for s_ in range(NSLOT):
        L = NS()
        L.s = s_
        L.k = lambda n, s_=s_: f"{n}_s{s_}"
        L.smx = [kb.sb(SC, f"smx{s_}_{h}", [128, 3, 128], BF16) for h in range(2)]
        L.smo = [kb.sb(SC, f"smo{s_}_{h}", [128, 3, 128], BF16) for h in range(2)]
        L.smxT = kb.sb(SC, f"smxT{s_}", [128, 2, 3, 128], BF16)
        L.XX = [kb.sb(SC, f"XX{s_}_{i}", [128, 2, 2, 128], BF16) for i in range(2)]
        L.PPb = kb.sb(SC, f"PPb{s_}", [128, 2, 2, 128], BF16)
        L.Yb = kb.sb(SC, f"Yb{s_}", [128, 2, 2, 128], BF16)
        L.YBb = kb.sb(SC, f"YBb{s_}", [128, 2, 128], BF16)
        L.T128b = kb.sb(SC, f"T128b{s_}", [128, 2, 128], BF16)
        L.mkv = kb.sb(SC, f"mkv{s_}", [128, 128], BF16)
        L.WTc = kb.sb(SC, f"WTc{s_}", [128, 128], BF16)
        L.UlTc = kb.sb(SC, f"UlTc{s_}", [128, 128], BF16)
        L.P0b = kb.sb(SC, f"P0b{s_}", [128, 128], F32)
        L.Gt = kb.sb(SC, f"Gt{s_}", [128, 128], F32)
        L.ofs = kb.sb(SC, f"ofs{s_}", [128, 128], F32)
        L.ofl = kb.sb(SC, f"ofl{s_}", [128, 128], F32)
        L.osum = kb.sb(SC, f"osum{s_}", [128, 128], F32)
        L.onrm = kb.sb(SC, f"onrm{s_}", [128, 128], F32)
        L.st6 = kb.sb(SC, f"st6{s_}", [128, 2, 6], F32)
        L.mv = kb.sb(SC, f"mv{s_}", [128, 2, 2], F32)
        L.rsd = kb.sb(SC, f"rsd{s_}", [128, 2], F32)
        L.yl = kb.sb(SC, f"yl{s_}", [128, 128], F32)
        L.yl2 = kb.sb(SC, f"yl2{s_}", [128, 128], F32)
        SL.append(L)
    ST = [[kb.sb(SC, f"ST{p}_{i}", [128, 128], F32) for i in range(2)] for p in range(2)]
    shared_i = [0]

    def shared_bank():
        return kb.pfx(7)

    def prep_bank():
        return kb.pfx(6)

    coef = kb.sb(SC, "coef", [128, 24], F32)
    coefh = kb.sb(SC, "coefh", [128, 24], BF16)
    coefl = kb.sb(SC, "coefl", [128, 24], F32)
    mu12 = cp[:, CP["mu"]:CP["mu"] + 12]
    op("dve", lambda e: e.tensor_scalar(out=coef[:, 0:12], in0=mu12, scalar1=0.5, scalar2=None, op0=ALU.mult),
       reads=["cp"], writes=["coef"])
    op("dve", lambda e: e.tensor_scalar(out=coef[:, 12:24], in0=mu12, scalar1=-1.0, scalar2=1.0, op0=ALU.mult,
                                        op1=ALU.add), reads=["cp"], writes=["coef"])
    op("dve", lambda e: e.tensor_copy(out=coefh[:], in_=coef[:]), reads=["coef"], writes=["coefh"])
    op("dve", lambda e: e.tensor_copy(out=coefl[:], in_=coefh[:]), reads=["coefh"], writes=["coefl"])
    op("dve", lambda e: e.tensor_tensor(out=coefl[:], in0=coef[:], in1=coefl[:], op=ALU.subtract),
       reads=["coef", "coefl"], writes=["coefl"])
    DGh = kb.sb(SC, "DGh", [128, 24, 128], BF16)
    DGl = kb.sb(SC, "DGl", [128, 24, 128], BF16)
    for i_ in range(24):
        op("dve", lambda e: e.tensor_scalar(out=DGh[:, i_, :], in0=identf[:], scalar1=coef[:, i_:i_ + 1], scalar2=None,
                                            op0=ALU.mult), reads=["identf", "coef"], writes=["DGh"])
        op("dve", lambda e: e.tensor_scalar(out=DGl[:, i_, :], in0=identf[:], scalar1=coefl[:, i_:i_ + 1], scalar2=None,
                                            op0=ALU.mult), reads=["identf", "coefl"], writes=["DGl"])

    def prep_loads(P, hp, d, tau0, n, col0, lat0, bwd):
        k = P.k
        chs = slice(hp * 128, (hp + 1) * 128)
        kb.dma("sp", P.rt[:, :n + 2], S_r[chs, col0 - 1:col0 + n + 1], reads=["Sstreams", "Shalo"], writes=[k("rt")])
        kb.dma("sp", P.kt[:, :n + 2], S_k[chs, col0 - 1:col0 + n + 1], reads=["Sstreams", "Shalo"], writes=[k("kt")])
        kb.dma("sp", P.vt[:, :n + 2], S_v[chs, col0 - 1:col0 + n + 1], reads=["Sstreams", "Shalo"], writes=[k("vt")])
        kb.dma("sp", P.swt[:, :n], S_sw[d][chs, tau0:tau0 + n], reads=["Slora"], writes=[k("swt")])
        kb.dma("sp", P.atd[:, :n], S_a[d][chs, tau0:tau0 + n], reads=["Slora"], writes=[k("atd")])
        if bwd and lat0 is not None:
            kb.dma("sp", P.at0[:, :n], S_a[0][chs, tau0:tau0 + n], reads=["Slora"], writes=[k("at0")])

    def prep_gen(P, hp, d, tau0, n, col0, lat0, bwd):
        k = P.k
        chs = slice(hp * 128, (hp + 1) * 128)
        nch = n // 128
        fin = bwd and lat0 is not None
        if fin:
            kb.dma("sp", P.gt_[:, :n], S_g[chs, tau0:tau0 + n], reads=["Slora"], writes=[k("gt_")])
            kb.dma("sp", P.zct[:, :n], S_zc[chs, lat0:lat0 + n], reads=["Sstreams"], writes=[k("zct")])
        yield
        tA, tB = P.tA, P.tB
        for (xt_, xk, mi) in ((P.rt, "rt", 0), (P.kt, "kt", 1), (P.vt, "vt", 2)):
            ps, pk = prep_bank()
            ih, io = mi * 4 + hp, 12 + mi * 4 + hp
            seq = [(DGh, ih, 0), (DGh, io, 1), (DGh, ih, 2)]
            for qi, (dg, ii, sh) in enumerate(seq):
                mm(ps[:, :n], dg[:, ii, :], xt_[:, sh:sh + n], qi == 0, qi == len(seq) - 1, ["DGh", "DGl", k(xk)], [pk])
            if mi == 0:
                op("act", lambda e: e.activation(out=P.rp[:, :n], in_=ps[:, :n], func=AF.Copy), reads=[pk], writes=[k("rp")])
            elif mi == 1:
                op("act", lambda e: e.activation(out=P.kp[:, :n], in_=ps[:, :n], func=AF.Copy), reads=[pk], writes=[k("kp")])
                op("act", lambda e: e.activation(out=P.kkr[:, :n], in_=ps[:, :n], func=AF.Copy, scale=cpc("k_k", hp)),
                   reads=[pk, "cp"], writes=[k("kkr")])
                op("act", lambda e: e.activation(out=sqb[:, :n], in_=ps[:, :n], func=AF.Square, scale=cpc("k_k", hp)),
                   reads=[pk, "cp"], writes=["sqb"])
            else:
                op("act", lambda e: e.activation(out=P.vp[:, :n], in_=ps[:, :n], func=AF.Copy), reads=[pk], writes=[k("vp")])
                op("act", lambda e: e.activation(out=P.vb16[:, :n], in_=ps[:, :n], func=AF.Copy), reads=[pk],
                   writes=[k("vb16")])
            yield
        ps, pk = prep_bank()
        mm(ps[:, :n], blkb[:], sqb[:, :n], True, True, ["blkb", "sqb"], [pk])
        op("act", lambda e: e.activation(out=tA[:, :n], in_=ps[:, :n], func=AF.Sqrt), reads=[pk], writes=[k("tA")])
        yield
        op("dve", lambda e: e.tensor_scalar(out=tA[:, :n], in0=tA[:, :n], scalar1=1e-12, scalar2=None, op0=ALU.max),
           reads=[k("tA")], writes=[k("tA")])
        op("dve", lambda e: e.reciprocal(out=tA[:, :n], in_=tA[:, :n]), reads=[k("tA")], writes=[k("tA")])
        op("dve", lambda e: e.tensor_tensor(out=P.kkn[:, :n], in0=P.kkr[:, :n], in1=tA[:, :n], op=ALU.mult),
           reads=[k("kkr"), k("tA")], writes=[k("kkn")])
        yield
        op("act", lambda e: e.activation(out=tB[:, :n], in_=P.atd[:, :n], func=AF.Identity, scale=cpc("k_a", hp),
                                         bias=omka[:, hp:hp + 1]), reads=[k("atd"), "cp", "omka"], writes=[k("tB")])
        op("dve", lambda e: e.tensor_tensor(out=P.kd[:, :n], in0=tB[:, :n], in1=P.kp[:, :n], op=ALU.mult),
           reads=[k("tB"), k("kp")], writes=[k("kd")])
        op("dve", lambda e: e.tensor_tensor(out=P.bb[:, :n], in0=P.kkn[:, :n], in1=P.atd[:, :n], op=ALU.mult),
           reads=[k("kkn"), k("atd")], writes=[k("bb")])
        op("act", lambda e: e.activation(out=P.lw[:, :n], in_=P.swt[:, :n], func=AF.Copy, scale=-EXPM05),
           reads=[k("swt")], writes=[k("lw")])
        yield
        for j in range(nch):
            js = slice(j * 128, (j + 1) * 128)
            op("dve", lambda e: e.tensor_tensor_scan(out=P.cs[:, js], data0=ones128[:], data1=P.lw[:, js], initial=0.0,
                                                     op0=ALU.mult, op1=ALU.add), reads=["ones128", k("lw")],
               writes=[k("cs")])
        cse, csk = P.cs, k("cs")
        if bwd:
            op("dve", lambda e: e.tensor_tensor(out=tB[:, :n], in0=P.lw[:, :n], in1=P.cs[:, :n], op=ALU.subtract),
               reads=[k("lw"), k("cs")], writes=[k("tB")])
            for j in range(nch):
                js = slice(j * 128, (j + 1) * 128)
                op("dve", lambda e: e.tensor_scalar(out=P.csr[:, js], in0=tB[:, js],
                                                    scalar1=P.cs[:, j * 128 + 127:j * 128 + 128], scalar2=None,
                                                    op0=ALU.add), reads=[k("tB"), k("cs")], writes=[k("csr")])
            cse, csk = P.csr, k("csr")
        yield
        op("act", lambda e: e.activation(out=P.gam[:, :n], in_=cse[:, :n], func=AF.Exp), reads=[csk], writes=[k("gam")])
        op("act", lambda e: e.activation(out=P.ig[:, :n], in_=cse[:, :n], func=AF.Exp, scale=-1.0), reads=[csk],
           writes=[k("ig")])
        g3 = lambda t_: t_[:, :n].rearrange("p (j t) -> p j t", t=128)
        if not bwd:
            op("act", lambda e: e.activation(out=g3(P.gp)[:, :, 1:128], in_=g3(P.gam)[:, :, 0:127], func=AF.Copy),
               reads=[k("gam")], writes=[k("gp")])
            op("act", lambda e: e.activation(out=g3(P.gp)[:, :, 0:1], in_=g3(ones128)[:, :nch, 0:1] if False else
                                             ones128[:, 0:nch].unsqueeze(2), func=AF.Copy),
               reads=["ones128"], writes=[k("gp")])
        else:
            op("act", lambda e: e.activation(out=g3(P.gp)[:, :, 0:127], in_=g3(P.gam)[:, :, 1:128], func=AF.Copy),
               reads=[k("gam")], writes=[k("gp")])
            op("act", lambda e: e.activation(out=g3(P.gp)[:, :, 127:128], in_=ones128[:, 0:nch].unsqueeze(2),
                                             func=AF.Copy), reads=["ones128"], writes=[k("gp")])
        yield
        v3 = lambda t_: t_[:, :n].rearrange("p (j t) -> p j t", t=128)
        op("dve", lambda e: e.scalar_tensor_tensor(out=P.Af[:, :n], in0=P.kkn[:, :n], scalar=-1.0, in1=P.gp[:, :n],
                                                   op0=ALU.mult, op1=ALU.mult), reads=[k("kkn"), k("gp")], writes=[k("Af")])
        op("act", lambda e: e.activation(out=P.AR[:, :nch, 0, :], in_=v3(P.Af), func=AF.Copy), reads=[k("Af")],
           writes=[k("AR")])
        op("dve", lambda e: e.tensor_tensor(out=P.Rf32[:, :n], in0=P.rp[:, :n], in1=P.gam[:, :n], op=ALU.mult),
           reads=[k("rp"), k("gam")], writes=[k("Rf32")])
        op("act", lambda e: e.activation(out=P.AR[:, :nch, 1, :], in_=v3(P.Rf32), func=AF.Copy), reads=[k("Rf32")],
           writes=[k("AR")])
        yield
        op("dve", lambda e: e.tensor_tensor(out=P.Bf[:, :n], in0=P.bb[:, :n], in1=P.ig[:, :n], op=ALU.mult),
           reads=[k("bb"), k("ig")], writes=[k("Bf")])
        op("dve", lambda e: e.tensor_tensor(out=P.Kf[:, :n], in0=P.kd[:, :n], in1=P.ig[:, :n], op=ALU.mult),
           reads=[k("kd"), k("ig")], writes=[k("Kf")])
        yield
        for (src, sk, dstT, dk) in ((P.Af, "Af", P.AT, "AT"), (P.Bf, "Bf", P.BT, "BT"), (P.Kf, "Kf", P.KT, "KT"),
                                    (P.vb16, "vb16", P.VT, "VT")):
            pf_, pk = prep_bank()
            pt = pf_[:].bitcast(BF16)
            for j in range(nch):
                op("pe", lambda e: e.transpose(out=pt[:, j * 128:(j + 1) * 128], in_=src[:, j * 128:(j + 1) * 128],
                                               identity=identb[:]), reads=[k(sk), "identb"], writes=[pk])
            op("act", lambda e: e.activation(out=dstT[:, :nch, :].rearrange("p j t -> p (j t)"), in_=pt[:, :n],
                                             func=AF.Copy), reads=[pk], writes=[k(dk)])
            yield
        if fin:
            op("dve", lambda e: e.tensor_tensor(out=tB[:, :n], in0=P.at0[:, :n], in1=P.atd[:, :n], op=ALU.add),
               reads=[k("at0"), k("atd")], writes=[k("tB")])
            op("dve", lambda e: e.tensor_scalar(out=tB[:, :n], in0=tB[:, :n], scalar1=-2.0, scalar2=cpc("k_a", hp),
                                                op0=ALU.add, op1=ALU.mult), reads=[k("tB"), "cp"], writes=[k("tB")])
            op("dve", lambda e: e.scalar_tensor_tensor(out=P.ksum[:, :n], in0=tB[:, :n], scalar=2.0, in1=P.kp[:, :n],
                                                       op0=ALU.add, op1=ALU.mult), reads=[k("tB"), k("kp")],
               writes=[k("ksum")])
            op("dve", lambda e: e.scalar_tensor_tensor(out=sqb[:, :n], in0=P.rp[:, :n], scalar=cpc("r_k", hp),
                                                       in1=P.ksum[:, :n], op0=ALU.mult, op1=ALU.mult),
               reads=[k("rp"), "cp", k("ksum")], writes=["sqb"])
            ps, pk = prep_bank()
            mm(ps[:, :n], blkb[:], sqb[:, :n], True, True, ["blkb", "sqb"], [pk])
            op("dve", lambda e: e.tensor_tensor(out=P.bonus[:, :n], in0=ps[:, :n], in1=P.vp[:, :n], op=ALU.mult),
               reads=[pk, k("vp")], writes=[k("bonus")])
            yield

    chain_turn = [0]

    def chunk_gen(L, P, inst, hp, d, j, lat_row, bwd, fin, want_out, stp, first_of_pass, sc_state):
        k = L.k
        pk_ = P.k
        js = slice(j * 128, (j + 1) * 128)
        bank, bk = kb.pfx(L.s)
        if fin:
            kb.dma("sp", L.ofl[:], S_of[lat_row:lat_row + 128, hp * 128:(hp + 1) * 128], reads=["Sof"], writes=[k("ofl")])
        smx, smo, smxT, PPb = L.smx, L.smo, L.smxT, L.PPb
        flat = lambda t_: t_[:].rearrange("p h a t -> p (h a t)")
        for h in range(2):
            hs = slice(64 * h, 64 * h + 64)
            arh = P.AR[hs, j, :, :].rearrange("p a t -> p (a t)")
            mm(bank[:, 0:256], P.Bf[hs, js], arh, True, False, [pk_("Bf"), pk_("AR")], [bk])
            mm(bank[:, 256:512], P.Kf[hs, js], arh, False, True, [pk_("Kf"), pk_("AR")], [bk])
            op("dve", lambda e: e.tensor_tensor(out=smx[h][:], in0=bank[:, 0:128].unsqueeze(1).broadcast_to([128, 3, 128]),
                                                in1=masks[:, d, 0:3, :], op=ALU.mult), reads=[bk, "masks"],
               writes=[k(f"smx{h}")])
            op("dve", lambda e: e.tensor_tensor(out=smo[h][:], in0=bank[:, 128:512].rearrange("p (m t) -> p m t", m=3),
                                                in1=masks[:, d, 3:6, :], op=ALU.mult), reads=[bk, "masks"],
               writes=[k(f"smo{h}")])
            yield
        pt = bank[:].bitcast(BF16)
        for h in range(2):
            for m in range(3):
                c0 = (h * 3 + m) * 128
                op("pe", lambda e: e.transpose(out=pt[:, c0:c0 + 128], in_=smx[h][:, m, :], identity=identb[:]),
                   reads=[k(f"smx{h}"), "identb"], writes=[bk])
        op("act", lambda e: e.activation(out=smxT[:].rearrange("p h m t -> p (h m t)"), in_=pt[:, 0:768], func=AF.Copy),
           reads=[bk], writes=[k("smxT")])
        yield
        Xc = [smx[0][:, 0, :], smx[1][:, 0, :]]
        XTc = [smxT[:, 0, 0, :], smxT[:, 1, 0, :]]
        xkeys = [k("smx0"), k("smx1"), k("smxT")]

        evi = [L.s]

        def evac_copy(dst_ap, src_ap, dkey):
            if L.s >= 2:
                op("act", lambda e: e.activation(out=dst_ap, in_=src_ap, func=AF.Copy), reads=[bk], writes=[dkey])
            else:
                op("dve", lambda e: e.tensor_copy(out=dst_ap, in_=src_ap), reads=[bk], writes=[dkey])

        def pp_seed(h, first):
            mm(bank[:, h * 256:(h + 1) * 256], identb[:], PPb[:, h, :, :].rearrange("p a t -> p (a t)"), first, False,
               ["identb", k("PPb")], [bk])

        def pp_update():
            evac_copy(flat(PPb), bank[:], k("PPb"))

        def pp_seed_all():
            mm(bank[:], identb[:], flat(PPb), True, False, ["identb", k("PPb")], [bk])

        for i in range(1, 5):
            last = i == 4
            xx = L.XX[i % 2]
            xk = k(f"XX{i % 2}")
            if not last:
                for h in range(2):
                    mm(bank[:, h * 256:h * 256 + 128], XTc[h], Xc[h], h == 0, False, xkeys, [bk])
                    mm(bank[:, h * 256 + 128:(h + 1) * 256], Xc[h], XTc[h], False, h == 1, xkeys, [bk])
                evac_copy(flat(xx), bank[:], xk)
            else:
                for h in range(2):
                    mm(bank[:, h * 256:h * 256 + 128], XTc[h], Xc[h], h == 0, h == 1, xkeys, [bk])
                evac_copy(xx[:, :, 0, :], bank[:].rearrange("p (h a t) -> p h a t", h=2, a=2)[:, :, 0, :], xk)
            Xc = [xx[:, 0, 0, :], xx[:, 1, 0, :]]
            XTc = [xx[:, 0, 1, :], xx[:, 1, 1, :]]
            xkeys = [xk]
            yield
            if i == 1:
                for h in range(2):
                    lo, hi = slice(h * 256, h * 256 + 128), slice(h * 256 + 128, (h + 1) * 256)
                    m0, m0t = smx[h][:, 0, :], smxT[:, h, 0, :]
                    kk_ = [k(f"smx{h}"), k("smxT"), xk, "identb"]
                    mm(bank[:, lo], identb[:], identb[:], h == 0, False, kk_, [bk])
                    mm(bank[:, lo], identb[:], m0, False, False, kk_, [bk])
                    mm(bank[:, lo], identb[:], Xc[h], False, False, kk_, [bk])
                    mm(bank[:, lo], m0t, Xc[h], False, False, kk_, [bk])
                    mm(bank[:, hi], identb[:], identb[:], False, False, kk_, [bk])
                    mm(bank[:, hi], identb[:], m0t, False, False, kk_, [bk])
                    mm(bank[:, hi], Xc[h], identb[:], False, False, kk_, [bk])
                    mm(bank[:, hi], Xc[h], m0t, False, h == 1, kk_, [bk])
            else:
                pp_seed_all()
                for h in range(2):
                    mm(bank[:, h * 256:h * 256 + 128], PPb[:, h, 1, :], Xc[h], False, False, [k("PPb"), xk], [bk])
                    mm(bank[:, h * 256 + 128:(h + 1) * 256], Xc[h], PPb[:, h, 1, :], False, h == 1, [k("PPb"), xk], [bk])
            pp_update()
            yield
        for h in range(2):
            mm(bank[:, h * 256:h * 256 + 128], smxT[:, h, 1, :], PPb[:, h, 0, :], h == 0, False, [k("smxT"), k("PPb")], [bk])
            mm(bank[:, h * 256 + 128:(h + 1) * 256], smx[h][:, 1, :], PPb[:, h, 1, :], False, h == 1,
               [k(f"smx{h}"), k("PPb")], [bk])
        evac_copy(flat(L.Yb), bank[:], k("Yb"))
        yield
        pp_seed_all()
        for h in range(2):
            mm(bank[:, h * 256:h * 256 + 128], PPb[:, h, 1, :], L.Yb[:, h, 0, :], False, False, [k("PPb"), k("Yb")], [bk])
            mm(bank[:, h * 256 + 128:(h + 1) * 256], PPb[:, h, 0, :], L.Yb[:, h, 1, :], False, h == 1,
               [k("PPb"), k("Yb")], [bk])
        pp_update()
        yield
        for h in range(2):
            mm(bank[:, h * 128:(h + 1) * 128], smxT[:, h, 2, :], PPb[:, h, 0, :], h == 0, False, [k("smxT"), k("PPb")], [bk])
        for h in range(2):
            hs = slice(64 * h, 64 * h + 64)
            mm(bank[:, 256 + 64 * h:256 + 64 * h + 64], smo[h][:, 1, :], P.VT[:, j, hs], False, h == 1,
               [k(f"smo{h}"), pk_("VT")], [bk])
        evac_copy(L.YBb[:].rearrange("p h t -> p (h t)"), bank[:, 0:256], k("YBb"))
        evac_copy(L.mkv[:], bank[:, 256:384], k("mkv"))
        yield
        for h in range(2):
            mm(bank[:, h * 128:(h + 1) * 128], identb[:], PPb[:, h, 0, :], h == 0, False, ["identb", k("PPb")], [bk])
            mm(bank[:, h * 128:(h + 1) * 128], PPb[:, h, 1, :], L.YBb[:, h, :], False, h == 1, [k("PPb"), k("YBb")], [bk])
        evac_copy(L.T128b[:].rearrange("p h t -> p (h t)"), bank[:, 0:256], k("T128b"))
        yield
        for h in range(2):
            hs = slice(64 * h, 64 * h + 64)
            mm(bank[:, hs], L.T128b[:, h, :], P.AT[:, j, hs], h == 0, False, [k("T128b"), pk_("AT")], [bk])
            mm(bank[:, 128 + 64 * h:128 + 64 * h + 64], L.T128b[:, h, :], L.mkv[:, hs], False, h == 1,
               [k("T128b"), k("mkv")], [bk])
        op("act", lambda e: e.activation(out=L.WTc[:], in_=bank[:, 0:128], func=AF.Copy), reads=[bk], writes=[k("WTc")])
        op("act", lambda e: e.activation(out=L.UlTc[:], in_=bank[:, 128:256], func=AF.Copy), reads=[bk],
           writes=[k("UlTc")])
        yield
        mm(bank[:, 0:128], L.WTc[:], P.BT[:, j, :], True, not want_out, [k("WTc"), pk_("BT")], [bk])
        if want_out:
            for h in range(2):
                mm(bank[64 * h:64 * h + 64, 128:256], L.WTc[:, 64 * h:64 * h + 64], smo[h][:, 0, :], False, True,
                   [k("WTc"), k(f"smo{h}")], [bk])
        op("dve", lambda e: e.tensor_tensor(out=L.P0b[:], in0=bank[:, 0:128], in1=identf[:], op=ALU.add),
           reads=[bk, "identf"], writes=[k("P0b")])
        if want_out:
            op("dve", lambda e: e.tensor_tensor(out=L.Gt[:], in0=bank[:, 128:256], in1=P.Rf32[:, js], op=ALU.add),
               reads=[bk, pk_("Rf32")], writes=[k("Gt")])
        yield
        while chain_turn[0] != inst:
            yield
        if first_of_pass:
            op("dve", lambda e: e.memset(ST[stp][0][:], 0.0), writes=[f"ST{stp}_0"])
            sc_state[0] = 0
        cur = sc_state[0]
        stc, stck = ST[stp][cur], f"ST{stp}_{cur}"
        stn, stnk = ST[stp][1 - cur], f"ST{stp}_{1 - cur}"
        if want_out:
            po, pok = shared_bank()
            mm(po[:, 0:128], L.Gt[:], stc[:], True, False, [k("Gt"), stck], [pok])
            for h in range(2):
                hs = slice(64 * h, 64 * h + 64)
                mm(po[:, hs], smo[h][:, 0, :], L.UlTc[:, hs], False, False, [k(f"smo{h}"), k("UlTc")], [pok])
                mm(po[:, hs], smo[h][:, 2, :], P.VT[:, j, hs], False, h == 1, [k(f"smo{h}"), pk_("VT")], [pok])
        mm(bank[:, 0:128], P.BT[:, j, :], L.UlTc[:], True, False, [pk_("BT"), k("UlTc")], [bk])
        mm(bank[:, 0:128], P.KT[:, j, :], P.VT[:, j, :], False, False, [pk_("KT"), pk_("VT")], [bk])
        mm(bank[:, 0:128], L.P0b[:], stc[:], False, True, [k("P0b"), stck], [bk])
        gcol = j * 128 if bwd else j * 128 + 127
        op("dve", lambda e: e.scalar_tensor_tensor(out=stn[:], in0=bank[:, 0:128], scalar=P.gam[:, gcol:gcol + 1],
                                                   in1=blk[:], op0=ALU.mult, op1=ALU.mult),
           reads=[bk, pk_("gam"), "blk"], writes=[stnk])
        sc_state[0] = 1 - cur
        chain_turn[0] += 1
        if not want_out:
            return
        if not fin:
            op("act", lambda e: e.activation(out=L.ofs[:], in_=po[:, 0:128], func=AF.Copy), reads=[pok],
               writes=[k("ofs")])
            kb.dma("sp", S_of[lat_row:lat_row + 128, hp * 128:(hp + 1) * 128], L.ofs[:], reads=[k("ofs")],
                   writes=["Sof"], key="ofst")
            return
        op("dve", lambda e: e.tensor_tensor(out=L.osum[:], in0=po[:, 0:128], in1=L.ofl[:], op=ALU.add),
           reads=[pok, k("ofl")], writes=[k("osum")])
        yield
        for h in range(2):
            hs = slice(64 * h, 64 * h + 64)
            op("dve", lambda e: e.bn_stats(out=L.st6[:, h, :], in_=L.osum[:, hs]), reads=[k("osum")], writes=[k("st6")])
            op("dve", lambda e: e.bn_aggr(out=L.mv[:, h, :], in_=L.st6[:, h, :]), reads=[k("st6")], writes=[k("mv")])
        op("dve", lambda e: e.tensor_scalar(out=L.rsd[:], in0=L.mv[:, :, 1], scalar1=64e-5, scalar2=None, op0=ALU.add),
           reads=[k("mv")], writes=[k("rsd")])
        op("act", lambda e: e.activation(out=L.rsd[:], in_=L.rsd[:], func=AF.Sqrt), reads=[k("rsd")], writes=[k("rsd")])
        yield
        op("dve", lambda e: e.reciprocal(out=L.rsd[:], in_=L.rsd[:]), reads=[k("rsd")], writes=[k("rsd")])
        for h in range(2):
            hs = slice(64 * h, 64 * h + 64)
            op("dve", lambda e: e.tensor_scalar(out=L.onrm[:, hs], in0=L.osum[:, hs], scalar1=L.mv[:, h, 0:1],
                                                scalar2=L.rsd[:, h:h + 1], op0=ALU.subtract, op1=ALU.mult),
               reads=[k("osum"), k("mv"), k("rsd")], writes=[k("onrm")])
        yield
        op("pe", lambda e: e.transpose(out=bank[:, 0:128], in_=L.onrm[:], identity=identf[:]),
           reads=[k("onrm"), "identf"], writes=[bk])
        op("dve", lambda e: e.tensor_scalar(out=L.yl[:], in0=bank[:, 0:128], scalar1=cpc("lnx_g", hp),
                                            scalar2=cpc("lnx_b", hp), op0=ALU.mult, op1=ALU.add),
           reads=[bk, "cp"], writes=[k("yl")])
        yield
        op("dve", lambda e: e.tensor_tensor(out=L.yl2[:], in0=L.yl[:], in1=P.bonus[:, js], op=ALU.add),
           reads=[k("yl"), pk_("bonus")], writes=[k("yl2")])
        op("dve", lambda e: e.tensor_tensor(out=L.yl[:], in0=L.yl2[:], in1=P.gt_[:, js], op=ALU.mult),
           reads=[k("yl2"), pk_("gt_")], writes=[k("yl")])
        op("dve", lambda e: e.tensor_tensor(out=P.yo[:, js], in0=L.yl[:], in1=P.zct[:, js], op=ALU.mult),
           reads=[k("yl"), pk_("zct")], writes=[pk_("yo")])

    sups = []
    npass = 0
    for hp in range(4):
        for d in range(2):
            bwd = d == 1
            lst = [(0, CTX, CTX0, None)] + [(CTX + it * 512, 512, LAT0 + it * 512, it * 512) for it in range(NT)]
            if bwd:
                lst = [lst[0]] + lst[1:][::-1]
            for qi, (tau0, n, col0, lat0) in enumerate(lst):
                sups.append((hp, d, tau0, n, col0, lat0, bwd, qi == 0, npass % 2))
            npass += 1
    lim = [int(x[8:]) for x in dump if x.startswith("SCANLIM_")]
    if lim:
        sups = sups[:lim[0]]
    nsup = len(sups)
    pass_state = {}
    prep_done = [False] * nsup
    prep_started = [False] * nsup
    chunks_left = [0] * nsup
    inst_list = []
    for q, (hp, d, tau0, n, col0, lat0, bwd, first, stp) in enumerate(sups):
        nch = n // 128
        order = list(range(nch))[::-1] if bwd else list(range(nch))
        chunks_left[q] = nch
        for oi, j in enumerate(order):
            inst_list.append((q, j, first and oi == 0))
    loads_issued = [False] * nsup

    def issue_loads(q):
        if q < nsup and not loads_issued[q]:
            (hp, d, tau0, n, col0, lat0, bwd, first, stp) = sups[q]
            prep_loads(PB[q % 2], hp, d, tau0, n, col0, lat0, bwd)
            loads_issued[q] = True

    active = []
    free_slots = list(range(NSLOT))
    next_inst = 0
    sc_states = {}
    while next_inst < len(inst_list) or active:
        for q in range(nsup):
            if prep_started[q]:
                continue
            if q >= 2 and chunks_left[q - 2] > 0:
                break
            if q >= 1 and not prep_done[q - 1]:
                break
            (hp, d, tau0, n, col0, lat0, bwd, first, stp) = sups[q]
            issue_loads(q)
            active.append([prep_gen(PB[q % 2], hp, d, tau0, n, col0, lat0, bwd), "prep", q, None])
            prep_started[q] = True
            break
        if next_inst < len(inst_list) and free_slots:
            q, j, first = inst_list[next_inst]
            if prep_done[q]:
                (hp, d, tau0, n, col0, lat0, bwd, firstq, stp) = sups[q]
                slot = free_slots.pop(0)
                fin = bwd and lat0 is not None
                lat_row = None if lat0 is None else lat0 + j * 128
                key = (hp, d)
                if key not in sc_states:
                    sc_states[key] = [0]
                g = chunk_gen(SL[slot], PB[q % 2], next_inst, hp, d, j, lat_row, bwd, fin, lat0 is not None, stp, first,
                              sc_states[key])
                active.append([g, "chunk", q, slot])
                next_inst += 1
        still = []
        for item in active:
            g, kind, q, slot = item
            try:
                next(g)
                still.append(item)
            except StopIteration:
                if kind == "prep":
                    prep_done[q] = True
                    issue_loads(q + 1)
                else:
                    free_slots.append(slot)
                    chunks_left[q] -= 1
                    if chunks_left[q] == 0:
                        (hp, d, tau0, n, col0, lat0, bwd, firstq, stp) = sups[q]
                        if bwd and lat0 is not None:
                            P = PB[q % 2]
                            kb.dma("sp", S_y[hp * 128:(hp + 1) * 128, lat0:lat0 + 512], P.yo[:], reads=[P.k("yo")],
                                   writes=["Sy"], key="yost")
        active = still


def prep_inputs(inp, SEQ):
    f = lambda a: np.ascontiguousarray(np.asarray(a, np.float32))
    shared = {
        "ada_w_e": f(inp["ada_w_e"][0]), "ada_w_o": f(inp["ada_w_o"][0]),
        "ada_b_e": f(inp["ada_b_e"][0][None]), "ada_b_o": f(inp["ada_b_o"][0][None]),
        "norm_e": f(inp["norm_e"][0][None]), "norm_o": f(inp["norm_o"][0][None]),
        "in_e": f(inp["in_e"][0]), "out_e": f(inp["out_e"][0]), "in_o": f(inp["in_o"][0]), "out_o": f(inp["out_o"][0]),
        "pool_w": f(np.asarray(inp["pool_w"][0]).transpose(1, 0, 2).reshape(128, 512)),
        "final_g": f(np.asarray(inp["final_g"])[None]),
    }
    cols = [_cols(inp["pool_scale"][0][None]), _cols(inp["sconv_w"][0]), _cols(inp["rwkv_mu"][0]),
            _cols(inp["k_k"][0][None]), _cols(inp["k_a"][0][None]), _cols(np.asarray(inp["r_k"][0]).reshape(1, 512)),
            _cols(inp["lnx_g"][0][None]), _cols(inp["lnx_b"][0][None]), _cols(inp["conf_dw_b"][0][None]),
            _cols(inp["conf_ln_g"][0][None]), _cols(inp["conf_ln_b"][0][None]), _cols(inp["w0"][0]),
            _cols(inp["a0"][0]), _cols(inp["conf_dw_w"][0])]
    shared["cp"] = np.ascontiguousarray(np.concatenate(cols, axis=1))
    assert shared["cp"].shape == (128, NCP)

    def l1(w):
        w = np.asarray(w, np.float32).transpose(1, 0, 2).reshape(4, 128, -1).transpose(1, 0, 2)
        return np.ascontiguousarray(w.reshape(128, -1))

    shared["w1l"] = l1(inp["w1"][0])
    shared["a1l"] = l1(inp["a1"][0])
    shared["g1l"] = np.ascontiguousarray(
        np.asarray(inp["g1"][0], np.float32).reshape(4, 128, 96).transpose(1, 0, 2).reshape(128, 384))
    shared["w2l"] = f(np.asarray(inp["w2"][0]).reshape(64, 512))
    shared["a2l"] = f(np.asarray(inp["a2"][0]).reshape(64, 512))
    shared["g2l"] = f(inp["g2"][0])
    shared.update(host_consts())
    maps = []
    for b in range(2):
        m = dict(shared)
        m["x"] = f(inp["x"][b][:SEQ])
        m["ctx"] = f(inp["ctx"][b])
        m["ccol"] = np.ascontiguousarray(np.concatenate(
            [np.asarray(inp["c"][b], np.float32).reshape(8, 128).T, np.asarray(inp["c_ctx"], np.float32).reshape(8, 128).T],
            axis=1))
        maps.append(m)
    return maps


_NC_CACHE = {}


def kernel(**inputs):
    SEQ = 8192
    if SEQ not in _NC_CACHE:
        _NC_CACHE[SEQ] = build(SEQ)
    nc = _NC_CACHE[SEQ]
    maps = prep_inputs(inputs, SEQ)
    in_maps = [maps[c // 4] for c in range(8)]
    res = run_bass_kernel_spmd(nc, in_maps, core_ids=list(range(8)))
    return np.stack([res.results[0]["out"], res.results[4]["out"]], axis=0).astype(np.float32)
```

```python
import numpy as np
from contextlib import ExitStack
import concourse.bass as bass
import concourse.mybir as mybir
from concourse.bass_utils import run_bass_kernel_spmd

F32 = mybir.dt.float32
BF16 = mybir.dt.bfloat16
ALU = mybir.AluOpType
AF = mybir.ActivationFunctionType

D = 1024
DB = 512
CTX = 256
CH = 128
CTX0 = 2
LAT0 = CTX0 + CTX + 2
GPAD = 960
EXPM05 = float(np.exp(-0.5))


class KB:
    def __init__(self, nc, es):
        self.nc = nc
        self.es = es
        self.eng = {"pe": nc.tensor, "act": nc.scalar, "dve": nc.vector, "pool": nc.gpsimd, "sp": nc.sync}
        self.sem = {}
        self.cnt = {}
        for e in self.eng:
            self.sem[e] = es.enter_context(nc.semaphore("sem_" + e))
            self.cnt[e] = 0
        self.dsem = {}
        self.dcnt = {}
        self.waited = {e: {} for e in self.eng}
        self.lastw = {}
        self.reads = {}
        self.psf = [es.enter_context(nc.psum_tensor(f"psf{i}", [128, 512], F32)) for i in range(8)]
        self.psf_i = 0
        self.psb_i = 0
        self.ndsem = 0

    def sb(self, es, name, shape, dt):
        self.nsb = getattr(self, "nsb", 0) + 1
        return es.enter_context(self.nc.sbuf_tensor("sb%d_%s" % (self.nsb, name), shape, dt))

    def pf(self):
        i = self.psf_i % 6
        self.psf_i += 1
        return self.psf[i], f"psf{i}"

    def pfx(self, i):
        return self.psf[i], f"psf{i}"

    def pb(self):
        i = 6 + self.psb_i % 2
        self.psb_i += 1
        return self.psf[i][:].bitcast(BF16), f"psf{i}"

    def _wait(self, e, ev):
        if ev is None:
            return
        semkey, val = ev
        if self.waited[e].get(semkey, 0) >= val:
            return
        sem = self.sem[semkey] if semkey in self.sem else self.dsem[semkey]
        self.eng[e].wait_ge(sem, val)
        self.waited[e][semkey] = val

    def _deps(self, e, reads, writes, sync_same):
        for k in reads:
            for sk, v in self.lastw.get(k, {}).items():
                if sync_same or sk != e:
                    self._wait(e, (sk, v))
            if k.startswith("ps"):
                for sk, v in self.reads.get(k, {}).items():
                    if sk != e:
                        self._wait(e, (sk, v))
        for k in writes:
            for sk, v in self.lastw.get(k, {}).items():
                if sync_same or sk != e:
                    self._wait(e, (sk, v))
            for sk, v in self.reads.get(k, {}).items():
                if sync_same or sk != e:
                    self._wait(e, (sk, v))

    def _record(self, ev, reads, writes):
        sk, v = ev
        for k in writes:
            self.lastw.setdefault(k, {})[sk] = v
        for k in reads:
            self.reads.setdefault(k, {})[sk] = v

    budget = None
    nops = 0

    def op(self, e, fn, reads=(), writes=()):
        self.nops += 1
        if self.budget is not None and self.nops > self.budget:
            return
        self._deps(e, reads, writes, e != "pe")
        inst = fn(self.eng[e])
        self.cnt[e] += 1
        inst.then_inc(self.sem[e], 1)
        self._record((e, self.cnt[e]), reads, writes)

    def dma(self, q, out, in_, reads=(), writes=(), key=None):
        self.nops += 1
        if self.budget is not None and self.nops > self.budget and not str(key).startswith("dbg"):
            return
        key = key or (list(writes) + list(reads))[0]
        skey = "d_" + str(key)
        if skey not in self.dsem:
            self.dsem[skey] = self.es.enter_context(self.nc.semaphore("ds%d" % self.ndsem))
            self.ndsem += 1
            self.dcnt[skey] = 0
        self._deps(q, reads, writes, True)
        self.dcnt[skey] += 16
        self.eng[q].dma_start(out=out, in_=in_).then_inc(self.dsem[skey], 16)
        self._record((skey, self.dcnt[skey]), reads, writes)

    def barrier(self):
        evs = [(e, self.cnt[e]) for e in self.eng if self.cnt[e] > 0] + [(k, v) for k, v in self.dcnt.items() if v > 0]
        for e in self.eng:
            for ev in evs:
                if ev[0] != e:
                    self._wait(e, ev)

    def finish(self, keys):
        for k in keys:
            for sk, v in self.lastw.get(k, {}).items():
                self._wait("sp", (sk, v))
            for sk, v in self.reads.get(k, {}).items():
                self._wait("sp", (sk, v))


CP = {}
_off = 0
for _n, _w in [("pool_scale", 4), ("sconv_w", 12), ("mu", 24), ("k_k", 4), ("k_a", 4), ("r_k", 4),
               ("lnx_g", 4), ("lnx_b", 4), ("conf_dw_b", 4), ("conf_ln_g", 4), ("conf_ln_b", 4),
               ("w0", 8), ("a0", 8), ("conf_dw_w", 124)]:
    CP[_n] = _off
    _off += _w
NCP = _off


def _cols(p):
    p = np.asarray(p, np.float32).reshape(-1, 4, 128)
    return np.ascontiguousarray(p.transpose(2, 0, 1).reshape(128, -1))


def host_consts():
    s = np.arange(128)[:, None]
    t = np.arange(128)[None, :]
    b32 = (s // 32) == (t // 32)
    b64 = (s // 64) == (t // 64)
    ml = []
    for st, inc in (((s < t), (s <= t)), ((s > t), (s >= t))):
        ml += [st & b32, st & b64 & ~b32, st & ~b64, inc, st, inc]
    masks = np.concatenate(ml, axis=1).astype(np.float32)
    blk = ((s // 64) == (t // 64)).astype(np.float32)

    def invc(l):
        tt = np.arange(l)
        rows = []
        for w in (2, 4, 8, 16):
            lo = np.clip(tt - w // 2, 0, l)
            hi = np.clip(tt - w // 2 + w, 0, l)
            rows.append(1.0 / (hi - lo).astype(np.float32))
        r = np.concatenate(rows).astype(np.float32)
        return np.ascontiguousarray(np.broadcast_to(r[None, :], (128, r.size)))

    return {
        "ident": np.eye(128, dtype=np.float32),
        "masks": masks,
        "blk": blk,
        "invc_x": invc(64),
        "invc_c": invc(256),
    }


def build(SEQ, dump=()):
    assert SEQ % 512 == 0
    NT = SEQ // 512
    NTAU = CTX + SEQ
    NCOL = LAT0 + SEQ + 2
    nc = bass.Bass("TRN2", target_bir_lowering=False)

    def din(name, shape, dt=F32):
        return nc.dram_tensor(name, shape, dt, kind="ExternalInput").ap()

    def dscr(name, shape, dt):
        kind = "ExternalOutput" if name in dump else "Internal"
        return nc.dram_tensor(name, shape, dt, kind=kind).ap()

    x_in = din("x", [SEQ, D])
    ctx_in = din("ctx", [CTX, D])
    ccol_in = din("ccol", [128, 16])
    ada_w = [din("ada_w_e", [D, 3 * D]), din("ada_w_o", [D, 3 * D])]
    ada_b = [din("ada_b_e", [1, 3 * D]), din("ada_b_o", [1, 3 * D])]
    norm_g = [din("norm_e", [1, D]), din("norm_o", [1, D])]
    in_e = din("in_e", [D, 3072])
    out_e = din("out_e", [D, D])
    in_o = din("in_o", [D, 4096])
    out_o = din("out_o", [D, D])
    pool_w = din("pool_w", [128, 512])
    cp_in = din("cp", [128, NCP])
    w1l = din("w1l", [128, 256])
    a1l = din("a1l", [128, 256])
    g1l = din("g1l", [128, 384])
    w2l = din("w2l", [64, 512])
    a2l = din("a2l", [64, 512])
    g2l = din("g2l", [96, 512])
    final_g = din("final_g", [1, D])
    ident_in = din("ident", [128, 128])
    masks_in = din("masks", [128, 1536])
    blk_in = din("blk", [128, 128])
    invcx_in = din("invc_x", [128, 256])
    invcc_in = din("invc_c", [128, 1024])
    out = nc.dram_tensor("out", [SEQ, D], F32, kind="ExternalOutput").ap()

    X1 = dscr("X1", [SEQ, D], F32)
    XC1 = dscr("XC1", [CTX, D], F32)
    S_u = dscr("S_u", [DB, NCOL], BF16)
    S_r = dscr("S_r", [DB, NCOL], BF16)
    S_k = dscr("S_k", [DB, NCOL], BF16)
    S_v = dscr("S_v", [DB, NCOL], BF16)
    S_zc = dscr("S_zc", [DB, SEQ], BF16)
    S_zd = dscr("S_zd", [DB, SEQ], BF16)
    S_glu = dscr("S_glu", [DB, SEQ + 2 * GPAD], BF16)
    S_sw = [dscr("S_sw0", [DB, NTAU], F32), dscr("S_sw1", [DB, NTAU], F32)]
    S_a = [dscr("S_a0", [DB, NTAU], BF16), dscr("S_a1", [DB, NTAU], BF16)]
    S_g = dscr("S_g", [DB, NTAU], BF16)
    S_of = dscr("S_of", [SEQ, DB], F32)
    S_y = dscr("S_y", [D, SEQ], BF16)

    with ExitStack() as es:
        kb = KB(nc, es)
        op = kb.op

        identf = kb.sb(es, "identf", [128, 128], F32)
        identb = kb.sb(es, "identb", [128, 128], BF16)
        blk = kb.sb(es, "blk", [128, 128], F32)
        onesf = kb.sb(es, "onesf", [128, 128], F32)
        onesb = kb.sb(es, "onesb", [128, 128], BF16)
        cp = kb.sb(es, "cp", [128, NCP], F32)
        ccol = kb.sb(es, "ccol", [128, 16], F32)
        zb = kb.sb(es, "zb", [128, GPAD], BF16)
        kb.dma("sp", identf[:], ident_in, writes=["identf"])
        kb.dma("sp", blk[:], blk_in, writes=["blk"])
        kb.dma("sp", cp[:], cp_in, writes=["cp"])
        kb.dma("sp", ccol[:], ccol_in, writes=["ccol"])
        op("dve", lambda e: e.tensor_copy(out=identb[:], in_=identf[:]), reads=["identf"], writes=["identb"])
        op("dve", lambda e: e.memset(onesf[:], 1.0), writes=["onesf"])
        op("dve", lambda e: e.memset(onesb[:], 1.0), writes=["onesb"])
        op("dve", lambda e: e.memset(zb[:], 0.0), writes=["zb"])
        for S in (S_u, S_r, S_k, S_v):
            for c0 in (0, LAT0 - 2, NCOL - 2):
                kb.dma("pool", S[:, c0:c0 + 2].rearrange("(c p) n -> p c n", p=128),
                       zb[:, 0:8].rearrange("p (c n) -> p c n", c=4), reads=["zb"], writes=["Shalo"], key="Shalo")
        for c0 in (0, GPAD + SEQ):
            for c in range(4):
                kb.dma("pool", S_glu[c * 128:(c + 1) * 128, c0:c0 + GPAD], zb[:, :], reads=["zb"],
                       writes=["Sgluhalo"], key="Shalo")

        def cpc(name, i=0):
            o = CP[name] + i
            return cp[:, o:o + 1]

        def load_cast(es_, name, src, rows, cols_list, dst_shape, view):
            dst = kb.sb(es_, name, dst_shape, BF16)
            return dst

        cm = {}

        def alloc_common(stk):
            cm["stage"] = [kb.sb(stk, f"stage{i}", [128, 1024], F32) for i in range(2)]
            cm["tmpm"] = [kb.sb(stk, f"tmpm{i}", [128, 1024], F32) for i in range(2)]
            cm["junk"] = kb.sb(stk, "junk", [128, 1024], BF16)

        stage_i = [0]

        def load_w_bf16(dst, src_ap, kchunks, ncols, dkey):
            step = 1024
            for k in range(kchunks):
                for n0 in range(0, ncols, step):
                    n1 = min(ncols, n0 + step)
                    i = stage_i[0] % 2
                    stage_i[0] += 1
                    st = cm["stage"][i]
                    kb.dma("sp", st[:, :n1 - n0], src_ap[k * 128:(k + 1) * 128, n0:n1], writes=[f"stage{i}"])
                    eng = "act" if (stage_i[0] % 2) else "dve"
                    if eng == "act":
                        op("act", lambda e: e.activation(out=dst[:, k, n0:n1], in_=st[:, :n1 - n0], func=AF.Copy),
                           reads=[f"stage{i}"], writes=[dkey])
                    else:
                        op("dve", lambda e: e.tensor_copy(out=dst[:, k, n0:n1], in_=st[:, :n1 - n0]),
                           reads=[f"stage{i}"], writes=[dkey])

        def load_small_bf16(dst2d, src_ap, rows, ncols, dkey):
            i = stage_i[0] % 2
            stage_i[0] += 1
            st = cm["stage"][i]
            kb.dma("sp", st[:rows, :ncols], src_ap, writes=[f"stage{i}"])
            op("dve", lambda e: e.tensor_copy(out=dst2d, in_=st[:rows, :ncols]), reads=[f"stage{i}"], writes=[dkey])

        ss = kb.sb(es, "ss", [128, 4], F32)
        rs = kb.sb(es, "rs", [128, 4], F32)
        tmpm_i = [0]

        def norm_mod_T(xt, xkey, nsub, A, B, akeys, hb, hT, tag):
            junk, tmpm = cm["junk"], cm["tmpm"]
            for j in range(nsub):
                op("act", lambda e: e.activation(out=junk[:], in_=xt[:, j, :], func=AF.Square,
                                                 accum_out=ss[:, j:j + 1]),
                   reads=[xkey], writes=["junk", "ss"])
            op("dve", lambda e: e.tensor_scalar(out=rs[:, :nsub], in0=ss[:, :nsub], scalar1=1.0 / D, scalar2=1e-6,
                                                op0=ALU.mult, op1=ALU.add), reads=["ss"], writes=["rs"])
            op("act", lambda e: e.activation(out=rs[:, :nsub], in_=rs[:, :nsub], func=AF.Sqrt), reads=["rs"],
               writes=["rs"])
            op("dve", lambda e: e.reciprocal(out=rs[:, :nsub], in_=rs[:, :nsub]), reads=["rs"], writes=["rs"])
            for j in range(nsub):
                i = tmpm_i[0] % 2
                tmpm_i[0] += 1
                tm = tmpm[i]
                op("dve", lambda e: e.scalar_tensor_tensor(out=tm[:], in0=xt[:, j, :], scalar=rs[:, j:j + 1], in1=A[:],
                                                           op0=ALU.mult, op1=ALU.mult),
                   reads=[xkey, "rs"] + akeys, writes=[f"tmpm{i}"])
                op("dve", lambda e: e.tensor_tensor(out=hb[:, j, :], in0=tm[:], in1=B[:], op=ALU.add),
                   reads=[f"tmpm{i}"] + akeys, writes=[f"hb{j}"])
            for c in range(8):
                pt, pk = kb.pb()
                for j in range(nsub):
                    op("pe", lambda e: e.transpose(out=pt[:, j * 128:(j + 1) * 128], in_=hb[:, j, c * 128:(c + 1) * 128],
                                                   identity=identb[:]),
                       reads=[f"hb{j}", "identb"], writes=[pk])
                eng = "act" if c % 2 == 0 else "dve"
                if eng == "act":
                    op("act", lambda e: e.activation(out=hT[:, c, :nsub * 128], in_=pt[:, :nsub * 128], func=AF.Copy),
                       reads=[pk], writes=[f"{tag}{c}"])
                else:
                    op("dve", lambda e: e.tensor_copy(out=hT[:, c, :nsub * 128], in_=pt[:, :nsub * 128]),
                       reads=[pk], writes=[f"{tag}{c}"])

        def mm(out_ap, lhsT, rhs, start, stop, reads, writes):
            op("pe", lambda e: e.matmul(out_ap, lhsT=lhsT, rhs=rhs, start=start, stop=stop), reads=reads, writes=writes)

        def adaln(es_, layer, want_gate_ctx, es_ctx=None, pre=None):
            t = dict(pre or {})
            for nme in ("A", "B", "G", "Ac", "Bc") + (("Gc",) if want_gate_ctx else ()):
                if nme in t:
                    continue
                stk = es_ctx if (es_ctx is not None and nme != "G") else es_
                t[nme] = kb.sb(stk, f"mod{layer}{nme}", [128, D], F32)
            with ExitStack() as sub:
                sil = kb.sb(sub, "sil", [128, 16], F32)
                sbc = kb.sb(sub, "sbc", [128, 16, 128], F32)
                brow = kb.sb(sub, "brow", [1, 3 * D], F32)
                gbc = kb.sb(sub, "gbc", [128, 1, D], F32)
                wts = [kb.sb(sub, f"adaw{i}", [128, 512], F32) for i in range(3)]
                op("act", lambda e: e.activation(out=sil[:], in_=ccol[:], func=AF.Silu), reads=["ccol"], writes=["sil"])
                for i in range(16):
                    op("dve", lambda e: e.tensor_scalar(out=sbc[:, i, :], in0=onesf[:], scalar1=sil[:, i:i + 1],
                                                        scalar2=None, op0=ALU.mult),
                       reads=["onesf", "sil"], writes=["sbc"])
                kb.dma("sp", brow[:], ada_b[layer], writes=["brow"])
                kb.dma("sp", gbc[:], norm_g[layer].partition_broadcast(128), writes=["gbc"])
                wi = 0
                for n in range(6):
                    px, pxk = kb.pf()
                    pc, pck = kb.pf()
                    for k in range(8):
                        w = wts[wi % 3]
                        wk = f"adaw{wi % 3}"
                        wi += 1
                        kb.dma("sp", w[:], ada_w[layer][k * 128:(k + 1) * 128, n * 512:(n + 1) * 512], writes=[wk])
                        mm(px[:], sbc[:, k, :], w[:], k == 0, False, ["sbc", wk], [pxk])
                        mm(pc[:], sbc[:, 8 + k, :], w[:], k == 0, False, ["sbc", wk], [pck])
                    mm(px[:], onesf[0:1, :], brow[0:1, n * 512:(n + 1) * 512], False, True, ["onesf", "brow"], [pxk])
                    mm(pc[:], onesf[0:1, :], brow[0:1, n * 512:(n + 1) * 512], False, True, ["onesf", "brow"], [pck])
                    part, half = n // 2, n % 2
                    hs = slice(half * 512, (half + 1) * 512)
                    for (p_, pk_, sfx) in ((px, pxk, ""), (pc, pck, "c")):
                        if part == 0:
                            dst = t["B" + sfx]
                            op("act", lambda e: e.activation(out=dst[:, hs], in_=p_[:], func=AF.Copy), reads=[pk_],
                               writes=[f"mod{layer}B{sfx}"])
                        elif part == 1:
                            dst = t["A" + sfx]
                            op("dve", lambda e: e.scalar_tensor_tensor(out=dst[:, hs], in0=p_[:], scalar=1.0,
                                                                       in1=gbc[:, 0, hs], op0=ALU.add, op1=ALU.mult),
                               reads=[pk_, "gbc"], writes=[f"mod{layer}A{sfx}"])
                        else:
                            if ("G" + sfx) in t:
                                dst = t["G" + sfx]
                                op("act", lambda e: e.activation(out=dst[:, hs], in_=p_[:], func=AF.Copy), reads=[pk_],
                                   writes=[f"mod{layer}G{sfx}"])
                            else:
                                op("act", lambda e: e.activation(out=sil[:, 0:8], in_=p_[:, 0:8], func=AF.Copy),
                                   reads=[pk_], writes=["sil"])
                kb.barrier()
            return t

        with ExitStack() as L0:
            alloc_common(L0)
            mod0 = adaln(L0, 0, True)
            ine = kb.sb(L0, "ine", [128, 8, 3072], BF16)
            oute = kb.sb(L0, "oute", [128, 8, D], BF16)
            poolw = kb.sb(L0, "poolw", [128, 4, 128], BF16)
            load_w_bf16(ine, in_e, 8, 3072, "ine")
            load_w_bf16(oute, out_e, 8, D, "oute")
            load_small_bf16(poolw[:].rearrange("p g d -> p (g d)"), pool_w, 128, 512, "poolw")
            invcx = kb.sb(L0, "invcx", [128, 4, 1, 64], F32)
            invcc = kb.sb(L0, "invcc", [128, 4, 1, 256], F32)
            kb.dma("sp", invcx[:, :, 0, :], invcx_in.rearrange("p (g t) -> p g t", g=4), writes=["invcx"])
            kb.dma("sp", invcc[:, :, 0, :], invcc_in.rearrange("p (g t) -> p g t", g=4), writes=["invcc"])
            xt = kb.sb(L0, "xt", [128, 4, D], F32)
            hb = kb.sb(L0, "hb", [128, 4, D], BF16)
            hTs = [kb.sb(L0, f"hT{i}", [128, 8, 512], BF16) for i in range(2)]
            ybuf = kb.sb(L0, "ybuf", [128, 8, 512], BF16)
            xn = [kb.sb(L0, f"xn{i}", [128, D], F32) for i in range(2)]
            t2 = [kb.sb(L0, f"t2_{i}", [128, 512], F32) for i in range(2)]
            sza = kb.sb(L0, "sza", [128, 512], BF16)
            szb = kb.sb(L0, "szb", [128, 512], BF16)
            vb = kb.sb(L0, "vb", [128, 512], F32)
            gb = kb.sb(L0, "gb", [128, 512], F32)
            pm = kb.sb(L0, "pm", [128, 512], BF16)
            dw = kb.sb(L0, "dw", [128, 512], F32)
            dw2 = kb.sb(L0, "dw2", [128, 512], F32)
            geo = {}
            for gname, nrows, rowlen in (("x", 8, 64), ("c", 1, 256)):
                W = rowlen + 32
                g_ = {"nrows": nrows, "rowlen": rowlen, "W": W}
                g_["upad"] = kb.sb(L0, f"upad{gname}", [128, nrows, W], F32)
                g_["sA"] = kb.sb(L0, f"sA{gname}", [128, nrows, W], F32)
                g_["sB"] = kb.sb(L0, f"sB{gname}", [128, nrows, W], F32)
                g_["cvpad"] = kb.sb(L0, f"cvpad{gname}", [128, nrows, rowlen + 2], F32)
                g_["invc"] = invcx if gname == "x" else invcc
                g_["invk"] = "invcx" if gname == "x" else "invcc"
                g_["k"] = gname
                op("dve", lambda e: e.memset(g_["upad"][:], 0.0), writes=[f"upad{gname}"])
                op("dve", lambda e: e.memset(g_["cvpad"][:], 0.0), writes=[f"cvpad{gname}"])
                op("dve", lambda e: e.memset(g_["sA"][:], 0.0), writes=[f"sA{gname}"])
                op("dve", lambda e: e.memset(g_["sB"][:], 0.0), writes=[f"sB{gname}"])
                geo[gname] = g_
            xn_i = [0]

            def l0_front(src, ntok, A, B, akeys, par):
                nsub = ntok // 128
                kb.dma("sp", xt[:, :nsub, :], src.rearrange("(j p) d -> p j d", p=128), writes=["xt"])
                norm_mod_T(xt, "xt", nsub, A, B, akeys, hb, hTs[par], f"hT{par}_")

            def l0_back(src, dst, ntok, g_, G, akeys, par):
                nsub = ntok // 128
                hT = hTs[par]
                hkey = lambda c: f"hT{par}_{c}"
                nrows, rowlen, W, gk = g_["nrows"], g_["rowlen"], g_["W"], g_["k"]
                upad, sA, sB, cvpad = g_["upad"], g_["sA"], g_["sB"], g_["cvpad"]

                def proj(m):
                    ps, pk = kb.pf()
                    for k in range(8):
                        mm(ps[:, :ntok], ine[:, k, m * 128:(m + 1) * 128], hT[:, k, :ntok], k == 0, k == 7,
                           ["ine", hkey(k)], [pk])
                    return ps, pk

                def rows(ap2d):
                    return ap2d.rearrange("p (r l) -> p r l", r=nrows)

                for g in range(4):
                    ps, pk = proj(4 + g)
                    op("act", lambda e: e.activation(out=sza[:, :ntok], in_=ps[:, :ntok], func=AF.Silu), reads=[pk],
                       writes=["sza"])
                    ps, pk = proj(g)
                    op("act", lambda e: e.activation(out=upad[:, :, 16:16 + rowlen], in_=rows(ps[:, :ntok]),
                                                     func=AF.Copy), reads=[pk], writes=[f"upad{gk}"])
                    op("dve", lambda e: e.tensor_tensor(out=sA[:, :, 1:W], in0=upad[:, :, 0:W - 1], in1=upad[:, :, 1:W],
                                                        op=ALU.add), reads=[f"upad{gk}"], writes=[f"sA{gk}"])
                    cur, curk, oth, othk = sA, f"sA{gk}", sB, f"sB{gk}"
                    lo, hi = 1, W
                    for lvl in range(g):
                        sh = 1 << lvl
                        nlo, nhi = lo + sh, hi - sh
                        op("dve", lambda e: e.tensor_tensor(out=oth[:, :, nlo:nhi], in0=cur[:, :, nlo - sh:nhi - sh],
                                                            in1=cur[:, :, nlo + sh:nhi + sh], op=ALU.add),
                           reads=[curk], writes=[othk])
                        cur, curk, oth, othk = oth, othk, cur, curk
                        lo, hi = nlo, nhi
                    op("dve", lambda e: e.tensor_tensor(out=rows(dw[:, :ntok]), in0=cur[:, :, 16:16 + rowlen],
                                                        in1=g_["invc"][:, g, :, :].broadcast_to([128, nrows, rowlen]),
                                                        op=ALU.mult), reads=[curk, g_["invk"]], writes=["dw"])
                    op("dve", lambda e: e.tensor_tensor(out=rows(pm[:, :ntok]), in0=rows(dw[:, :ntok]),
                                                        in1=upad[:, :, 16:16 + rowlen], op=ALU.subtract),
                       reads=["dw", f"upad{gk}"], writes=["pm"])
                    ps, pk = kb.pf()
                    mm(ps[:, :ntok], poolw[:, g, :], pm[:, :ntok], True, True, ["poolw", "pm"], [pk])
                    op("dve", lambda e: e.scalar_tensor_tensor(out=ybuf[:, g, :ntok], in0=ps[:, :ntok],
                                                               scalar=cpc("pool_scale", g), in1=sza[:, :ntok],
                                                               op0=ALU.mult, op1=ALU.mult),
                       reads=[pk, "cp", "sza"], writes=[f"y{g}"])
                for c in range(4):
                    ps, pk = proj(8 + c)
                    op("act", lambda e: e.activation(out=vb[:, :ntok], in_=ps[:, :ntok], func=AF.Copy), reads=[pk],
                       writes=["vb"])
                    ps, pk = proj(12 + c)
                    op("act", lambda e: e.activation(out=gb[:, :ntok], in_=ps[:, :ntok], func=AF.Copy), reads=[pk],
                       writes=["gb"])
                    ps, pk = proj(20 + c)
                    op("act", lambda e: e.activation(out=szb[:, :ntok], in_=ps[:, :ntok], func=AF.Silu), reads=[pk],
                       writes=["szb"])
                    ps, pk = proj(16 + c)
                    op("dve", lambda e: e.tensor_tensor(out=cvpad[:, :, 1:1 + rowlen], in0=rows(ps[:, :ntok]),
                                                        in1=rows(vb[:, :ntok]), op=ALU.mult),
                       reads=[pk, "vb"], writes=[f"cvpad{gk}"])
                    op("dve", lambda e: e.tensor_scalar(out=rows(dw[:, :ntok]), in0=cvpad[:, :, 0:rowlen],
                                                        scalar1=cpc("sconv_w", 0 * 4 + c), scalar2=None, op0=ALU.mult),
                       reads=[f"cvpad{gk}", "cp"], writes=["dw"])
                    op("dve", lambda e: e.scalar_tensor_tensor(out=rows(dw2[:, :ntok]), in0=cvpad[:, :, 1:1 + rowlen],
                                                               scalar=cpc("sconv_w", 1 * 4 + c), in1=rows(dw[:, :ntok]),
                                                               op0=ALU.mult, op1=ALU.add),
                       reads=[f"cvpad{gk}", "cp", "dw"], writes=["dw2"])
                    op("dve", lambda e: e.scalar_tensor_tensor(out=rows(dw[:, :ntok]), in0=cvpad[:, :, 2:2 + rowlen],
                                                               scalar=cpc("sconv_w", 2 * 4 + c), in1=rows(dw2[:, :ntok]),
                                                               op0=ALU.mult, op1=ALU.add),
                       reads=[f"cvpad{gk}", "cp", "dw2"], writes=["dw"])
                    op("dve", lambda e: e.tensor_tensor(out=dw2[:, :ntok], in0=dw[:, :ntok], in1=gb[:, :ntok],
                                                         op=ALU.mult), reads=["dw", "gb"], writes=["dw2"])
                    op("dve", lambda e: e.tensor_tensor(out=ybuf[:, 4 + c, :ntok], in0=dw2[:, :ntok], in1=szb[:, :ntok],
                                                         op=ALU.mult), reads=["dw2", "szb"], writes=[f"y{4 + c}"])
                ykeys = [f"y{c}" for c in range(8)]
                for j in range(nsub):
                    i = xn_i[0] % 2
                    xn_i[0] += 1
                    kb.dma("pool", xn[i][:], src[j * 128:(j + 1) * 128, :], writes=[f"xn{i}"], key=f"xnld{i}")
                    for half in range(2):
                        hs = slice(half * 512, (half + 1) * 512)
                        ps, pk = kb.pf()
                        for c in range(8):
                            mm(ps[:], ybuf[:, c, j * 128:(j + 1) * 128], oute[:, c, hs], c == 0, c == 7,
                               [f"y{c}", "oute"], [pk])
                        tt = t2[half]
                        op("dve", lambda e: e.tensor_tensor(out=tt[:], in0=ps[:], in1=G[:, hs], op=ALU.mult),
                           reads=[pk] + akeys, writes=[f"t2_{half}"])
                        op("dve", lambda e: e.tensor_tensor(out=xn[i][:, hs], in0=tt[:], in1=xn[i][:, hs], op=ALU.add),
                           reads=[f"t2_{half}", f"xn{i}"], writes=[f"xn{i}"])
                    kb.dma("pool", dst[j * 128:(j + 1) * 128, :], xn[i][:], reads=[f"xn{i}"], writes=["X1scr"],
                           key="xnst")

            tiles0 = [(ctx_in, XC1, CTX, geo["c"], mod0["Ac"], mod0["Bc"], mod0["Gc"], ["mod0Ac", "mod0Bc", "mod0Gc"])]
            for it in range(NT):
                tiles0.append((x_in[it * 512:(it + 1) * 512, :], X1[it * 512:(it + 1) * 512, :], 512, geo["x"],
                               mod0["A"], mod0["B"], mod0["G"], ["mod0A", "mod0B", "mod0G"]))
            for ti, (src, dst, ntok, g_, A_, B_, G_, ak_) in enumerate(tiles0):
                if ti == 0:
                    l0_front(src, ntok, A_, B_, ak_, 0)
                if ti + 1 < len(tiles0):
                    nx = tiles0[ti + 1]
                    l0_front(nx[0], nx[2], nx[4], nx[5], nx[7], (ti + 1) % 2)
                l0_back(src, dst, ntok, g_, G_, ak_, ti % 2)
            kb.barrier()

        if "STOP_L0" in dump:
            kb.finish(["X1scr"])
            return nc

        with ExitStack() as L1:
            G1pre = kb.sb(L1, "mod1G", [128, D], F32)
            with ExitStack() as L1a:
                alloc_common(L1a)
                mod1 = adaln(L1, 1, False, es_ctx=L1a, pre={"G": G1pre})
                ino = kb.sb(L1a, "ino", [128, 8, 4096], BF16)
                load_w_bf16(ino, in_o, 8, 4096, "ino")
                xt = kb.sb(L1a, "xt1", [128, 4, D], F32)
                hb = kb.sb(L1a, "hb1", [128, 4, D], BF16)
                hT1s = [kb.sb(L1a, f"hT1_{i}", [128, 8, 512], BF16) for i in range(2)]
                obuf = [kb.sb(L1a, f"obuf{s}", [128, 4, 512], BF16) for s in range(7)]
                p1t = kb.sb(L1a, "p1t", [128, 512], F32)
                sgt = kb.sb(L1a, "sgt", [128, 512], F32)

                def l1_front(src, ntok, A, B, akeys, par):
                    nsub = ntok // 128
                    kb.dma("sp", xt[:, :nsub, :], src.rearrange("(j p) d -> p j d", p=128), reads=["X1scr"],
                           writes=["xt1"])
                    norm_mod_T(xt, "xt1", nsub, A, B, akeys, hb, hT1s[par], f"hU{par}_")

                def l1_back(ntok, col0, lat0, is_ctx, par):
                    hT = hT1s[par]

                    def proj(m):
                        ps, pk = kb.pf()
                        for k in range(8):
                            mm(ps[:, :ntok], ino[:, k, m * 128:(m + 1) * 128], hT[:, k, :ntok], k == 0, k == 7,
                               ["ino", f"hU{par}_{k}"], [pk])
                        return ps, pk

                    for s, S in enumerate((S_u, S_r, S_k, S_v)):
                        for c in range(4):
                            ps, pk = proj(s * 4 + c)
                            if c % 2 == 0:
                                op("act", lambda e: e.activation(out=obuf[s][:, c, :ntok], in_=ps[:, :ntok],
                                                                 func=AF.Copy), reads=[pk], writes=[f"obuf{s}"])
                            else:
                                op("dve", lambda e: e.tensor_copy(out=obuf[s][:, c, :ntok], in_=ps[:, :ntok]),
                                   reads=[pk], writes=[f"obuf{s}"])
                        kb.dma("pool", S[:, col0:col0 + ntok].rearrange("(c p) n -> p c n", p=128),
                               obuf[s][:, :, :ntok], reads=[f"obuf{s}"], writes=["Sstreams"], key=f"obst{s}")
                    if is_ctx:
                        return
                    for c in range(4):
                        ps, pk = proj(16 + c)
                        op("act", lambda e: e.activation(out=obuf[4][:, c, :], in_=ps[:], func=AF.Silu), reads=[pk],
                           writes=["obuf4"])
                        ps, pk = proj(28 + c)
                        op("act", lambda e: e.activation(out=obuf[5][:, c, :], in_=ps[:], func=AF.Silu), reads=[pk],
                           writes=["obuf5"])
                        ps, pk = proj(20 + c)
                        op("act", lambda e: e.activation(out=p1t[:], in_=ps[:], func=AF.Copy), reads=[pk],
                           writes=["p1t"])
                        ps, pk = proj(24 + c)
                        op("act", lambda e: e.activation(out=sgt[:], in_=ps[:], func=AF.Sigmoid), reads=[pk],
                           writes=["sgt"])
                        op("dve", lambda e: e.tensor_tensor(out=obuf[6][:, c, :], in0=p1t[:], in1=sgt[:], op=ALU.mult),
                           reads=["p1t", "sgt"], writes=["obuf6"])
                    kb.dma("pool", S_zc[:, lat0:lat0 + 512].rearrange("(c p) n -> p c n", p=128), obuf[4][:],
                           reads=["obuf4"], writes=["Sstreams"], key="obst4")
                    kb.dma("pool", S_zd[:, lat0:lat0 + 512].rearrange("(c p) n -> p c n", p=128), obuf[5][:],
                           reads=["obuf5"], writes=["Sstreams"], key="obst5")
                    kb.dma("pool", S_glu[:, GPAD + lat0:GPAD + lat0 + 512].rearrange("(c p) n -> p c n", p=128),
                           obuf[6][:], reads=["obuf6"], writes=["Sstreams"], key="obst6")

                tiles1 = [(XC1, CTX, CTX0, 0, mod1["Ac"], mod1["Bc"], ["mod1Ac", "mod1Bc"], True)]
                for it in range(NT):
                    tiles1.append((X1[it * 512:(it + 1) * 512, :], 512, LAT0 + it * 512, it * 512, mod1["A"], mod1["B"],
                                   ["mod1A", "mod1B"], False))
                for ti, (src, ntok, col0, lat0, A_, B_, ak_, isc) in enumerate(tiles1):
                    if ti == 0:
                        l1_front(src, ntok, A_, B_, ak_, 0)
                    if ti + 1 < len(tiles1):
                        nx = tiles1[ti + 1]
                        l1_front(nx[0], nx[1], nx[4], nx[5], nx[6], (ti + 1) % 2)
                    l1_back(ntok, col0, lat0, isc, ti % 2)
                kb.barrier()

            if "STOP_L1A" in dump:
                kb.finish(["Sstreams"])
                return nc

            with ExitStack() as L1b:
                alloc_common(L1b)
                w1b = kb.sb(L1b, "w1b", [128, 4, 64], BF16)
                a1b = kb.sb(L1b, "a1b", [128, 4, 64], BF16)
                g1b = kb.sb(L1b, "g1b", [128, 4, 96], BF16)
                w2b = kb.sb(L1b, "w2b", [64, 512], BF16)
                a2b = kb.sb(L1b, "a2b", [64, 512], BF16)
                g2b = kb.sb(L1b, "g2b", [96, 512], BF16)
                load_small_bf16(w1b[:].rearrange("p c r -> p (c r)"), w1l, 128, 256, "w1b")
                load_small_bf16(a1b[:].rearrange("p c r -> p (c r)"), a1l, 128, 256, "a1b")
                load_small_bf16(g1b[:].rearrange("p c r -> p (c r)"), g1l, 128, 384, "g1b")
                load_small_bf16(w2b[:], w2l, 64, 512, "w2b")
                load_small_bf16(a2b[:], a2l, 64, 512, "a2b")
                load_small_bf16(g2b[:], g2l, 96, 512, "g2b")
                LS = []
                for p_ in range(2):
                    d_ = {"ut": kb.sb(L1b, f"ut{p_}", [128, 4, 514], BF16),
                          "nbt": kb.sb(L1b, f"nbt{p_}", [128, 4, 512], F32),
                          "um": [kb.sb(L1b, f"um{p_}_{j}", [128, 4, 512], BF16) for j in range(3)],
                          "hw": kb.sb(L1b, f"hw{p_}", [64, 512], BF16),
                          "ha": kb.sb(L1b, f"ha{p_}", [64, 512], BF16),
                          "hg": kb.sb(L1b, f"hg{p_}", [96, 512], BF16),
                          "swo": [kb.sb(L1b, f"swo{p_}_{d}", [128, 4, 512], F32) for d in range(2)],
                          "ao": [kb.sb(L1b, f"ao{p_}_{d}", [128, 4, 512], BF16) for d in range(2)],
                          "go": kb.sb(L1b, f"go{p_}", [128, 4, 512], BF16)}
                    LS.append(d_)
                dw_l = kb.sb(L1b, "dw_l", [128, 512], F32)

                def lora_front(col0, ntok, p_):
                    d_ = LS[p_]
                    u, nbt, um = d_["ut"], d_["nbt"], d_["um"]
                    uk, nk = f"ut{p_}", f"nbt{p_}"
                    kb.dma("sp", u[:, :, :ntok + 2], S_u[:, col0 - 1:col0 + ntok + 1].rearrange("(c p) n -> p c n", p=128),
                           reads=["Sstreams", "Shalo"], writes=[uk])
                    op("dve", lambda e: e.tensor_tensor(out=nbt[:, :, :ntok], in0=u[:, :, 0:ntok], in1=u[:, :, 2:ntok + 2],
                                                        op=ALU.add), reads=[uk], writes=[nk])
                    op("dve", lambda e: e.scalar_tensor_tensor(out=nbt[:, :, :ntok], in0=nbt[:, :, :ntok], scalar=0.5,
                                                               in1=u[:, :, 1:ntok + 1], op0=ALU.mult, op1=ALU.subtract),
                       reads=[nk, uk], writes=[nk])
                    for j in range(3):
                        for c in range(4):
                            if c != 3:
                                op("dve", lambda e: e.scalar_tensor_tensor(out=um[j][:, c, :ntok], in0=nbt[:, c, :ntok],
                                                                           scalar=cpc("mu", (3 + j) * 4 + c),
                                                                           in1=u[:, c, 1:ntok + 1], op0=ALU.mult,
                                                                           op1=ALU.add),
                                   reads=[nk, uk, "cp"], writes=[f"um{p_}_{j}"])
                            else:
                                op("dve", lambda e: e.tensor_scalar(out=dw_l[:, :ntok], in0=nbt[:, c, :ntok],
                                                                     scalar1=cpc("mu", (3 + j) * 4 + c), scalar2=None,
                                                                     op0=ALU.mult),
                                   reads=[nk, "cp"], writes=["dw_l"])
                                op("dve", lambda e: e.tensor_tensor(out=um[j][:, c, :ntok], in0=dw_l[:, :ntok],
                                                                     in1=u[:, c, 1:ntok + 1], op=ALU.add),
                                   reads=["dw_l", uk], writes=[f"um{p_}_{j}"])

                def lora_back(tau0, ntok, p_):
                    d_ = LS[p_]
                    um, hw, ha, hg, swo, ao, go = d_["um"], d_["hw"], d_["ha"], d_["hg"], d_["swo"], d_["ao"], d_["go"]
                    for j, (wb_, hid, nh, fn) in enumerate(((w1b, hw, 64, AF.Tanh), (a1b, ha, 64, AF.Copy),
                                                             (g1b, hg, 96, AF.Sigmoid))):
                        ps, pk = kb.pf()
                        for c in range(4):
                            mm(ps[:nh, :ntok], wb_[:, c, :], um[j][:, c, :ntok], c == 0, c == 3,
                               [("w1b", "a1b", "g1b")[j], f"um{p_}_{j}"], [pk])
                        op("act", lambda e: e.activation(out=hid[:, :ntok], in_=ps[:nh, :ntok], func=fn), reads=[pk],
                           writes=[("hw", "ha", "hg")[j] + str(p_)])
                    for d in range(2):
                        for c in range(4):
                            ps, pk = kb.pf()
                            mm(ps[:, :ntok], w2b[32 * d:32 * d + 32, c * 128:(c + 1) * 128], hw[32 * d:32 * d + 32, :ntok],
                               True, True, ["w2b", f"hw{p_}"], [pk])
                            op("act", lambda e: e.activation(out=swo[d][:, c, :ntok], in_=ps[:, :ntok], func=AF.Sigmoid,
                                                             bias=cpc("w0", d * 4 + c), scale=1.0),
                               reads=[pk, "cp"], writes=[f"swo{p_}_{d}"])
                            ps, pk = kb.pf()
                            mm(ps[:, :ntok], a2b[32 * d:32 * d + 32, c * 128:(c + 1) * 128], ha[32 * d:32 * d + 32, :ntok],
                               True, True, ["a2b", f"ha{p_}"], [pk])
                            op("act", lambda e: e.activation(out=ao[d][:, c, :ntok], in_=ps[:, :ntok], func=AF.Sigmoid,
                                                             bias=cpc("a0", d * 4 + c), scale=1.0),
                               reads=[pk, "cp"], writes=[f"ao{p_}_{d}"])
                        kb.dma("pool", S_sw[d][:, tau0:tau0 + ntok].rearrange("(c p) n -> p c n", p=128),
                               swo[d][:, :, :ntok], reads=[f"swo{p_}_{d}"], writes=["Slora"], key=f"swst{d}")
                        kb.dma("pool", S_a[d][:, tau0:tau0 + ntok].rearrange("(c p) n -> p c n", p=128),
                               ao[d][:, :, :ntok], reads=[f"ao{p_}_{d}"], writes=["Slora"], key=f"aost{d}")
                    for c in range(4):
                        ps, pk = kb.pf()
                        mm(ps[:, :ntok], g2b[:, c * 128:(c + 1) * 128], hg[:, :ntok], True, True, ["g2b", f"hg{p_}"], [pk])
                        op("dve", lambda e: e.tensor_copy(out=go[:, c, :ntok], in_=ps[:, :ntok]), reads=[pk],
                           writes=[f"go{p_}"])
                    kb.dma("pool", S_g[:, tau0:tau0 + ntok].rearrange("(c p) n -> p c n", p=128), go[:, :, :ntok],
                           reads=[f"go{p_}"], writes=["Slora"], key="gost")

                ltiles = [(CTX0, 0, CTX)] + [(LAT0 + it * 512, CTX + it * 512, 512) for it in range(NT)]
                lora_front(ltiles[0][0], ltiles[0][2], 0)
                for ti, (col0, tau0, ntok) in enumerate(ltiles):
                    if ti + 1 < len(ltiles):
                        lora_front(ltiles[ti + 1][0], ltiles[ti + 1][2], (ti + 1) % 2)
                    lora_back(tau0, ntok, ti % 2)
                kb.barrier()

            if "STOP_L1B" in dump:
                kb.finish(["Slora", "Sstreams"])
                return nc

            with ExitStack() as SC:
                scan_phase(nc, kb, SC, SEQ, NT, cp, cpc, identf, identb, masks_in, blk, onesf,
                           S_r, S_k, S_v, S_sw, S_a, S_g, S_zc, S_of, S_y, mm, dump=dump)
                kb.barrier()

            if "STOP_SCAN" in dump:
                kb.finish(["Sy", "Sof"])
                return nc

            with ExitStack() as CF:
                cdiag = kb.sb(CF, "cdiag", [128, 124, 128], BF16)
                for j in range(31):
                    for c in range(4):
                        eng = "dve"
                        op(eng, lambda e: e.tensor_scalar(out=cdiag[:, j * 4 + c, :], in0=identf[:],
                                                          scalar1=cpc("conf_dw_w", j * 4 + c), scalar2=None,
                                                          op0=ALU.mult), reads=["identf", "cp"], writes=["cdiag"])
                gl = [kb.sb(CF, f"gl{i}", [128, 512 + 2 * GPAD], BF16) for i in range(3)]
                CS = []
                for p_ in range(2):
                    d_ = {}
                    d_["hc"] = kb.sb(CF, f"hc{p_}", [128, 4, 512], F32)
                    d_["hcb"] = kb.sb(CF, f"hcb{p_}", [128, 4, 512], BF16)
                    d_["hsq"] = kb.sb(CF, f"hsq{p_}", [128, 4, 512], BF16)
                    d_["szd"] = kb.sb(CF, f"szd{p_}", [128, 4, 512], BF16)
                    d_["yc"] = kb.sb(CF, f"yc{p_}", [128, 4, 512], BF16)
                    CS.append(d_)
                mean = kb.sb(CF, "mean", [128, 512], F32)
                msq = kb.sb(CF, "msq", [128, 512], F32)
                rstd = kb.sb(CF, "rstd", [128, 512], F32)
                t1s = [kb.sb(CF, f"t1_{i}", [128, 512], F32) for i in range(2)]
                t3s = [kb.sb(CF, f"t3_{i}", [128, 512], F32) for i in range(2)]
                gi = [0]
                cacc = [kb.sb(CF, f"cacc{i}", [128, 512], F32) for i in range(2)]
                cacc_i = [0]

                def conf_A(it):
                    p_ = it % 2
                    d_ = CS[p_]
                    t0 = it * 512
                    kb.dma("sp", d_["szd"][:], S_zd[:, t0:t0 + 512].rearrange("(c p) n -> p c n", p=128),
                           reads=["Sstreams"], writes=[f"szd{p_}"])
                    for c in range(4):
                        g_ = gl[gi[0] % 3]
                        gk = f"gl{gi[0] % 3}"
                        gi[0] += 1
                        kb.dma("sp", g_[:], S_glu[c * 128:(c + 1) * 128, t0:t0 + 512 + 2 * GPAD],
                               reads=["Sstreams", "Sgluhalo"], writes=[gk])
                        NPE = 21
                        ps, pk = kb.pf()
                        for j in range(NPE):
                            mm(ps[:], cdiag[:, j * 4 + c, :], g_[:, 64 * j:64 * j + 512], j == 0, j == NPE - 1,
                               ["cdiag", gk], [pk])
                        ac = cacc[cacc_i[0] % 2]
                        ack = f"cacc{cacc_i[0] % 2}"
                        cacc_i[0] += 1
                        for j in range(NPE, 31):
                            if j == NPE:
                                op("dve", lambda e: e.tensor_scalar(out=ac[:], in0=g_[:, 64 * j:64 * j + 512],
                                                                    scalar1=cpc("conf_dw_w", j * 4 + c), scalar2=None,
                                                                    op0=ALU.mult), reads=[gk, "cp"], writes=[ack])
                            else:
                                op("dve", lambda e: e.scalar_tensor_tensor(out=ac[:], in0=g_[:, 64 * j:64 * j + 512],
                                                                           scalar=cpc("conf_dw_w", j * 4 + c), in1=ac[:],
                                                                           op0=ALU.mult, op1=ALU.add),
                                   reads=[gk, "cp", ack], writes=[ack])
                        op("dve", lambda e: e.scalar_tensor_tensor(out=d_["hc"][:, c, :], in0=ps[:],
                                                                   scalar=cpc("conf_dw_b", c), in1=ac[:], op0=ALU.add,
                                                                   op1=ALU.add),
                           reads=[pk, "cp", ack], writes=[f"hc{p_}_{c}"])
                        op("act", lambda e: e.activation(out=d_["hsq"][:, c, :], in_=d_["hc"][:, c, :], func=AF.Square),
                           reads=[f"hc{p_}_{c}"], writes=[f"hsq{p_}_{c}"])
                        op("act", lambda e: e.activation(out=d_["hcb"][:, c, :], in_=d_["hc"][:, c, :], func=AF.Copy),
                           reads=[f"hc{p_}_{c}"], writes=[f"hcb{p_}_{c}"])

                def conf_B(it):
                    p_ = it % 2
                    d_ = CS[p_]
                    t0 = it * 512
                    pm_, pmk = kb.pf()
                    pq_, pqk = kb.pf()
                    for c in range(4):
                        mm(pm_[:], onesb[:], d_["hcb"][:, c, :], c == 0, c == 3, ["onesb", f"hcb{p_}_{c}"], [pmk])
                    for c in range(4):
                        mm(pq_[:], onesb[:], d_["hsq"][:, c, :], c == 0, c == 3, ["onesb", f"hsq{p_}_{c}"], [pqk])
                    op("dve", lambda e: e.tensor_scalar(out=mean[:], in0=pm_[:], scalar1=1.0 / DB, scalar2=None,
                                                        op0=ALU.mult), reads=[pmk], writes=["mean"])
                    op("dve", lambda e: e.tensor_tensor(out=msq[:], in0=mean[:], in1=mean[:], op=ALU.mult),
                       reads=["mean"], writes=["msq"])
                    op("dve", lambda e: e.scalar_tensor_tensor(out=rstd[:], in0=pq_[:], scalar=1.0 / DB, in1=msq[:],
                                                               op0=ALU.mult, op1=ALU.subtract),
                       reads=[pqk, "msq"], writes=["rstd"])
                    op("dve", lambda e: e.tensor_scalar(out=rstd[:], in0=rstd[:], scalar1=1e-5, scalar2=None,
                                                        op0=ALU.add), reads=["rstd"], writes=["rstd"])
                    op("act", lambda e: e.activation(out=rstd[:], in_=rstd[:], func=AF.Sqrt), reads=["rstd"],
                       writes=["rstd"])
                    op("dve", lambda e: e.reciprocal(out=rstd[:], in_=rstd[:]), reads=["rstd"], writes=["rstd"])
                    for c in range(4):
                        t1, t3 = t1s[c % 2], t3s[c % 2]
                        t1k, t3k = f"t1_{c % 2}", f"t3_{c % 2}"
                        op("dve", lambda e: e.tensor_tensor(out=t1[:], in0=d_["hc"][:, c, :], in1=mean[:], op=ALU.subtract),
                           reads=[f"hc{p_}_{c}", "mean"], writes=[t1k])
                        op("dve", lambda e: e.tensor_tensor(out=t3[:], in0=t1[:], in1=rstd[:], op=ALU.mult),
                           reads=[t1k, "rstd"], writes=[t3k])
                        op("act", lambda e: e.activation(out=t1[:], in_=t3[:], func=AF.Silu,
                                                         bias=cpc("conf_ln_b", c), scale=cpc("conf_ln_g", c)),
                           reads=[t3k, "cp"], writes=[t1k])
                        op("dve", lambda e: e.tensor_tensor(out=d_["yc"][:, c, :], in0=t1[:], in1=d_["szd"][:, c, :],
                                                            op=ALU.mult), reads=[t1k, f"szd{p_}"], writes=[f"yc{p_}"])
                    kb.dma("pool", S_y[DB:D, t0:t0 + 512].rearrange("(c p) n -> p c n", p=128), d_["yc"][:],
                           reads=[f"yc{p_}"], writes=[f"Syc{p_}"], key=f"ycst{p_}")

                fuse_post = "STOP_CONF" not in dump
                if fuse_post:
                    cm["stage"] = [kb.sb(CF, f"stageP{i}", [128, 1024], F32) for i in range(2)]
                    cm["junk"] = kb.sb(CF, "junkP", [128, 1024], BF16)
                    outo = kb.sb(CF, "outo", [128, 8, D], BF16)
                    load_w_bf16(outo, out_o, 8, D, "outo")
                    fg = kb.sb(CF, "fg", [128, 1, D], F32)
                    kb.dma("sp", fg[:], final_g.partition_broadcast(128), writes=["fg"])
                    G1 = mod1["G"]
                    yt = [kb.sb(CF, f"yt{i}", [128, 8, 512], BF16) for i in range(2)]
                    x1t = kb.sb(CF, "x1t", [128, 4, D], F32)
                    x2 = [kb.sb(CF, f"x2_{i}", [128, D], F32) for i in range(2)]
                    ot = [kb.sb(CF, f"ot{i}", [128, D], F32) for i in range(2)]
                    t2 = [kb.sb(CF, f"t2p{i}", [128, 512], F32) for i in range(2)]
                    ssp = kb.sb(CF, "ssp", [128, 2], F32)
                    rsp = kb.sb(CF, "rsp", [128, 2], F32)
                xi = [0]

                def post_tile(it):
                    t0 = it * 512
                    y_ = yt[it % 2]
                    kb.dma("sp", y_[:], S_y[:, t0:t0 + 512].rearrange("(c p) n -> p c n", p=128),
                           reads=["Sy", f"Syc{it % 2}"], writes=[f"yt{it % 2}"])
                    kb.dma("sp", x1t[:], X1[t0:t0 + 512, :].rearrange("(j p) d -> p j d", p=128), reads=["X1scr"],
                           writes=["x1t"])
                    for j in range(4):
                        i = xi[0] % 2
                        xi[0] += 1
                        for half in range(2):
                            hs = slice(half * 512, (half + 1) * 512)
                            ps, pk = kb.pf()
                            for c in range(8):
                                mm(ps[:], y_[:, c, j * 128:(j + 1) * 128], outo[:, c, hs], c == 0, c == 7,
                                   [f"yt{it % 2}", "outo"], [pk])
                            op("dve", lambda e: e.tensor_tensor(out=t2[half][:], in0=ps[:], in1=G1[:, hs], op=ALU.mult),
                               reads=[pk, "mod1G"], writes=[f"t2p{half}"])
                            op("dve", lambda e: e.tensor_tensor(out=x2[i][:, hs], in0=t2[half][:], in1=x1t[:, j, hs],
                                                                op=ALU.add),
                               reads=[f"t2p{half}", "x1t"], writes=[f"x2_{i}"])
                        op("act", lambda e: e.activation(out=cm["junk"][:], in_=x2[i][:], func=AF.Square,
                                                         accum_out=ssp[:, i:i + 1]),
                           reads=[f"x2_{i}"], writes=["junk", f"ssp{i}"])
                        op("dve", lambda e: e.tensor_scalar(out=rsp[:, i:i + 1], in0=ssp[:, i:i + 1], scalar1=1.0 / D,
                                                            scalar2=1e-6, op0=ALU.mult, op1=ALU.add),
                           reads=[f"ssp{i}"], writes=[f"rsp{i}"])
                        op("act", lambda e: e.activation(out=rsp[:, i:i + 1], in_=rsp[:, i:i + 1], func=AF.Sqrt),
                           reads=[f"rsp{i}"], writes=[f"rsp{i}"])
                        op("dve", lambda e: e.reciprocal(out=rsp[:, i:i + 1], in_=rsp[:, i:i + 1]), reads=[f"rsp{i}"],
                           writes=[f"rsp{i}"])
                        op("dve", lambda e: e.scalar_tensor_tensor(out=ot[i][:], in0=x2[i][:], scalar=rsp[:, i:i + 1],
                                                                   in1=fg[:, 0, :], op0=ALU.mult, op1=ALU.mult),
                           reads=[f"x2_{i}", f"rsp{i}", "fg"], writes=[f"ot{i}"])
                        kb.dma("pool", out[t0 + j * 128:t0 + (j + 1) * 128, :], ot[i][:], reads=[f"ot{i}"],
                               writes=["OUT"], key=f"otst{i}")

                conf_A(0)
                for it in range(NT):
                    if it + 1 < NT:
                        conf_A(it + 1)
                    conf_B(it)
                    if fuse_post and it >= 1:
                        post_tile(it - 1)
                if fuse_post:
                    post_tile(NT - 1)
                kb.barrier()

            if "STOP_CONF" in dump:
                kb.finish(["Sy", "Syc0", "Syc1"])
                return nc
        kb.finish(["OUT"])
    return nc


def scan_phase(nc, kb, SC, SEQ, NT, cp, cpc, identf, identb, masks_in, blk, onesf,
               S_r, S_k, S_v, S_sw, S_a, S_g, S_zc, S_of, S_y, mm, dump=()):
    op = kb.op
    masks = kb.sb(SC, "masks", [128, 2, 6, 128], F32)
    kb.dma("sp", masks[:], masks_in.rearrange("p (d m t) -> p d m t", d=2, m=6), writes=["masks"])
    blkb = kb.sb(SC, "blkb", [128, 128], BF16)
    op("dve", lambda e: e.tensor_copy(out=blkb[:], in_=blk[:]), reads=["blk"], writes=["blkb"])
    sqb = kb.sb(SC, "sqb", [128, 512], BF16)
    omka = kb.sb(SC, "omka", [128, 4], F32)
    kac = cp[:, CP["k_a"]:CP["k_a"] + 4]
    op("dve", lambda e: e.tensor_scalar(out=omka[:], in0=kac, scalar1=-1.0, scalar2=1.0, op0=ALU.mult, op1=ALU.add),
       reads=["cp"], writes=["omka"])
    epsb = kb.sb(SC, "epsb", [128, 2], F32)
    op("dve", lambda e: e.memset(epsb[:], 1e-18), writes=["epsb"])
    ones128 = kb.sb(SC, "ones128", [128, 128], F32)
    op("dve", lambda e: e.memset(ones128[:], 1.0), writes=["ones128"])
    NSLOT = 6

    class NS:
        pass

    TMP = NS()
    for n in ("rt", "kt", "vt"):
        setattr(TMP, n, kb.sb(SC, f"{n}T", [128, 514], BF16))
    for n in ("atd", "at0", "vb16", "Af"):
        setattr(TMP, n, kb.sb(SC, f"{n}T", [128, 512], BF16))
    for n in ("swt", "rp", "kp", "vp", "kkr", "kkn", "kd", "bb", "lw", "cs", "csr", "ig", "gp", "tA", "tB", "ksum"):
        setattr(TMP, n, kb.sb(SC, f"{n}T", [128, 512], F32))
    TMPN = set(vars(TMP).keys())
    PB = []
    for b in range(2):
        P = NS()
        P.b = b
        P.k = lambda n, b=b: (f"{n}_pT" if n in TMPN else f"{n}_p{b}")
        for n in TMPN:
            setattr(P, n, getattr(TMP, n))
        for n in ("gt_", "zct", "Bf", "Kf", "yo"):
            setattr(P, n, kb.sb(SC, f"{n}{b}", [128, 512], BF16))
        for n in ("gam", "Rf32", "bonus"):
            setattr(P, n, kb.sb(SC, f"{n}{b}", [128, 512], F32))
        P.AR = kb.sb(SC, f"AR{b}", [128, 4, 2, 128], BF16)
        for n in ("AT", "BT", "KT", "VT"):
            setattr(P, n, kb.sb(SC, f"{n}{b}", [128, 4, 128], BF16))
        PB.append(P)
    SL = []
    for s_ in range(NSLOT):
        L = NS()
        L.s = s_
        L.k = lambda n, s_=s_: f"{n}_s{s_}"
        L.smx = [kb.sb(SC, f"smx{s_}_{h}", [128, 3, 128], BF16) for h in range(2)]
        L.smo = [kb.sb(SC, f"smo{s_}_{h}", [128, 3, 128], BF16) for h in range(2)]
        L.smxT = kb.sb(SC, f"smxT{s_}", [128, 2, 3, 128], BF16)
        L.XX = [kb.sb(SC, f"XX{s_}_{i}", [128, 2, 2, 128], BF16) for i in range(2)]
        L.PPb = kb.sb(SC, f"PPb{s_}", [128, 2, 2, 128], BF16)
        L.Yb = kb.sb(SC, f"Yb{s_}", [128, 2, 2, 128], BF16)
        L.YBb = kb.sb(SC, f"YBb{s_}", [128, 2, 128], BF16)
        L.T128b = kb.sb(SC, f"T128b{s_}", [128, 2, 128], BF16)
        L.mkv = kb.sb(SC, f"mkv{s_}", [128, 128], BF16)
        L.WTc = kb.sb(SC, f"WTc{s_}", [128, 128], BF16)
        L.UlTc = kb.sb(SC, f"UlTc{s_}", [128, 128], BF16)
        L.P0b = kb.sb(SC, f"P0b{s_}", [128, 128], F32)
        L.Gt = kb.sb(SC, f"Gt{s_}", [128, 128], F32)
        L.ofs = kb.sb(SC, f"ofs{s_}", [128, 128], F32)
        L.ofl = kb.sb(SC, f"ofl{s_}", [128, 128], F32)
        L.osum = kb.sb(SC, f"osum{s_}", [128, 128], F32)
        L.onrm = kb.sb(SC, f"onrm{s_}", [128, 128], F32)
        L.st6 = kb.sb(SC, f"st6{s_}", [128, 2, 6], F32)
        L.mv = kb.sb(SC, f"mv{s_}", [128, 2, 2], F32)
        L.rsd = kb.sb(SC, f"rsd{s_}", [128, 2], F32)
        L.yl = kb.sb(SC, f"yl{s_}", [128, 128], F32)
        L.yl2 = kb.sb(SC, f"yl2{s_}", [128, 128], F32)
        SL.append(L)
    ST = [[kb.sb(SC, f"ST{p}_{i}", [128, 128], F32) for i in range(2)] for p in range(2)]
    shared_i = [0]

    def shared_bank():
        return kb.pfx(7)

    def prep_bank():
        return kb.pfx(6)

    coef = kb.sb(SC, "coef", [128, 24], F32)
    coefh = kb.sb(SC, "coefh", [128, 24], BF16)
    coefl = kb.sb(SC, "coefl", [128, 24], F32)
    mu12 = cp[:, CP["mu"]:CP["mu"] + 12]
    op("dve", lambda e: e.tensor_scalar(out=coef[:, 0:12], in0=mu12, scalar1=0.5, scalar2=None, op0=ALU.mult),
       reads=["cp"], writes=["coef"])
    op("dve", lambda e: e.tensor_scalar(out=coef[:, 12:24], in0=mu12, scalar1=-1.0, scalar2=1.0, op0=ALU.mult,
                                        op1=ALU.add), reads=["cp"], writes=["coef"])
    op("dve", lambda e: e.tensor_copy(out=coefh[:], in_=coef[:]), reads=["coef"], writes=["coefh"])
    op("dve", lambda e: e.tensor_copy(out=coefl[:], in_=coefh[:]), reads=["coefh"], writes=["coefl"])
    op("dve", lambda e: e.tensor_tensor(out=coefl[:], in0=coef[:], in1=coefl[:], op=ALU.subtract),
       reads=["coef", "coefl"], writes=["coefl"])
    DGh = kb.sb(SC, "DGh", [128, 24, 128], BF16)
    DGl = kb.sb(SC, "DGl", [128, 24, 128], BF16)
    for i_ in range(24):
        op("dve", lambda e: e.tensor_scalar(out=DGh[:, i_, :], in0=identf[:], scalar1=coef[:, i_:i_ + 1], scalar2=None,
                                            op0=ALU.mult), reads=["identf", "coef"], writes=["DGh"])
        op("dve", lambda e: e.tensor_scalar(out=DGl[:, i_, :], in0=identf[:], scalar1=coefl[:, i_:i_ + 1], scalar2=None,
                                            op0=ALU.mult), reads=["identf", "coefl"], writes=["DGl"])

    def prep_loads(P, hp, d, tau0, n, col0, lat0, bwd):
        k = P.k
        chs = slice(hp * 128, (hp + 1) * 128)
        kb.dma("sp", P.rt[:, :n + 2], S_r[chs, col0 - 1:col0 + n + 1], reads=["Sstreams", "Shalo"], writes=[k("rt")])
        kb.dma("sp", P.kt[:, :n + 2], S_k[chs, col0 - 1:col0 + n + 1], reads=["Sstreams", "Shalo"], writes=[k("kt")])
        kb.dma("sp", P.vt[:, :n + 2], S_v[chs, col0 - 1:col0 + n + 1], reads=["Sstreams", "Shalo"], writes=[k("vt")])
        kb.dma("sp", P.swt[:, :n], S_sw[d][chs, tau0:tau0 + n], reads=["Slora"], writes=[k("swt")])
        kb.dma("sp", P.atd[:, :n], S_a[d][chs, tau0:tau0 + n], reads=["Slora"], writes=[k("atd")])
        if bwd and lat0 is not None:
            kb.dma("sp", P.at0[:, :n], S_a[0][chs, tau0:tau0 + n], reads=["Slora"], writes=[k("at0")])

    def prep_gen(P, hp, d, tau0, n, col0, lat0, bwd):
        k = P.k
        chs = slice(hp * 128, (hp + 1) * 128)
        nch = n // 128
        fin = bwd and lat0 is not None
        if fin:
            kb.dma("sp", P.gt_[:, :n], S_g[chs, tau0:tau0 + n], reads=["Slora"], writes=[k("gt_")])
            kb.dma("sp", P.zct[:, :n], S_zc[chs, lat0:lat0 + n], reads=["Sstreams"], writes=[k("zct")])
        yield
        tA, tB = P.tA, P.tB
        for (xt_, xk, mi) in ((P.rt, "rt", 0), (P.kt, "kt", 1), (P.vt, "vt", 2)):
            ps, pk = prep_bank()
            ih, io = mi * 4 + hp, 12 + mi * 4 + hp
            seq = [(DGh, ih, 0), (DGh, io, 1), (DGh, ih, 2)]
            for qi, (dg, ii, sh) in enumerate(seq):
                mm(ps[:, :n], dg[:, ii, :], xt_[:, sh:sh + n], qi == 0, qi == len(seq) - 1, ["DGh", "DGl", k(xk)], [pk])
            if mi == 0:
                op("act", lambda e: e.activation(out=P.rp[:, :n], in_=ps[:, :n], func=AF.Copy), reads=[pk], writes=[k("rp")])
            elif mi == 1:
                op("act", lambda e: e.activation(out=P.kp[:, :n], in_=ps[:, :n], func=AF.Copy), reads=[pk], writes=[k("kp")])
                op("act", lambda e: e.activation(out=P.kkr[:, :n], in_=ps[:, :n], func=AF.Copy, scale=cpc("k_k", hp)),
                   reads=[pk, "cp"], writes=[k("kkr")])
                op("act", lambda e: e.activation(out=sqb[:, :n], in_=ps[:, :n], func=AF.Square, scale=cpc("k_k", hp)),
                   reads=[pk, "cp"], writes=["sqb"])
            else:
                op("act", lambda e: e.activation(out=P.vp[:, :n], in_=ps[:, :n], func=AF.Copy), reads=[pk], writes=[k("vp")])
                op("act", lambda e: e.activation(out=P.vb16[:, :n], in_=ps[:, :n], func=AF.Copy), reads=[pk],
                   writes=[k("vb16")])
            yield
        ps, pk = prep_bank()
        mm(ps[:, :n], blkb[:], sqb[:, :n], True, True, ["blkb", "sqb"], [pk])
        op("act", lambda e: e.activation(out=tA[:, :n], in_=ps[:, :n], func=AF.Ln, bias=epsb[:, 0:1], scale=1.0),
           reads=[pk, "epsb"], writes=[k("tA")])
        op("act", lambda e: e.activation(out=tA[:, :n], in_=tA[:, :n], func=AF.Exp, scale=-0.5), reads=[k("tA")],
           writes=[k("tA")])
        yield
        op("dve", lambda e: e.tensor_tensor(out=P.kkn[:, :n], in0=P.kkr[:, :n], in1=tA[:, :n], op=ALU.mult),
           reads=[k("kkr"), k("tA")], writes=[k("kkn")])
        yield
        op("act", lambda e: e.activation(out=tB[:, :n], in_=P.atd[:, :n], func=AF.Identity, scale=cpc("k_a", hp),
                                         bias=omka[:, hp:hp + 1]), reads=[k("atd"), "cp", "omka"], writes=[k("tB")])
        op("dve", lambda e: e.tensor_tensor(out=P.kd[:, :n], in0=tB[:, :n], in1=P.kp[:, :n], op=ALU.mult),
           reads=[k("tB"), k("kp")], writes=[k("kd")])
        op("dve", lambda e: e.tensor_tensor(out=P.bb[:, :n], in0=P.kkn[:, :n], in1=P.atd[:, :n], op=ALU.mult),
           reads=[k("kkn"), k("atd")], writes=[k("bb")])
        op("act", lambda e: e.activation(out=P.lw[:, :n], in_=P.swt[:, :n], func=AF.Copy, scale=-EXPM05),
           reads=[k("swt")], writes=[k("lw")])
        yield
        for j in range(nch):
            js = slice(j * 128, (j + 1) * 128)
            op("dve", lambda e: e.tensor_tensor_scan(out=P.cs[:, js], data0=ones128[:], data1=P.lw[:, js], initial=0.0,
                                                     op0=ALU.mult, op1=ALU.add), reads=["ones128", k("lw")],
               writes=[k("cs")])
        cse, csk = P.cs, k("cs")
        if bwd:
            op("dve", lambda e: e.tensor_tensor(out=tB[:, :n], in0=P.lw[:, :n], in1=P.cs[:, :n], op=ALU.subtract),
               reads=[k("lw"), k("cs")], writes=[k("tB")])
            for j in range(nch):
                js = slice(j * 128, (j + 1) * 128)
                op("dve", lambda e: e.tensor_scalar(out=P.csr[:, js], in0=tB[:, js],
                                                    scalar1=P.cs[:, j * 128 + 127:j * 128 + 128], scalar2=None,
                                                    op0=ALU.add), reads=[k("tB"), k("cs")], writes=[k("csr")])
            cse, csk = P.csr, k("csr")
        yield
        op("act", lambda e: e.activation(out=P.gam[:, :n], in_=cse[:, :n], func=AF.Exp), reads=[csk], writes=[k("gam")])
        op("act", lambda e: e.activation(out=P.ig[:, :n], in_=cse[:, :n], func=AF.Exp, scale=-1.0), reads=[csk],
           writes=[k("ig")])
        g3 = lambda t_: t_[:, :n].rearrange("p (j t) -> p j t", t=128)
        if not bwd:
            op("act", lambda e: e.activation(out=g3(P.gp)[:, :, 1:128], in_=g3(P.gam)[:, :, 0:127], func=AF.Copy),
               reads=[k("gam")], writes=[k("gp")])
            op("act", lambda e: e.activation(out=g3(P.gp)[:, :, 0:1], in_=g3(ones128)[:, :nch, 0:1] if False else
                                             ones128[:, 0:nch].unsqueeze(2), func=AF.Copy),
               reads=["ones128"], writes=[k("gp")])
        else:
            op("act", lambda e: e.activation(out=g3(P.gp)[:, :, 0:127], in_=g3(P.gam)[:, :, 1:128], func=AF.Copy),
               reads=[k("gam")], writes=[k("gp")])
            op("act", lambda e: e.activation(out=g3(P.gp)[:, :, 127:128], in_=ones128[:, 0:nch].unsqueeze(2),
                                             func=AF.Copy), reads=["ones128"], writes=[k("gp")])
        yield
        v3 = lambda t_: t_[:, :n].rearrange("p (j t) -> p j t", t=128)
        op("dve", lambda e: e.scalar_tensor_tensor(out=P.Af[:, :n], in0=P.kkn[:, :n], scalar=-1.0, in1=P.gp[:, :n],
                                                   op0=ALU.mult, op1=ALU.mult), reads=[k("kkn"), k("gp")], writes=[k("Af")])
        op("act", lambda e: e.activation(out=P.AR[:, :nch, 0, :], in_=v3(P.Af), func=AF.Copy), reads=[k("Af")],
           writes=[k("AR")])
        op("dve", lambda e: e.tensor_tensor(out=P.Rf32[:, :n], in0=P.rp[:, :n], in1=P.gam[:, :n], op=ALU.mult),
           reads=[k("rp"), k("gam")], writes=[k("Rf32")])
        op("act", lambda e: e.activation(out=P.AR[:, :nch, 1, :], in_=v3(P.Rf32), func=AF.Copy), reads=[k("Rf32")],
           writes=[k("AR")])
        yield
        op("dve", lambda e: e.tensor_tensor(out=P.Bf[:, :n], in0=P.bb[:, :n], in1=P.ig[:, :n], op=ALU.mult),
           reads=[k("bb"), k("ig")], writes=[k("Bf")])
        op("dve", lambda e: e.tensor_tensor(out=P.Kf[:, :n], in0=P.kd[:, :n], in1=P.ig[:, :n], op=ALU.mult),
           reads=[k("kd"), k("ig")], writes=[k("Kf")])
        yield
        for (src, sk, dstT, dk) in ((P.Af, "Af", P.AT, "AT"), (P.Bf, "Bf", P.BT, "BT"), (P.Kf, "Kf", P.KT, "KT"),
                                    (P.vb16, "vb16", P.VT, "VT")):
            pf_, pk = prep_bank()
            pt = pf_[:].bitcast(BF16)
            for j in range(nch):
                op("pe", lambda e: e.transpose(out=pt[:, j * 128:(j + 1) * 128], in_=src[:, j * 128:(j + 1) * 128],
                                               identity=identb[:]), reads=[k(sk), "identb"], writes=[pk])
            op("act", lambda e: e.activation(out=dstT[:, :nch, :].rearrange("p j t -> p (j t)"), in_=pt[:, :n],
                                             func=AF.Copy), reads=[pk], writes=[k(dk)])
            yield
        if fin:
            op("dve", lambda e: e.tensor_tensor(out=tB[:, :n], in0=P.at0[:, :n], in1=P.atd[:, :n], op=ALU.add),
               reads=[k("at0"), k("atd")], writes=[k("tB")])
            op("dve", lambda e: e.tensor_scalar(out=tB[:, :n], in0=tB[:, :n], scalar1=-2.0, scalar2=cpc("k_a", hp),
                                                op0=ALU.add, op1=ALU.mult), reads=[k("tB"), "cp"], writes=[k("tB")])
            op("dve", lambda e: e.scalar_tensor_tensor(out=P.ksum[:, :n], in0=tB[:, :n], scalar=2.0, in1=P.kp[:, :n],
                                                       op0=ALU.add, op1=ALU.mult), reads=[k("tB"), k("kp")],
               writes=[k("ksum")])
            op("dve", lambda e: e.scalar_tensor_tensor(out=sqb[:, :n], in0=P.rp[:, :n], scalar=cpc("r_k", hp),
                                                       in1=P.ksum[:, :n], op0=ALU.mult, op1=ALU.mult),
               reads=[k("rp"), "cp", k("ksum")], writes=["sqb"])
            ps, pk = prep_bank()
            mm(ps[:, :n], blkb[:], sqb[:, :n], True, True, ["blkb", "sqb"], [pk])
            op("dve", lambda e: e.tensor_tensor(out=P.bonus[:, :n], in0=ps[:, :n], in1=P.vp[:, :n], op=ALU.mult),
               reads=[pk, k("vp")], writes=[k("bonus")])
            yield

    chain_turn = [0]

    def chunk_gen(L, P, inst, hp, d, j, lat_row, bwd, fin, want_out, stp, first_of_pass, sc_state):
        k = L.k
        pk_ = P.k
        js = slice(j * 128, (j + 1) * 128)
        bank, bk = kb.pfx(L.s)
        if fin:
            kb.dma("sp", L.ofl[:], S_of[lat_row:lat_row + 128, hp * 128:(hp + 1) * 128], reads=["Sof"], writes=[k("ofl")])
        smx, smo, smxT, PPb = L.smx, L.smo, L.smxT, L.PPb
        flat = lambda t_: t_[:].rearrange("p h a t -> p (h a t)")
        for h in range(2):
            hs = slice(64 * h, 64 * h + 64)
            arh = P.AR[hs, j, :, :].rearrange("p a t -> p (a t)")
            mm(bank[:, 0:256], P.Bf[hs, js], arh, True, False, [pk_("Bf"), pk_("AR")], [bk])
            mm(bank[:, 256:512], P.Kf[hs, js], arh, False, True, [pk_("Kf"), pk_("AR")], [bk])
            op("dve", lambda e: e.tensor_tensor(out=smx[h][:], in0=bank[:, 0:128].unsqueeze(1).broadcast_to([128, 3, 128]),
                                                in1=masks[:, d, 0:3, :], op=ALU.mult), reads=[bk, "masks"],
               writes=[k(f"smx{h}")])
            op("dve", lambda e: e.tensor_tensor(out=smo[h][:], in0=bank[:, 128:512].rearrange("p (m t) -> p m t", m=3),
                                                in1=masks[:, d, 3:6, :], op=ALU.mult), reads=[bk, "masks"],
               writes=[k(f"smo{h}")])
            yield
        pt = bank[:].bitcast(BF16)
        for h in range(2):
            for m in range(3):
                c0 = (h * 3 + m) * 128
                op("pe", lambda e: e.transpose(out=pt[:, c0:c0 + 128], in_=smx[h][:, m, :], identity=identb[:]),
                   reads=[k(f"smx{h}"), "identb"], writes=[bk])
        op("act", lambda e: e.activation(out=smxT[:].rearrange("p h m t -> p (h m t)"), in_=pt[:, 0:768], func=AF.Copy),
           reads=[bk], writes=[k("smxT")])
        yield
        Xc = [smx[0][:, 0, :], smx[1][:, 0, :]]
        XTc = [smxT[:, 0, 0, :], smxT[:, 1, 0, :]]
        xkeys = [k("smx0"), k("smx1"), k("smxT")]

        evi = [L.s]

        def evac_copy(dst_ap, src_ap, dkey):
            if L.s >= 2:
                op("act", lambda e: e.activation(out=dst_ap, in_=src_ap, func=AF.Copy), reads=[bk], writes=[dkey])
            else:
                op("dve", lambda e: e.tensor_copy(out=dst_ap, in_=src_ap), reads=[bk], writes=[dkey])

        def pp_seed(h, first):
            mm(bank[:, h * 256:(h + 1) * 256], identb[:], PPb[:, h, :, :].rearrange("p a t -> p (a t)"), first, False,
               ["identb", k("PPb")], [bk])

        def pp_update():
            evac_copy(flat(PPb), bank[:], k("PPb"))

        def pp_seed_all():
            mm(bank[:], identb[:], flat(PPb), True, False, ["identb", k("PPb")], [bk])

        for i in range(1, 5):
            last = i == 4
            xx = L.XX[i % 2]
            xk = k(f"XX{i % 2}")
            if not last:
                for h in range(2):
                    mm(bank[:, h * 256:h * 256 + 128], XTc[h], Xc[h], h == 0, False, xkeys, [bk])
                    mm(bank[:, h * 256 + 128:(h + 1) * 256], Xc[h], XTc[h], False, h == 1, xkeys, [bk])
                evac_copy(flat(xx), bank[:], xk)
            else:
                for h in range(2):
                    mm(bank[:, h * 256:h * 256 + 128], XTc[h], Xc[h], h == 0, h == 1, xkeys, [bk])
                evac_copy(xx[:, :, 0, :], bank[:].rearrange("p (h a t) -> p h a t", h=2, a=2)[:, :, 0, :], xk)
            Xc = [xx[:, 0, 0, :], xx[:, 1, 0, :]]
            XTc = [xx[:, 0, 1, :], xx[:, 1, 1, :]]
            xkeys = [xk]
            yield
            if i == 1:
                for h in range(2):
                    lo, hi = slice(h * 256, h * 256 + 128), slice(h * 256 + 128, (h + 1) * 256)
                    m0, m0t = smx[h][:, 0, :], smxT[:, h, 0, :]
                    kk_ = [k(f"smx{h}"), k("smxT"), xk, "identb"]
                    mm(bank[:, lo], identb[:], identb[:], h == 0, False, kk_, [bk])
                    mm(bank[:, lo], identb[:], m0, False, False, kk_, [bk])
                    mm(bank[:, lo], identb[:], Xc[h], False, False, kk_, [bk])
                    mm(bank[:, lo], m0t, Xc[h], False, False, kk_, [bk])
                    mm(bank[:, hi], identb[:], identb[:], False, False, kk_, [bk])
                    mm(bank[:, hi], identb[:], m0t, False, False, kk_, [bk])
                    mm(bank[:, hi], Xc[h], identb[:], False, False, kk_, [bk])
                    mm(bank[:, hi], Xc[h], m0t, False, h == 1, kk_, [bk])
            else:
                pp_seed_all()
                for h in range(2):
                    mm(bank[:, h * 256:h * 256 + 128], PPb[:, h, 1, :], Xc[h], False, False, [k("PPb"), xk], [bk])
                    mm(bank[:, h * 256 + 128:(h + 1) * 256], Xc[h], PPb[:, h, 1, :], False, h == 1, [k("PPb"), xk], [bk])
            pp_update()
            yield
        for h in range(2):
            mm(bank[:, h * 256:h * 256 + 128], smxT[:, h, 1, :], PPb[:, h, 0, :], h == 0, False, [k("smxT"), k("PPb")], [bk])
            mm(bank[:, h * 256 + 128:(h + 1) * 256], smx[h][:, 1, :], PPb[:, h, 1, :], False, h == 1,
               [k(f"smx{h}"), k("PPb")], [bk])
        evac_copy(flat(L.Yb), bank[:], k("Yb"))
        yield
        pp_seed_all()
        for h in range(2):
            mm(bank[:, h * 256:h * 256 + 128], PPb[:, h, 1, :], L.Yb[:, h, 0, :], False, False, [k("PPb"), k("Yb")], [bk])
            mm(bank[:, h * 256 + 128:(h + 1) * 256], PPb[:, h, 0, :], L.Yb[:, h, 1, :], False, h == 1,
               [k("PPb"), k("Yb")], [bk])
        pp_update()
        yield
        for h in range(2):
            mm(bank[:, h * 128:(h + 1) * 128], smxT[:, h, 2, :], PPb[:, h, 0, :], h == 0, False, [k("smxT"), k("PPb")], [bk])
        for h in range(2):
            hs = slice(64 * h, 64 * h + 64)
            mm(bank[:, 256 + 64 * h:256 + 64 * h + 64], smo[h][:, 1, :], P.VT[:, j, hs], False, h == 1,
               [k(f"smo{h}"), pk_("VT")], [bk])
        evac_copy(L.YBb[:].rearrange("p h t -> p (h t)"), bank[:, 0:256], k("YBb"))
        evac_copy(L.mkv[:], bank[:, 256:384], k("mkv"))
        yield
        for h in range(2):
            mm(bank[:, h * 128:(h + 1) * 128], identb[:], PPb[:, h, 0, :], h == 0, False, ["identb", k("PPb")], [bk])
            mm(bank[:, h * 128:(h + 1) * 128], PPb[:, h, 1, :], L.YBb[:, h, :], False, h == 1, [k("PPb"), k("YBb")], [bk])
        evac_copy(L.T128b[:].rearrange("p h t -> p (h t)"), bank[:, 0:256], k("T128b"))
        yield
        for h in range(2):
            hs = slice(64 * h, 64 * h + 64)
            mm(bank[:, hs], L.T128b[:, h, :], P.AT[:, j, hs], h == 0, False, [k("T128b"), pk_("AT")], [bk])
            mm(bank[:, 128 + 64 * h:128 + 64 * h + 64], L.T128b[:, h, :], L.mkv[:, hs], False, h == 1,
               [k("T128b"), k("mkv")], [bk])
        op("act", lambda e: e.activation(out=L.WTc[:], in_=bank[:, 0:128], func=AF.Copy), reads=[bk], writes=[k("WTc")])
        op("act", lambda e: e.activation(out=L.UlTc[:], in_=bank[:, 128:256], func=AF.Copy), reads=[bk],
           writes=[k("UlTc")])
        yield
        mm(bank[:, 0:128], L.WTc[:], P.BT[:, j, :], True, not want_out, [k("WTc"), pk_("BT")], [bk])
        if want_out:
            for h in range(2):
                mm(bank[64 * h:64 * h + 64, 128:256], L.WTc[:, 64 * h:64 * h + 64], smo[h][:, 0, :], False, True,
                   [k("WTc"), k(f"smo{h}")], [bk])
        op("dve", lambda e: e.tensor_tensor(out=L.P0b[:], in0=bank[:, 0:128], in1=identf[:], op=ALU.add),
           reads=[bk, "identf"], writes=[k("P0b")])
        if want_out:
            op("dve", lambda e: e.tensor_tensor(out=L.Gt[:], in0=bank[:, 128:256], in1=P.Rf32[:, js], op=ALU.add),
               reads=[bk, pk_("Rf32")], writes=[k("Gt")])
        yield
        while chain_turn[0] != inst:
            yield
        if first_of_pass:
            op("dve", lambda e: e.memset(ST[stp][0][:], 0.0), writes=[f"ST{stp}_0"])
            sc_state[0] = 0
        cur = sc_state[0]
        stc, stck = ST[stp][cur], f"ST{stp}_{cur}"
        stn, stnk = ST[stp][1 - cur], f"ST{stp}_{1 - cur}"
        if want_out:
            po, pok = shared_bank()
            mm(po[:, 0:128], L.Gt[:], stc[:], True, False, [k("Gt"), stck], [pok])
            for h in range(2):
                hs = slice(64 * h, 64 * h + 64)
                mm(po[:, hs], smo[h][:, 0, :], L.UlTc[:, hs], False, False, [k(f"smo{h}"), k("UlTc")], [pok])
                mm(po[:, hs], smo[h][:, 2, :], P.VT[:, j, hs], False, h == 1, [k(f"smo{h}"), pk_("VT")], [pok])
        mm(bank[:, 0:128], P.BT[:, j, :], L.UlTc[:], True, False, [pk_("BT"), k("UlTc")], [bk])
        mm(bank[:, 0:128], P.KT[:, j, :], P.VT[:, j, :], False, False, [pk_("KT"), pk_("VT")], [bk])
        mm(bank[:, 0:128], L.P0b[:], stc[:], False, True, [k("P0b"), stck], [bk])
        gcol = j * 128 if bwd else j * 128 + 127
        op("dve", lambda e: e.scalar_tensor_tensor(out=stn[:], in0=bank[:, 0:128], scalar=P.gam[:, gcol:gcol + 1],
                                                   in1=blk[:], op0=ALU.mult, op1=ALU.mult),
           reads=[bk, pk_("gam"), "blk"], writes=[stnk])
        sc_state[0] = 1 - cur
        chain_turn[0] += 1
        if not want_out:
            return
        if not fin:
            op("act", lambda e: e.activation(out=L.ofs[:], in_=po[:, 0:128], func=AF.Copy), reads=[pok],
               writes=[k("ofs")])
            kb.dma("sp", S_of[lat_row:lat_row + 128, hp * 128:(hp + 1) * 128], L.ofs[:], reads=[k("ofs")],
                   writes=["Sof"], key="ofst")
            return
        op("dve", lambda e: e.tensor_tensor(out=L.osum[:], in0=po[:, 0:128], in1=L.ofl[:], op=ALU.add),
           reads=[pok, k("ofl")], writes=[k("osum")])
        yield
        for h in range(2):
            hs = slice(64 * h, 64 * h + 64)
            op("dve", lambda e: e.bn_stats(out=L.st6[:, h, :], in_=L.osum[:, hs]), reads=[k("osum")], writes=[k("st6")])
            op("dve", lambda e: e.bn_aggr(out=L.mv[:, h, :], in_=L.st6[:, h, :]), reads=[k("st6")], writes=[k("mv")])
        op("dve", lambda e: e.tensor_scalar(out=L.rsd[:], in0=L.mv[:, :, 1], scalar1=64e-5, scalar2=None, op0=ALU.add),
           reads=[k("mv")], writes=[k("rsd")])
        op("act", lambda e: e.activation(out=L.rsd[:], in_=L.rsd[:], func=AF.Sqrt), reads=[k("rsd")], writes=[k("rsd")])
        yield
        op("dve", lambda e: e.reciprocal(out=L.rsd[:], in_=L.rsd[:]), reads=[k("rsd")], writes=[k("rsd")])
        for h in range(2):
            hs = slice(64 * h, 64 * h + 64)
            op("dve", lambda e: e.tensor_scalar(out=L.onrm[:, hs], in0=L.osum[:, hs], scalar1=L.mv[:, h, 0:1],
                                                scalar2=L.rsd[:, h:h + 1], op0=ALU.subtract, op1=ALU.mult),
               reads=[k("osum"), k("mv"), k("rsd")], writes=[k("onrm")])
        yield
        op("pe", lambda e: e.transpose(out=bank[:, 0:128], in_=L.onrm[:], identity=identf[:]),
           reads=[k("onrm"), "identf"], writes=[bk])
        op("dve", lambda e: e.tensor_scalar(out=L.yl[:], in0=bank[:, 0:128], scalar1=cpc("lnx_g", hp),
                                            scalar2=cpc("lnx_b", hp), op0=ALU.mult, op1=ALU.add),
           reads=[bk, "cp"], writes=[k("yl")])
        yield
        op("dve", lambda e: e.tensor_tensor(out=L.yl2[:], in0=L.yl[:], in1=P.bonus[:, js], op=ALU.add),
           reads=[k("yl"), pk_("bonus")], writes=[k("yl2")])
        op("dve", lambda e: e.tensor_tensor(out=L.yl[:], in0=L.yl2[:], in1=P.gt_[:, js], op=ALU.mult),
           reads=[k("yl2"), pk_("gt_")], writes=[k("yl")])
        op("dve", lambda e: e.tensor_tensor(out=P.yo[:, js], in0=L.yl[:], in1=P.zct[:, js], op=ALU.mult),
           reads=[k("yl"), pk_("zct")], writes=[pk_("yo")])

    sups = []
    npass = 0
    for hp in range(4):
        for d in range(2):
            bwd = d == 1
            lst = [(0, CTX, CTX0, None)] + [(CTX + it * 512, 512, LAT0 + it * 512, it * 512) for it in range(NT)]
            if bwd:
                lst = [lst[0]] + lst[1:][::-1]
            for qi, (tau0, n, col0, lat0) in enumerate(lst):
                sups.append((hp, d, tau0, n, col0, lat0, bwd, qi == 0, npass % 2))
            npass += 1
    lim = [int(x[8:]) for x in dump if x.startswith("SCANLIM_")]
    if lim:
        sups = sups[:lim[0]]
    nsup = len(sups)
    pass_state = {}
    prep_done = [False] * nsup
    prep_started = [False] * nsup
    chunks_left = [0] * nsup
    inst_list = []
    for q, (hp, d, tau0, n, col0, lat0, bwd, first, stp) in enumerate(sups):
        nch = n // 128
        order = list(range(nch))[::-1] if bwd else list(range(nch))
        chunks_left[q] = nch
        for oi, j in enumerate(order):
            inst_list.append((q, j, first and oi == 0))
    loads_issued = [False] * nsup

    def issue_loads(q):
        if q < nsup and not loads_issued[q]:
            (hp, d, tau0, n, col0, lat0, bwd, first, stp) = sups[q]
            prep_loads(PB[q % 2], hp, d, tau0, n, col0, lat0, bwd)
            loads_issued[q] = True

    active = []
    free_slots = list(range(NSLOT))
    next_inst = 0
    sc_states = {}
    while next_inst < len(inst_list) or active:
        for q in range(nsup):
            if prep_started[q]:
                continue
            if q >= 2 and chunks_left[q - 2] > 0:
                break
            if q >= 1 and not prep_done[q - 1]:
                break
            (hp, d, tau0, n, col0, lat0, bwd, first, stp) = sups[q]
            issue_loads(q)
            active.append([prep_gen(PB[q % 2], hp, d, tau0, n, col0, lat0, bwd), "prep", q, None])
            prep_started[q] = True
            break
        if next_inst < len(inst_list) and free_slots:
            q, j, first = inst_list[next_inst]
            if prep_done[q]:
                (hp, d, tau0, n, col0, lat0, bwd, firstq, stp) = sups[q]
                slot = free_slots.pop(0)
                fin = bwd and lat0 is not None
                lat_row = None if lat0 is None else lat0 + j * 128
                key = (hp, d)
                if key not in sc_states:
                    sc_states[key] = [0]
                g = chunk_gen(SL[slot], PB[q % 2], next_inst, hp, d, j, lat_row, bwd, fin, lat0 is not None, stp, first,
                              sc_states[key])
                active.append([g, "chunk", q, slot])
                next_inst += 1
        still = []
        for item in active:
            g, kind, q, slot = item
            try:
                next(g)
                still.append(item)
            except StopIteration:
                if kind == "prep":
                    prep_done[q] = True
                    issue_loads(q + 1)
                else:
                    free_slots.append(slot)
                    chunks_left[q] -= 1
                    if chunks_left[q] == 0:
                        (hp, d, tau0, n, col0, lat0, bwd, firstq, stp) = sups[q]
                        if bwd and lat0 is not None:
                            P = PB[q % 2]
                            kb.dma("sp", S_y[hp * 128:(hp + 1) * 128, lat0:lat0 + 512], P.yo[:], reads=[P.k("yo")],
                                   writes=["Sy"], key="yost")
        active = still


def prep_inputs(inp, SEQ):
    f = lambda a: np.ascontiguousarray(np.asarray(a, np.float32))
    shared = {
        "ada_w_e": f(inp["ada_w_e"][0]), "ada_w_o": f(inp["ada_w_o"][0]),
        "ada_b_e": f(inp["ada_b_e"][0][None]), "ada_b_o": f(inp["ada_b_o"][0][None]),
        "norm_e": f(inp["norm_e"][0][None]), "norm_o": f(inp["norm_o"][0][None]),
        "in_e": f(inp["in_e"][0]), "out_e": f(inp["out_e"][0]), "in_o": f(inp["in_o"][0]), "out_o": f(inp["out_o"][0]),
        "pool_w": f(np.asarray(inp["pool_w"][0]).transpose(1, 0, 2).reshape(128, 512)),
        "final_g": f(np.asarray(inp["final_g"])[None]),
    }
    cols = [_cols(inp["pool_scale"][0][None]), _cols(inp["sconv_w"][0]), _cols(inp["rwkv_mu"][0]),
            _cols(inp["k_k"][0][None]), _cols(inp["k_a"][0][None]), _cols(np.asarray(inp["r_k"][0]).reshape(1, 512)),
            _cols(inp["lnx_g"][0][None]), _cols(inp["lnx_b"][0][None]), _cols(inp["conf_dw_b"][0][None]),
            _cols(inp["conf_ln_g"][0][None]), _cols(inp["conf_ln_b"][0][None]), _cols(inp["w0"][0]),
            _cols(inp["a0"][0]), _cols(inp["conf_dw_w"][0])]
    shared["cp"] = np.ascontiguousarray(np.concatenate(cols, axis=1))
    assert shared["cp"].shape == (128, NCP)

    def l1(w):
        w = np.asarray(w, np.float32).transpose(1, 0, 2).reshape(4, 128, -1).transpose(1, 0, 2)
        return np.ascontiguousarray(w.reshape(128, -1))

    shared["w1l"] = l1(inp["w1"][0])
    shared["a1l"] = l1(inp["a1"][0])
    shared["g1l"] = np.ascontiguousarray(
        np.asarray(inp["g1"][0], np.float32).reshape(4, 128, 96).transpose(1, 0, 2).reshape(128, 384))
    shared["w2l"] = f(np.asarray(inp["w2"][0]).reshape(64, 512))
    shared["a2l"] = f(np.asarray(inp["a2"][0]).reshape(64, 512))
    shared["g2l"] = f(inp["g2"][0])
    shared.update(host_consts())
    maps = []
    for b in range(2):
        m = dict(shared)
        m["x"] = f(inp["x"][b][:SEQ])
        m["ctx"] = f(inp["ctx"][b])
        m["ccol"] = np.ascontiguousarray(np.concatenate(
            [np.asarray(inp["c"][b], np.float32).reshape(8, 128).T, np.asarray(inp["c_ctx"], np.float32).reshape(8, 128).T],
            axis=1))
        maps.append(m)
    return maps


_NC_CACHE = {}


def kernel(**inputs):
    SEQ = 8192
    if SEQ not in _NC_CACHE:
        _NC_CACHE[SEQ] = build(SEQ)
    nc = _NC_CACHE[SEQ]
    maps = prep_inputs(inputs, SEQ)
    in_maps = [maps[c // 4] for c in range(8)]
    res = run_bass_kernel_spmd(nc, in_maps, core_ids=list(range(8)))
    return np.stack([res.results[0]["out"], res.results[4]["out"]], axis=0).astype(np.float32)
```

```python
import numpy as np
from contextlib import ExitStack
import concourse.bass as bass
import concourse.mybir as mybir
from concourse.bass_utils import run_bass_kernel_spmd

F32 = mybir.dt.float32
BF16 = mybir.dt.bfloat16
ALU = mybir.AluOpType
AF = mybir.ActivationFunctionType

D = 1024
DB = 512
CTX = 256
CH = 128
CTX0 = 2
LAT0 = CTX0 + CTX + 2
GPAD = 960
EXPM05 = float(np.exp(-0.5))


class KB:
    def __init__(self, nc, es):
        self.nc = nc
        self.es = es
        self.eng = {"pe": nc.tensor, "act": nc.scalar, "dve": nc.vector, "pool": nc.gpsimd, "sp": nc.sync}
        self.sem = {}
        self.cnt = {}
        for e in self.eng:
            self.sem[e] = es.enter_context(nc.semaphore("sem_" + e))
            self.cnt[e] = 0
        self.dsem = {}
        self.dcnt = {}
        self.waited = {e: {} for e in self.eng}
        self.lastw = {}
        self.reads = {}
        self.psf = [es.enter_context(nc.psum_tensor(f"psf{i}", [128, 512], F32)) for i in range(8)]
        self.psf_i = 0
        self.psb_i = 0
        self.ndsem = 0

    def sb(self, es, name, shape, dt):
        self.nsb = getattr(self, "nsb", 0) + 1
        return es.enter_context(self.nc.sbuf_tensor("sb%d_%s" % (self.nsb, name), shape, dt))

    def pf(self):
        i = self.psf_i % 6
        self.psf_i += 1
        return self.psf[i], f"psf{i}"

    def pfx(self, i):
        return self.psf[i], f"psf{i}"

    def pb(self):
        i = 6 + self.psb_i % 2
        self.psb_i += 1
        return self.psf[i][:].bitcast(BF16), f"psf{i}"

    def _wait(self, e, ev):
        if ev is None:
            return
        semkey, val = ev
        if self.waited[e].get(semkey, 0) >= val:
            return
        sem = self.sem[semkey] if semkey in self.sem else self.dsem[semkey]
        self.eng[e].wait_ge(sem, val)
        self.waited[e][semkey] = val

    def _deps(self, e, reads, writes, sync_same):
        for k in reads:
            for sk, v in self.lastw.get(k, {}).items():
                if sync_same or sk != e:
                    self._wait(e, (sk, v))
            if k.startswith("ps"):
                for sk, v in self.reads.get(k, {}).items():
                    if sk != e:
                        self._wait(e, (sk, v))
        for k in writes:
            for sk, v in self.lastw.get(k, {}).items():
                if sync_same or sk != e:
                    self._wait(e, (sk, v))
            for sk, v in self.reads.get(k, {}).items():
                if sync_same or sk != e:
                    self._wait(e, (sk, v))

    def _record(self, ev, reads, writes):
        sk, v = ev
        for k in writes:
            self.lastw.setdefault(k, {})[sk] = v
        for k in reads:
            self.reads.setdefault(k, {})[sk] = v

    budget = None
    nops = 0

    def op(self, e, fn, reads=(), writes=()):
        self.nops += 1
        if self.budget is not None and self.nops > self.budget:
            return
        self._deps(e, reads, writes, e != "pe")
        inst = fn(self.eng[e])
        self.cnt[e] += 1
        inst.then_inc(self.sem[e], 1)
        self._record((e, self.cnt[e]), reads, writes)

    def dma(self, q, out, in_, reads=(), writes=(), key=None):
        self.nops += 1
        if self.budget is not None and self.nops > self.budget and not str(key).startswith("dbg"):
            return
        key = key or (list(writes) + list(reads))[0]
        skey = "d_" + str(key)
        if skey not in self.dsem:
            self.dsem[skey] = self.es.enter_context(self.nc.semaphore("ds%d" % self.ndsem))
            self.ndsem += 1
            self.dcnt[skey] = 0
        self._deps(q, reads, writes, True)
        self.dcnt[skey] += 16
        self.eng[q].dma_start(out=out, in_=in_).then_inc(self.dsem[skey], 16)
        self._record((skey, self.dcnt[skey]), reads, writes)

    def barrier(self):
        evs = [(e, self.cnt[e]) for e in self.eng if self.cnt[e] > 0] + [(k, v) for k, v in self.dcnt.items() if v > 0]
        for e in self.eng:
            for ev in evs:
                if ev[0] != e:
                    self._wait(e, ev)

    def finish(self, keys):
        for k in keys:
            for sk, v in self.lastw.get(k, {}).items():
                self._wait("sp", (sk, v))
            for sk, v in self.reads.get(k, {}).items():
                self._wait("sp", (sk, v))


CP = {}
_off = 0
for _n, _w in [("pool_scale", 4), ("sconv_w", 12), ("mu", 24), ("k_k", 4), ("k_a", 4), ("r_k", 4),
               ("lnx_g", 4), ("lnx_b", 4), ("conf_dw_b", 4), ("conf_ln_g", 4), ("conf_ln_b", 4),
               ("w0", 8), ("a0", 8), ("conf_dw_w", 124)]:
    CP[_n] = _off
    _off += _w
NCP = _off


def _cols(p):
    p = np.asarray(p, np.float32).reshape(-1, 4, 128)
    return np.ascontiguousarray(p.transpose(2, 0, 1).reshape(128, -1))


def host_consts():
    s = np.arange(128)[:, None]
    t = np.arange(128)[None, :]
    b32 = (s // 32) == (t // 32)
    b64 = (s // 64) == (t // 64)
    ml = []
    for st, inc in (((s < t), (s <= t)), ((s > t), (s >= t))):
        ml += [st & b32, st & b64 & ~b32, st & ~b64, inc, st, inc]
    masks = np.concatenate(ml, axis=1).astype(np.float32)
    blk = ((s // 64) == (t // 64)).astype(np.float32)

    def invc(l):
        tt = np.arange(l)
        rows = []
        for w in (2, 4, 8, 16):
            lo = np.clip(tt - w // 2, 0, l)
            hi = np.clip(tt - w // 2 + w, 0, l)
            rows.append(1.0 / (hi - lo).astype(np.float32))
        r = np.concatenate(rows).astype(np.float32)
        return np.ascontiguousarray(np.broadcast_to(r[None, :], (128, r.size)))

    return {
        "ident": np.eye(128, dtype=np.float32),
        "masks": masks,
        "blk": blk,
        "invc_x": invc(64),
        "invc_c": invc(256),
    }


def build(SEQ, dump=()):
    assert SEQ % 512 == 0
    NT = SEQ // 512
    NTAU = CTX + SEQ
    NCOL = LAT0 + SEQ + 2
    nc = bass.Bass("TRN2", target_bir_lowering=False)

    def din(name, shape, dt=F32):
        return nc.dram_tensor(name, shape, dt, kind="ExternalInput").ap()

    def dscr(name, shape, dt):
        kind = "ExternalOutput" if name in dump else "Internal"
        return nc.dram_tensor(name, shape, dt, kind=kind).ap()

    x_in = din("x", [SEQ, D])
    ctx_in = din("ctx", [CTX, D])
    ccol_in = din("ccol", [128, 16])
    ada_w = [din("ada_w_e", [D, 3 * D]), din("ada_w_o", [D, 3 * D])]
    ada_b = [din("ada_b_e", [1, 3 * D]), din("ada_b_o", [1, 3 * D])]
    norm_g = [din("norm_e", [1, D]), din("norm_o", [1, D])]
    in_e = din("in_e", [D, 3072])
    out_e = din("out_e", [D, D])
    in_o = din("in_o", [D, 4096])
    out_o = din("out_o", [D, D])
    pool_w = din("pool_w", [128, 512])
    cp_in = din("cp", [128, NCP])
    w1l = din("w1l", [128, 256])
    a1l = din("a1l", [128, 256])
    g1l = din("g1l", [128, 384])
    w2l = din("w2l", [64, 512])
    a2l = din("a2l", [64, 512])
    g2l = din("g2l", [96, 512])
    final_g = din("final_g", [1, D])
    ident_in = din("ident", [128, 128])
    masks_in = din("masks", [128, 1536])
    blk_in = din("blk", [128, 128])
    invcx_in = din("invc_x", [128, 256])
    invcc_in = din("invc_c", [128, 1024])
    out = nc.dram_tensor("out", [SEQ, D], F32, kind="ExternalOutput").ap()

    X1 = dscr("X1", [SEQ, D], F32)
    XC1 = dscr("XC1", [CTX, D], F32)
    S_u = dscr("S_u", [DB, NCOL], BF16)
    S_r = dscr("S_r", [DB, NCOL], BF16)
    S_k = dscr("S_k", [DB, NCOL], BF16)
    S_v = dscr("S_v", [DB, NCOL], BF16)
    S_zc = dscr("S_zc", [DB, SEQ], BF16)
    S_zd = dscr("S_zd", [DB, SEQ], BF16)
    S_glu = dscr("S_glu", [DB, SEQ + 2 * GPAD], BF16)
    S_sw = [dscr("S_sw0", [DB, NTAU], F32), dscr("S_sw1", [DB, NTAU], F32)]
    S_a = [dscr("S_a0", [DB, NTAU], BF16), dscr("S_a1", [DB, NTAU], BF16)]
    S_g = dscr("S_g", [DB, NTAU], BF16)
    S_of = dscr("S_of", [SEQ, DB], F32)
    S_y = dscr("S_y", [D, SEQ], BF16)

    with ExitStack() as es:
        kb = KB(nc, es)
        op = kb.op

        identf = kb.sb(es, "identf", [128, 128], F32)
        identb = kb.sb(es, "identb", [128, 128], BF16)
        blk = kb.sb(es, "blk", [128, 128], F32)
        onesf = kb.sb(es, "onesf", [128, 128], F32)
        onesb = kb.sb(es, "onesb", [128, 128], BF16)
        cp = kb.sb(es, "cp", [128, NCP], F32)
        ccol = kb.sb(es, "ccol", [128, 16], F32)
        zb = kb.sb(es, "zb", [128, GPAD], BF16)
        kb.dma("sp", identf[:], ident_in, writes=["identf"])
        kb.dma("sp", blk[:], blk_in, writes=["blk"])
        kb.dma("sp", cp[:], cp_in, writes=["cp"])
        kb.dma("sp", ccol[:], ccol_in, writes=["ccol"])
        op("dve", lambda e: e.tensor_copy(out=identb[:], in_=identf[:]), reads=["identf"], writes=["identb"])
        op("dve", lambda e: e.memset(onesf[:], 1.0), writes=["onesf"])
        op("dve", lambda e: e.memset(onesb[:], 1.0), writes=["onesb"])
        op("dve", lambda e: e.memset(zb[:], 0.0), writes=["zb"])
        for S in (S_u, S_r, S_k, S_v):
            for c0 in (0, LAT0 - 2, NCOL - 2):
                kb.dma("pool", S[:, c0:c0 + 2].rearrange("(c p) n -> p c n", p=128),
                       zb[:, 0:8].rearrange("p (c n) -> p c n", c=4), reads=["zb"], writes=["Shalo"], key="Shalo")
        for c0 in (0, GPAD + SEQ):
            for c in range(4):
                kb.dma("pool", S_glu[c * 128:(c + 1) * 128, c0:c0 + GPAD], zb[:, :], reads=["zb"],
                       writes=["Sgluhalo"], key="Shalo")

        def cpc(name, i=0):
            o = CP[name] + i
            return cp[:, o:o + 1]

        def load_cast(es_, name, src, rows, cols_list, dst_shape, view):
            dst = kb.sb(es_, name, dst_shape, BF16)
            return dst

        cm = {}

        def alloc_common(stk):
            cm["stage"] = [kb.sb(stk, f"stage{i}", [128, 1024], F32) for i in range(2)]
            cm["tmpm"] = [kb.sb(stk, f"tmpm{i}", [128, 1024], F32) for i in range(2)]
            cm["junk"] = kb.sb(stk, "junk", [128, 1024], BF16)

        stage_i = [0]

        def load_w_bf16(dst, src_ap, kchunks, ncols, dkey):
            step = 1024
            for k in range(kchunks):
                for n0 in range(0, ncols, step):
                    n1 = min(ncols, n0 + step)
                    i = stage_i[0] % 2
                    stage_i[0] += 1
                    st = cm["stage"][i]
                    kb.dma("sp", st[:, :n1 - n0], src_ap[k * 128:(k + 1) * 128, n0:n1], writes=[f"stage{i}"])
                    eng = "act" if (stage_i[0] % 2) else "dve"
                    if eng == "act":
                        op("act", lambda e: e.activation(out=dst[:, k, n0:n1], in_=st[:, :n1 - n0], func=AF.Copy),
                           reads=[f"stage{i}"], writes=[dkey])
                    else:
                        op("dve", lambda e: e.tensor_copy(out=dst[:, k, n0:n1], in_=st[:, :n1 - n0]),
                           reads=[f"stage{i}"], writes=[dkey])

        def load_small_bf16(dst2d, src_ap, rows, ncols, dkey):
            i = stage_i[0] % 2
            stage_i[0] += 1
            st = cm["stage"][i]
            kb.dma("sp", st[:rows, :ncols], src_ap, writes=[f"stage{i}"])
            op("dve", lambda e: e.tensor_copy(out=dst2d, in_=st[:rows, :ncols]), reads=[f"stage{i}"], writes=[dkey])

        ss = kb.sb(es, "ss", [128, 4], F32)
        rs = kb.sb(es, "rs", [128, 4], F32)
        tmpm_i = [0]

        def norm_mod_T(xt, xkey, nsub, A, B, akeys, hb, hT, tag):
            junk, tmpm = cm["junk"], cm["tmpm"]
            for j in range(nsub):
                op("act", lambda e: e.activation(out=junk[:], in_=xt[:, j, :], func=AF.Square,
                                                 accum_out=ss[:, j:j + 1]),
                   reads=[xkey], writes=["junk", "ss"])
            op("dve", lambda e: e.tensor_scalar(out=rs[:, :nsub], in0=ss[:, :nsub], scalar1=1.0 / D, scalar2=1e-6,
                                                op0=ALU.mult, op1=ALU.add), reads=["ss"], writes=["rs"])
            op("act", lambda e: e.activation(out=rs[:, :nsub], in_=rs[:, :nsub], func=AF.Sqrt), reads=["rs"],
               writes=["rs"])
            op("dve", lambda e: e.reciprocal(out=rs[:, :nsub], in_=rs[:, :nsub]), reads=["rs"], writes=["rs"])
            for j in range(nsub):
                i = tmpm_i[0] % 2
                tmpm_i[0] += 1
                tm = tmpm[i]
                op("dve", lambda e: e.scalar_tensor_tensor(out=tm[:], in0=xt[:, j, :], scalar=rs[:, j:j + 1], in1=A[:],
                                                           op0=ALU.mult, op1=ALU.mult),
                   reads=[xkey, "rs"] + akeys, writes=[f"tmpm{i}"])
                op("dve", lambda e: e.tensor_tensor(out=hb[:, j, :], in0=tm[:], in1=B[:], op=ALU.add),
                   reads=[f"tmpm{i}"] + akeys, writes=[f"hb{j}"])
            for c in range(8):
                pt, pk = kb.pb()
                for j in range(nsub):
                    op("pe", lambda e: e.transpose(out=pt[:, j * 128:(j + 1) * 128], in_=hb[:, j, c * 128:(c + 1) * 128],
                                                   identity=identb[:]),
                       reads=[f"hb{j}", "identb"], writes=[pk])
                eng = "act" if c % 2 == 0 else "dve"
                if eng == "act":
                    op("act", lambda e: e.activation(out=hT[:, c, :nsub * 128], in_=pt[:, :nsub * 128], func=AF.Copy),
                       reads=[pk], writes=[f"{tag}{c}"])
                else:
                    op("dve", lambda e: e.tensor_copy(out=hT[:, c, :nsub * 128], in_=pt[:, :nsub * 128]),
                       reads=[pk], writes=[f"{tag}{c}"])

        def mm(out_ap, lhsT, rhs, start, stop, reads, writes):
            op("pe", lambda e: e.matmul(out_ap, lhsT=lhsT, rhs=rhs, start=start, stop=stop), reads=reads, writes=writes)

        def adaln(es_, layer, want_gate_ctx, es_ctx=None, pre=None):
            t = dict(pre or {})
            for nme in ("A", "B", "G", "Ac", "Bc") + (("Gc",) if want_gate_ctx else ()):
                if nme in t:
                    continue
                stk = es_ctx if (es_ctx is not None and nme != "G") else es_
                t[nme] = kb.sb(stk, f"mod{layer}{nme}", [128, D], F32)
            with ExitStack() as sub:
                sil = kb.sb(sub, "sil", [128, 16], F32)
                sbc = kb.sb(sub, "sbc", [128, 16, 128], F32)
                brow = kb.sb(sub, "brow", [1, 3 * D], F32)
                gbc = kb.sb(sub, "gbc", [128, 1, D], F32)
                wts = [kb.sb(sub, f"adaw{i}", [128, 512], F32) for i in range(3)]
                op("act", lambda e: e.activation(out=sil[:], in_=ccol[:], func=AF.Silu), reads=["ccol"], writes=["sil"])
                for i in range(16):
                    op("dve", lambda e: e.tensor_scalar(out=sbc[:, i, :], in0=onesf[:], scalar1=sil[:, i:i + 1],
                                                        scalar2=None, op0=ALU.mult),
                       reads=["onesf", "sil"], writes=["sbc"])
                kb.dma("sp", brow[:], ada_b[layer], writes=["brow"])
                kb.dma("sp", gbc[:], norm_g[layer].partition_broadcast(128), writes=["gbc"])
                wi = 0
                for n in range(6):
                    px, pxk = kb.pf()
                    pc, pck = kb.pf()
                    for k in range(8):
                        w = wts[wi % 3]
                        wk = f"adaw{wi % 3}"
                        wi += 1
                        kb.dma("sp", w[:], ada_w[layer][k * 128:(k + 1) * 128, n * 512:(n + 1) * 512], writes=[wk])
                        mm(px[:], sbc[:, k, :], w[:], k == 0, False, ["sbc", wk], [pxk])
                        mm(pc[:], sbc[:, 8 + k, :], w[:], k == 0, False, ["sbc", wk], [pck])
                    mm(px[:], onesf[0:1, :], brow[0:1, n * 512:(n + 1) * 512], False, True, ["onesf", "brow"], [pxk])
                    mm(pc[:], onesf[0:1, :], brow[0:1, n * 512:(n + 1) * 512], False, True, ["onesf", "brow"], [pck])
                    part, half = n // 2, n % 2
                    hs = slice(half * 512, (half + 1) * 512)
                    for (p_, pk_, sfx) in ((px, pxk, ""), (pc, pck, "c")):
                        if part == 0:
                            dst = t["B" + sfx]
                            op("act", lambda e: e.activation(out=dst[:, hs], in_=p_[:], func=AF.Copy), reads=[pk_],
                               writes=[f"mod{layer}B{sfx}"])
                        elif part == 1:
                            dst = t["A" + sfx]
                            op("dve", lambda e: e.scalar_tensor_tensor(out=dst[:, hs], in0=p_[:], scalar=1.0,
                                                                       in1=gbc[:, 0, hs], op0=ALU.add, op1=ALU.mult),
                               reads=[pk_, "gbc"], writes=[f"mod{layer}A{sfx}"])
                        else:
                            if ("G" + sfx) in t:
                                dst = t["G" + sfx]
                                op("act", lambda e: e.activation(out=dst[:, hs], in_=p_[:], func=AF.Copy), reads=[pk_],
                                   writes=[f"mod{layer}G{sfx}"])
                            else:
                                op("act", lambda e: e.activation(out=sil[:, 0:8], in_=p_[:, 0:8], func=AF.Copy),
                                   reads=[pk_], writes=["sil"])
                kb.barrier()
            return t

        with ExitStack() as L0:
            alloc_common(L0)
            mod0 = adaln(L0, 0, True)
            ine = kb.sb(L0, "ine", [128, 8, 3072], BF16)
            oute = kb.sb(L0, "oute", [128, 8, D], BF16)
            poolw = kb.sb(L0, "poolw", [128, 4, 128], BF16)
            load_w_bf16(ine, in_e, 8, 3072, "ine")
            load_w_bf16(oute, out_e, 8, D, "oute")
            load_small_bf16(poolw[:].rearrange("p g d -> p (g d)"), pool_w, 128, 512, "poolw")
            invcx = kb.sb(L0, "invcx", [128, 4, 1, 64], F32)
            invcc = kb.sb(L0, "invcc", [128, 4, 1, 256], F32)
            kb.dma("sp", invcx[:, :, 0, :], invcx_in.rearrange("p (g t) -> p g t", g=4), writes=["invcx"])
            kb.dma("sp", invcc[:, :, 0, :], invcc_in.rearrange("p (g t) -> p g t", g=4), writes=["invcc"])
            xt = kb.sb(L0, "xt", [128, 4, D], F32)
            hb = kb.sb(L0, "hb", [128, 4, D], BF16)
            hTs = [kb.sb(L0, f"hT{i}", [128, 8, 512], BF16) for i in range(2)]
            ybuf = kb.sb(L0, "ybuf", [128, 8, 512], BF16)
            xn = [kb.sb(L0, f"xn{i}", [128, D], F32) for i in range(2)]
            t2 = [kb.sb(L0, f"t2_{i}", [128, 512], F32) for i in range(2)]
            sza = kb.sb(L0, "sza", [128, 512], BF16)
            szb = kb.sb(L0, "szb", [128, 512], BF16)
            vb = kb.sb(L0, "vb", [128, 512], F32)
            gb = kb.sb(L0, "gb", [128, 512], F32)
            pm = kb.sb(L0, "pm", [128, 512], BF16)
            dw = kb.sb(L0, "dw", [128, 512], F32)
            dw2 = kb.sb(L0, "dw2", [128, 512], F32)
            geo = {}
            for gname, nrows, rowlen in (("x", 8, 64), ("c", 1, 256)):
                W = rowlen + 32
                g_ = {"nrows": nrows, "rowlen": rowlen, "W": W}
                g_["upad"] = kb.sb(L0, f"upad{gname}", [128, nrows, W], F32)
                g_["sA"] = kb.sb(L0, f"sA{gname}", [128, nrows, W], F32)
                g_["sB"] = kb.sb(L0, f"sB{gname}", [128, nrows, W], F32)
                g_["cvpad"] = kb.sb(L0, f"cvpad{gname}", [128, nrows, rowlen + 2], F32)
                g_["invc"] = invcx if gname == "x" else invcc
                g_["invk"] = "invcx" if gname == "x" else "invcc"
                g_["k"] = gname
                op("dve", lambda e: e.memset(g_["upad"][:], 0.0), writes=[f"upad{gname}"])
                op("dve", lambda e: e.memset(g_["cvpad"][:], 0.0), writes=[f"cvpad{gname}"])
                op("dve", lambda e: e.memset(g_["sA"][:], 0.0), writes=[f"sA{gname}"])
                op("dve", lambda e: e.memset(g_["sB"][:], 0.0), writes=[f"sB{gname}"])
                geo[gname] = g_
            xn_i = [0]

            def l0_front(src, ntok, A, B, akeys, par):
                nsub = ntok // 128
                kb.dma("sp", xt[:, :nsub, :], src.rearrange("(j p) d -> p j d", p=128), writes=["xt"])
                norm_mod_T(xt, "xt", nsub, A, B, akeys, hb, hTs[par], f"hT{par}_")

            def l0_back(src, dst, ntok, g_, G, akeys, par):
                nsub = ntok // 128
                hT = hTs[par]
                hkey = lambda c: f"hT{par}_{c}"
                nrows, rowlen, W, gk = g_["nrows"], g_["rowlen"], g_["W"], g_["k"]
                upad, sA, sB, cvpad = g_["upad"], g_["sA"], g_["sB"], g_["cvpad"]

                def proj(m):
                    ps, pk = kb.pf()
                    for k in range(8):
                        mm(ps[:, :ntok], ine[:, k, m * 128:(m + 1) * 128], hT[:, k, :ntok], k == 0, k == 7,
                           ["ine", hkey(k)], [pk])
                    return ps, pk

                def rows(ap2d):
                    return ap2d.rearrange("p (r l) -> p r l", r=nrows)

                for g in range(4):
                    ps, pk = proj(4 + g)
                    op("act", lambda e: e.activation(out=sza[:, :ntok], in_=ps[:, :ntok], func=AF.Silu), reads=[pk],
                       writes=["sza"])
                    ps, pk = proj(g)
                    op("act", lambda e: e.activation(out=upad[:, :, 16:16 + rowlen], in_=rows(ps[:, :ntok]),
                                                     func=AF.Copy), reads=[pk], writes=[f"upad{gk}"])
                    op("dve", lambda e: e.tensor_tensor(out=sA[:, :, 1:W], in0=upad[:, :, 0:W - 1], in1=upad[:, :, 1:W],
                                                        op=ALU.add), reads=[f"upad{gk}"], writes=[f"sA{gk}"])
                    cur, curk, oth, othk = sA, f"sA{gk}", sB, f"sB{gk}"
                    lo, hi = 1, W
                    for lvl in range(g):
                        sh = 1 << lvl
                        nlo, nhi = lo + sh, hi - sh
                        op("dve", lambda e: e.tensor_tensor(out=oth[:, :, nlo:nhi], in0=cur[:, :, nlo - sh:nhi - sh],
                                                            in1=cur[:, :, nlo + sh:nhi + sh], op=ALU.add),
                           reads=[curk], writes=[othk])
                        cur, curk, oth, othk = oth, othk, cur, curk
                        lo, hi = nlo, nhi
                    op("dve", lambda e: e.tensor_tensor(out=rows(dw[:, :ntok]), in0=cur[:, :, 16:16 + rowlen],
                                                        in1=g_["invc"][:, g, :, :].broadcast_to([128, nrows, rowlen]),
                                                        op=ALU.mult), reads=[curk, g_["invk"]], writes=["dw"])
                    op("dve", lambda e: e.tensor_tensor(out=rows(pm[:, :ntok]), in0=rows(dw[:, :ntok]),
                                                        in1=upad[:, :, 16:16 + rowlen], op=ALU.subtract),
                       reads=["dw", f"upad{gk}"], writes=["pm"])
                    ps, pk = kb.pf()
                    mm(ps[:, :ntok], poolw[:, g, :], pm[:, :ntok], True, True, ["poolw", "pm"], [pk])
                    op("dve", lambda e: e.scalar_tensor_tensor(out=ybuf[:, g, :ntok], in0=ps[:, :ntok],
                                                               scalar=cpc("pool_scale", g), in1=sza[:, :ntok],
                                                               op0=ALU.mult, op1=ALU.mult),
                       reads=[pk, "cp", "sza"], writes=[f"y{g}"])
                for c in range(4):
                    ps, pk = proj(8 + c)
                    op("act", lambda e: e.activation(out=vb[:, :ntok], in_=ps[:, :ntok], func=AF.Copy), reads=[pk],
                       writes=["vb"])
                    ps, pk = proj(12 + c)
                    op("act", lambda e: e.activation(out=gb[:, :ntok], in_=ps[:, :ntok], func=AF.Copy), reads=[pk],
                       writes=["gb"])
                    ps, pk = proj(20 + c)
                    op("act", lambda e: e.activation(out=szb[:, :ntok], in_=ps[:, :ntok], func=AF.Silu), reads=[pk],
                       writes=["szb"])
                    ps, pk = proj(16 + c)
                    op("dve", lambda e: e.tensor_tensor(out=cvpad[:, :, 1:1 + rowlen], in0=rows(ps[:, :ntok]),
                                                        in1=rows(vb[:, :ntok]), op=ALU.mult),
                       reads=[pk, "vb"], writes=[f"cvpad{gk}"])
                    op("dve", lambda e: e.tensor_scalar(out=rows(dw[:, :ntok]), in0=cvpad[:, :, 0:rowlen],
                                                        scalar1=cpc("sconv_w", 0 * 4 + c), scalar2=None, op0=ALU.mult),
                       reads=[f"cvpad{gk}", "cp"], writes=["dw"])
                    op("dve", lambda e: e.scalar_tensor_tensor(out=rows(dw2[:, :ntok]), in0=cvpad[:, :, 1:1 + rowlen],
                                                               scalar=cpc("sconv_w", 1 * 4 + c), in1=rows(dw[:, :ntok]),
                                                               op0=ALU.mult, op1=ALU.add),
                       reads=[f"cvpad{gk}", "cp", "dw"], writes=["dw2"])
                    op("dve", lambda e: e.scalar_tensor_tensor(out=rows(dw[:, :ntok]), in0=cvpad[:, :, 2:2 + rowlen],
                                                               scalar=cpc("sconv_w", 2 * 4 + c), in1=rows(dw2[:, :ntok]),
                                                               op0=ALU.mult, op1=ALU.add),
                       reads=[f"cvpad{gk}", "cp", "dw2"], writes=["dw"])
                    op("dve", lambda e: e.tensor_tensor(out=dw2[:, :ntok], in0=dw[:, :ntok], in1=gb[:, :ntok],
                                                         op=ALU.mult), reads=["dw", "gb"], writes=["dw2"])
                    op("dve", lambda e: e.tensor_tensor(out=ybuf[:, 4 + c, :ntok], in0=dw2[:, :ntok], in1=szb[:, :ntok],
                                                         op=ALU.mult), reads=["dw2", "szb"], writes=[f"y{4 + c}"])
                ykeys = [f"y{c}" for c in range(8)]
                for j in range(nsub):
                    i = xn_i[0] % 2
                    xn_i[0] += 1
                    kb.dma("pool", xn[i][:], src[j * 128:(j + 1) * 128, :], writes=[f"xn{i}"], key=f"xnld{i}")
                    for half in range(2):
                        hs = slice(half * 512, (half + 1) * 512)
                        ps, pk = kb.pf()
                        for c in range(8):
                            mm(ps[:], ybuf[:, c, j * 128:(j + 1) * 128], oute[:, c, hs], c == 0, c == 7,
                               [f"y{c}", "oute"], [pk])
                        tt = t2[half]
                        op("dve", lambda e: e.tensor_tensor(out=tt[:], in0=ps[:], in1=G[:, hs], op=ALU.mult),
                           reads=[pk] + akeys, writes=[f"t2_{half}"])
                        op("dve", lambda e: e.tensor_tensor(out=xn[i][:, hs], in0=tt[:], in1=xn[i][:, hs], op=ALU.add),
                           reads=[f"t2_{half}", f"xn{i}"], writes=[f"xn{i}"])
                    kb.dma("pool", dst[j * 128:(j + 1) * 128, :], xn[i][:], reads=[f"xn{i}"], writes=["X1scr"],
                           key="xnst")

            tiles0 = [(ctx_in, XC1, CTX, geo["c"], mod0["Ac"], mod0["Bc"], mod0["Gc"], ["mod0Ac", "mod0Bc", "mod0Gc"])]
            for it in range(NT):
                tiles0.append((x_in[it * 512:(it + 1) * 512, :], X1[it * 512:(it + 1) * 512, :], 512, geo["x"],
                               mod0["A"], mod0["B"], mod0["G"], ["mod0A", "mod0B", "mod0G"]))
            for ti, (src, dst, ntok, g_, A_, B_, G_, ak_) in enumerate(tiles0):
                if ti == 0:
                    l0_front(src, ntok, A_, B_, ak_, 0)
                if ti + 1 < len(tiles0):
                    nx = tiles0[ti + 1]
                    l0_front(nx[0], nx[2], nx[4], nx[5], nx[7], (ti + 1) % 2)
                l0_back(src, dst, ntok, g_, G_, ak_, ti % 2)
            kb.barrier()

        if "STOP_L0" in dump:
            kb.finish(["X1scr"])
            return nc

        with ExitStack() as L1:
            G1pre = kb.sb(L1, "mod1G", [128, D], F32)
            with ExitStack() as L1a:
                alloc_common(L1a)
                mod1 = adaln(L1, 1, False, es_ctx=L1a, pre={"G": G1pre})
                ino = kb.sb(L1a, "ino", [128, 8, 4096], BF16)
                load_w_bf16(ino, in_o, 8, 4096, "ino")
                xt = kb.sb(L1a, "xt1", [128, 4, D], F32)
                hb = kb.sb(L1a, "hb1", [128, 4, D], BF16)
                hT1s = [kb.sb(L1a, f"hT1_{i}", [128, 8, 512], BF16) for i in range(2)]
                obuf = [kb.sb(L1a, f"obuf{s}", [128, 4, 512], BF16) for s in range(7)]
                p1t = kb.sb(L1a, "p1t", [128, 512], F32)
                sgt = kb.sb(L1a, "sgt", [128, 512], F32)

                def l1_front(src, ntok, A, B, akeys, par):
                    nsub = ntok // 128
                    kb.dma("sp", xt[:, :nsub, :], src.rearrange("(j p) d -> p j d", p=128), reads=["X1scr"],
                           writes=["xt1"])
                    norm_mod_T(xt, "xt1", nsub, A, B, akeys, hb, hT1s[par], f"hU{par}_")

                def l1_back(ntok, col0, lat0, is_ctx, par):
                    hT = hT1s[par]

                    def proj(m):
                        ps, pk = kb.pf()
                        for k in range(8):
                            mm(ps[:, :ntok], ino[:, k, m * 128:(m + 1) * 128], hT[:, k, :ntok], k == 0, k == 7,
                               ["ino", f"hU{par}_{k}"], [pk])
                        return ps, pk

                    for s, S in enumerate((S_u, S_r, S_k, S_v)):
                        for c in range(4):
                            ps, pk = proj(s * 4 + c)
                            if c % 2 == 0:
                                op("act", lambda e: e.activation(out=obuf[s][:, c, :ntok], in_=ps[:, :ntok],
                                                                 func=AF.Copy), reads=[pk], writes=[f"obuf{s}"])
                            else:
                                op("dve", lambda e: e.tensor_copy(out=obuf[s][:, c, :ntok], in_=ps[:, :ntok]),
                                   reads=[pk], writes=[f"obuf{s}"])
                        kb.dma("pool", S[:, col0:col0 + ntok].rearrange("(c p) n -> p c n", p=128),
                               obuf[s][:, :, :ntok], reads=[f"obuf{s}"], writes=["Sstreams"], key=f"obst{s}")
                    if is_ctx:
                        return
                    for c in range(4):
                        ps, pk = proj(16 + c)
                        op("act", lambda e: e.activation(out=obuf[4][:, c, :], in_=ps[:], func=AF.Silu), reads=[pk],
                           writes=["obuf4"])
                        ps, pk = proj(28 + c)
                        op("act", lambda e: e.activation(out=obuf[5][:, c, :], in_=ps[:], func=AF.Silu), reads=[pk],
                           writes=["obuf5"])
                        ps, pk = proj(20 + c)
                        op("act", lambda e: e.activation(out=p1t[:], in_=ps[:], func=AF.Copy), reads=[pk],
                           writes=["p1t"])
                        ps, pk = proj(24 + c)
                        op("act", lambda e: e.activation(out=sgt[:], in_=ps[:], func=AF.Sigmoid), reads=[pk],
                           writes=["sgt"])
                        op("dve", lambda e: e.tensor_tensor(out=obuf[6][:, c, :], in0=p1t[:], in1=sgt[:], op=ALU.mult),
                           reads=["p1t", "sgt"], writes=["obuf6"])
                    kb.dma("pool", S_zc[:, lat0:lat0 + 512].rearrange("(c p) n -> p c n", p=128), obuf[4][:],
                           reads=["obuf4"], writes=["Sstreams"], key="obst4")
                    kb.dma("pool", S_zd[:, lat0:lat0 + 512].rearrange("(c p) n -> p c n", p=128), obuf[5][:],
                           reads=["obuf5"], writes=["Sstreams"], key="obst5")
                    kb.dma("pool", S_glu[:, GPAD + lat0:GPAD + lat0 + 512].rearrange("(c p) n -> p c n", p=128),
                           obuf[6][:], reads=["obuf6"], writes=["Sstreams"], key="obst6")

                tiles1 = [(XC1, CTX, CTX0, 0, mod1["Ac"], mod1["Bc"], ["mod1Ac", "mod1Bc"], True)]
                for it in range(NT):
                    tiles1.append((X1[it * 512:(it + 1) * 512, :], 512, LAT0 + it * 512, it * 512, mod1["A"], mod1["B"],
                                   ["mod1A", "mod1B"], False))
                for ti, (src, ntok, col0, lat0, A_, B_, ak_, isc) in enumerate(tiles1):
                    if ti == 0:
                        l1_front(src, ntok, A_, B_, ak_, 0)
                    if ti + 1 < len(tiles1):
                        nx = tiles1[ti + 1]
                        l1_front(nx[0], nx[1], nx[4], nx[5], nx[6], (ti + 1) % 2)
                    l1_back(ntok, col0, lat0, isc, ti % 2)
                kb.barrier()

            if "STOP_L1A" in dump:
                kb.finish(["Sstreams"])
                return nc

            with ExitStack() as L1b:
                alloc_common(L1b)
                w1b = kb.sb(L1b, "w1b", [128, 4, 64], BF16)
                a1b = kb.sb(L1b, "a1b", [128, 4, 64], BF16)
                g1b = kb.sb(L1b, "g1b", [128, 4, 96], BF16)
                w2b = kb.sb(L1b, "w2b", [64, 512], BF16)
                a2b = kb.sb(L1b, "a2b", [64, 512], BF16)
                g2b = kb.sb(L1b, "g2b", [96, 512], BF16)
                load_small_bf16(w1b[:].rearrange("p c r -> p (c r)"), w1l, 128, 256, "w1b")
                load_small_bf16(a1b[:].rearrange("p c r -> p (c r)"), a1l, 128, 256, "a1b")
                load_small_bf16(g1b[:].rearrange("p c r -> p (c r)"), g1l, 128, 384, "g1b")
                load_small_bf16(w2b[:], w2l, 64, 512, "w2b")
                load_small_bf16(a2b[:], a2l, 64, 512, "a2b")
                load_small_bf16(g2b[:], g2l, 96, 512, "g2b")
                LS = []
                for p_ in range(2):
                    d_ = {"ut": kb.sb(L1b, f"ut{p_}", [128, 4, 514], BF16),
                          "nbt": kb.sb(L1b, f"nbt{p_}", [128, 4, 512], F32),
                          "um": [kb.sb(L1b, f"um{p_}_{j}", [128, 4, 512], BF16) for j in range(3)],
                          "hw": kb.sb(L1b, f"hw{p_}", [64, 512], BF16),
                          "ha": kb.sb(L1b, f"ha{p_}", [64, 512], BF16),
                          "hg": kb.sb(L1b, f"hg{p_}", [96, 512], BF16),
                          "swo": [kb.sb(L1b, f"swo{p_}_{d}", [128, 4, 512], F32) for d in range(2)],
                          "ao": [kb.sb(L1b, f"ao{p_}_{d}", [128, 4, 512], BF16) for d in range(2)],
                          "go": kb.sb(L1b, f"go{p_}", [128, 4, 512], BF16)}
                    LS.append(d_)
                dw_l = kb.sb(L1b, "dw_l", [128, 512], F32)

                def lora_front(col0, ntok, p_):
                    d_ = LS[p_]
                    u, nbt, um = d_["ut"], d_["nbt"], d_["um"]
                    uk, nk = f"ut{p_}", f"nbt{p_}"
                    kb.dma("sp", u[:, :, :ntok + 2], S_u[:, col0 - 1:col0 + ntok + 1].rearrange("(c p) n -> p c n", p=128),
                           reads=["Sstreams", "Shalo"], writes=[uk])
                    op("dve", lambda e: e.tensor_tensor(out=nbt[:, :, :ntok], in0=u[:, :, 0:ntok], in1=u[:, :, 2:ntok + 2],
                                                        op=ALU.add), reads=[uk], writes=[nk])
                    op("dve", lambda e: e.scalar_tensor_tensor(out=nbt[:, :, :ntok], in0=nbt[:, :, :ntok], scalar=0.5,
                                                               in1=u[:, :, 1:ntok + 1], op0=ALU.mult, op1=ALU.subtract),
                       reads=[nk, uk], writes=[nk])
                    for j in range(3):
                        for c in range(4):
                            if c != 3:
                                op("dve", lambda e: e.scalar_tensor_tensor(out=um[j][:, c, :ntok], in0=nbt[:, c, :ntok],
                                                                           scalar=cpc("mu", (3 + j) * 4 + c),
                                                                           in1=u[:, c, 1:ntok + 1], op0=ALU.mult,
                                                                           op1=ALU.add),
                                   reads=[nk, uk, "cp"], writes=[f"um{p_}_{j}"])
                            else:
                                op("dve", lambda e: e.tensor_scalar(out=dw_l[:, :ntok], in0=nbt[:, c, :ntok],
                                                                     scalar1=cpc("mu", (3 + j) * 4 + c), scalar2=None,
                                                                     op0=ALU.mult),
                                   reads=[nk, "cp"], writes=["dw_l"])
                                op("dve", lambda e: e.tensor_tensor(out=um[j][:, c, :ntok], in0=dw_l[:, :ntok],
                                                                     in1=u[:, c, 1:ntok + 1], op=ALU.add),
                                   reads=["dw_l", uk], writes=[f"um{p_}_{j}"])

                def lora_back(tau0, ntok, p_):
                    d_ = LS[p_]
                    um, hw, ha, hg, swo, ao, go = d_["um"], d_["hw"], d_["ha"], d_["hg"], d_["swo"], d_["ao"], d_["go"]
                    for j, (wb_, hid, nh, fn) in enumerate(((w1b, hw, 64, AF.Tanh), (a1b, ha, 64, AF.Copy),
                                                             (g1b, hg, 96, AF.Sigmoid))):
                        ps, pk = kb.pf()
                        for c in range(4):
                            mm(ps[:nh, :ntok], wb_[:, c, :], um[j][:, c, :ntok], c == 0, c == 3,
                               [("w1b", "a1b", "g1b")[j], f"um{p_}_{j}"], [pk])
                        op("act", lambda e: e.activation(out=hid[:, :ntok], in_=ps[:nh, :ntok], func=fn), reads=[pk],
                           writes=[("hw", "ha", "hg")[j] + str(p_)])
                    for d in range(2):
                        for c in range(4):
                            ps, pk = kb.pf()
                            mm(ps[:, :ntok], w2b[32 * d:32 * d + 32, c * 128:(c + 1) * 128], hw[32 * d:32 * d + 32, :ntok],
                               True, True, ["w2b", f"hw{p_}"], [pk])
                            op("act", lambda e: e.activation(out=swo[d][:, c, :ntok], in_=ps[:, :ntok], func=AF.Sigmoid,
                                                             bias=cpc("w0", d * 4 + c), scale=1.0),
                               reads=[pk, "cp"], writes=[f"swo{p_}_{d}"])
                            ps, pk = kb.pf()
                            mm(ps[:, :ntok], a2b[32 * d:32 * d + 32, c * 128:(c + 1) * 128], ha[32 * d:32 * d + 32, :ntok],
                               True, True, ["a2b", f"ha{p_}"], [pk])
                            op("act", lambda e: e.activation(out=ao[d][:, c, :ntok], in_=ps[:, :ntok], func=AF.Sigmoid,
                                                             bias=cpc("a0", d * 4 + c), scale=1.0),
                               reads=[pk, "cp"], writes=[f"ao{p_}_{d}"])
                        kb.dma("pool", S_sw[d][:, tau0:tau0 + ntok].rearrange("(c p) n -> p c n", p=128),
                               swo[d][:, :, :ntok], reads=[f"swo{p_}_{d}"], writes=["Slora"], key=f"swst{d}")
                        kb.dma("pool", S_a[d][:, tau0:tau0 + ntok].rearrange("(c p) n -> p c n", p=128),
                               ao[d][:, :, :ntok], reads=[f"ao{p_}_{d}"], writes=["Slora"], key=f"aost{d}")
                    for c in range(4):
                        ps, pk = kb.pf()
                        mm(ps[:, :ntok], g2b[:, c * 128:(c + 1) * 128], hg[:, :ntok], True, True, ["g2b", f"hg{p_}"], [pk])
                        op("dve", lambda e: e.tensor_copy(out=go[:, c, :ntok], in_=ps[:, :ntok]), reads=[pk],
                           writes=[f"go{p_}"])
                    kb.dma("pool", S_g[:, tau0:tau0 + ntok].rearrange("(c p) n -> p c n", p=128), go[:, :, :ntok],
                           reads=[f"go{p_}"], writes=["Slora"], key="gost")

                ltiles = [(CTX0, 0, CTX)] + [(LAT0 + it * 512, CTX + it * 512, 512) for it in range(NT)]
                lora_front(ltiles[0][0], ltiles[0][2], 0)
                for ti, (col0, tau0, ntok) in enumerate(ltiles):
                    if ti + 1 < len(ltiles):
                        lora_front(ltiles[ti + 1][0], ltiles[ti + 1][2], (ti + 1) % 2)
                    lora_back(tau0, ntok, ti % 2)
                kb.barrier()

            if "STOP_L1B" in dump:
                kb.finish(["Slora", "Sstreams"])
                return nc

            with ExitStack() as SC:
                scan_phase(nc, kb, SC, SEQ, NT, cp, cpc, identf, identb, masks_in, blk, onesf,
                           S_r, S_k, S_v, S_sw, S_a, S_g, S_zc, S_of, S_y, mm, dump=dump)
                kb.barrier()

            if "STOP_SCAN" in dump:
                kb.finish(["Sy", "Sof"])
                return nc

            with ExitStack() as CF:
                cdiag = kb.sb(CF, "cdiag", [128, 124, 128], BF16)
                for j in range(31):
                    for c in range(4):
                        eng = "dve"
                        op(eng, lambda e: e.tensor_scalar(out=cdiag[:, j * 4 + c, :], in0=identf[:],
                                                          scalar1=cpc("conf_dw_w", j * 4 + c), scalar2=None,
                                                          op0=ALU.mult), reads=["identf", "cp"], writes=["cdiag"])
                gl = [kb.sb(CF, f"gl{i}", [128, 512 + 2 * GPAD], BF16) for i in range(3)]
                CS = []
                for p_ in range(2):
                    d_ = {}
                    d_["hc"] = kb.sb(CF, f"hc{p_}", [128, 4, 512], F32)
                    d_["hcb"] = kb.sb(CF, f"hcb{p_}", [128, 4, 512], BF16)
                    d_["hsq"] = kb.sb(CF, f"hsq{p_}", [128, 4, 512], BF16)
                    d_["szd"] = kb.sb(CF, f"szd{p_}", [128, 4, 512], BF16)
                    d_["yc"] = kb.sb(CF, f"yc{p_}", [128, 4, 512], BF16)
                    CS.append(d_)
                mean = kb.sb(CF, "mean", [128, 512], F32)
                msq = kb.sb(CF, "msq", [128, 512], F32)
                rstd = kb.sb(CF, "rstd", [128, 512], F32)
                t1s = [kb.sb(CF, f"t1_{i}", [128, 512], F32) for i in range(2)]
                t3s = [kb.sb(CF, f"t3_{i}", [128, 512], F32) for i in range(2)]
                gi = [0]
                cacc = [kb.sb(CF, f"cacc{i}", [128, 512], F32) for i in range(2)]
                cacc_i = [0]
                lneps = kb.sb(CF, "lneps", [128, 2], F32)
                op("dve", lambda e: e.memset(lneps[:], 1e-5), writes=["lneps"])

                def conf_A(it):
                    p_ = it % 2
                    d_ = CS[p_]
                    t0 = it * 512
                    kb.dma("sp", d_["szd"][:], S_zd[:, t0:t0 + 512].rearrange("(c p) n -> p c n", p=128),
                           reads=["Sstreams"], writes=[f"szd{p_}"])
                    for c in range(4):
                        g_ = gl[gi[0] % 3]
                        gk = f"gl{gi[0] % 3}"
                        gi[0] += 1
                        kb.dma("sp", g_[:], S_glu[c * 128:(c + 1) * 128, t0:t0 + 512 + 2 * GPAD],
                               reads=["Sstreams", "Sgluhalo"], writes=[gk])
                        NPE = 21
                        ps, pk = kb.pf()
                        for j in range(NPE):
                            mm(ps[:], cdiag[:, j * 4 + c, :], g_[:, 64 * j:64 * j + 512], j == 0, j == NPE - 1,
                               ["cdiag", gk], [pk])
                        ac = cacc[cacc_i[0] % 2]
                        ack = f"cacc{cacc_i[0] % 2}"
                        cacc_i[0] += 1
                        for j in range(NPE, 31):
                            if j == NPE:
                                op("dve", lambda e: e.tensor_scalar(out=ac[:], in0=g_[:, 64 * j:64 * j + 512],
                                                                    scalar1=cpc("conf_dw_w", j * 4 + c), scalar2=None,
                                                                    op0=ALU.mult), reads=[gk, "cp"], writes=[ack])
                            else:
                                op("dve", lambda e: e.scalar_tensor_tensor(out=ac[:], in0=g_[:, 64 * j:64 * j + 512],
                                                                           scalar=cpc("conf_dw_w", j * 4 + c), in1=ac[:],
                                                                           op0=ALU.mult, op1=ALU.add),
                                   reads=[gk, "cp", ack], writes=[ack])
                        op("dve", lambda e: e.scalar_tensor_tensor(out=d_["hc"][:, c, :], in0=ps[:],
                                                                   scalar=cpc("conf_dw_b", c), in1=ac[:], op0=ALU.add,
                                                                   op1=ALU.add),
                           reads=[pk, "cp", ack], writes=[f"hc{p_}_{c}"])
                        op("act", lambda e: e.activation(out=d_["hsq"][:, c, :], in_=d_["hc"][:, c, :], func=AF.Square),
                           reads=[f"hc{p_}_{c}"], writes=[f"hsq{p_}_{c}"])
                        op("act", lambda e: e.activation(out=d_["hcb"][:, c, :], in_=d_["hc"][:, c, :], func=AF.Copy),
                           reads=[f"hc{p_}_{c}"], writes=[f"hcb{p_}_{c}"])

                def conf_B(it):
                    p_ = it % 2
                    d_ = CS[p_]
                    t0 = it * 512
                    pm_, pmk = kb.pf()
                    pq_, pqk = kb.pf()
                    for c in range(4):
                        mm(pm_[:], onesb[:], d_["hcb"][:, c, :], c == 0, c == 3, ["onesb", f"hcb{p_}_{c}"], [pmk])
                    for c in range(4):
                        mm(pq_[:], onesb[:], d_["hsq"][:, c, :], c == 0, c == 3, ["onesb", f"hsq{p_}_{c}"], [pqk])
                    op("dve", lambda e: e.tensor_scalar(out=mean[:], in0=pm_[:], scalar1=1.0 / DB, scalar2=None,
                                                        op0=ALU.mult), reads=[pmk], writes=["mean"])
                    op("dve", lambda e: e.tensor_tensor(out=msq[:], in0=mean[:], in1=mean[:], op=ALU.mult),
                       reads=["mean"], writes=["msq"])
                    op("dve", lambda e: e.scalar_tensor_tensor(out=rstd[:], in0=pq_[:], scalar=1.0 / DB, in1=msq[:],
                                                               op0=ALU.mult, op1=ALU.subtract),
                       reads=[pqk, "msq"], writes=["rstd"])
                    op("act", lambda e: e.activation(out=rstd[:], in_=rstd[:], func=AF.Ln, bias=lneps[:, 0:1], scale=1.0),
                       reads=["rstd", "lneps"], writes=["rstd"])
                    op("act", lambda e: e.activation(out=rstd[:], in_=rstd[:], func=AF.Exp, scale=-0.5), reads=["rstd"],
                       writes=["rstd"])
                    for c in range(4):
                        t1, t3 = t1s[c % 2], t3s[c % 2]
                        t1k, t3k = f"t1_{c % 2}", f"t3_{c % 2}"
                        op("dve", lambda e: e.tensor_tensor(out=t1[:], in0=d_["hc"][:, c, :], in1=mean[:], op=ALU.subtract),
                           reads=[f"hc{p_}_{c}", "mean"], writes=[t1k])
                        op("dve", lambda e: e.tensor_tensor(out=t3[:], in0=t1[:], in1=rstd[:], op=ALU.mult),
                           reads=[t1k, "rstd"], writes=[t3k])
                        op("act", lambda e: e.activation(out=t1[:], in_=t3[:], func=AF.Silu,
                                                         bias=cpc("conf_ln_b", c), scale=cpc("conf_ln_g", c)),
                           reads=[t3k, "cp"], writes=[t1k])
                        op("dve", lambda e: e.tensor_tensor(out=d_["yc"][:, c, :], in0=t1[:], in1=d_["szd"][:, c, :],
                                                            op=ALU.mult), reads=[t1k, f"szd{p_}"], writes=[f"yc{p_}"])
                    kb.dma("pool", S_y[DB:D, t0:t0 + 512].rearrange("(c p) n -> p c n", p=128), d_["yc"][:],
                           reads=[f"yc{p_}"], writes=[f"Syc{p_}"], key=f"ycst{p_}")

                fuse_post = "STOP_CONF" not in dump
                if fuse_post:
                    cm["stage"] = [kb.sb(CF, f"stageP{i}", [128, 1024], F32) for i in range(2)]
                    cm["junk"] = kb.sb(CF, "junkP", [128, 1024], BF16)
                    outo = kb.sb(CF, "outo", [128, 8, D], BF16)
                    load_w_bf16(outo, out_o, 8, D, "outo")
                    fg = kb.sb(CF, "fg", [128, 1, D], F32)
                    kb.dma("sp", fg[:], final_g.partition_broadcast(128), writes=["fg"])
                    G1 = mod1["G"]
                    yt = [kb.sb(CF, f"yt{i}", [128, 8, 512], BF16) for i in range(2)]
                    x1t = kb.sb(CF, "x1t", [128, 4, D], F32)
                    x2 = [kb.sb(CF, f"x2_{i}", [128, D], F32) for i in range(2)]
                    ot = [kb.sb(CF, f"ot{i}", [128, D], F32) for i in range(2)]
                    t2 = [kb.sb(CF, f"t2p{i}", [128, 512], F32) for i in range(2)]
                    ssp = kb.sb(CF, "ssp", [128, 2], F32)
                    rsp = kb.sb(CF, "rsp", [128, 2], F32)
                xi = [0]

                def post_tile(it):
                    t0 = it * 512
                    y_ = yt[it % 2]
                    kb.dma("sp", y_[:], S_y[:, t0:t0 + 512].rearrange("(c p) n -> p c n", p=128),
                           reads=["Sy", f"Syc{it % 2}"], writes=[f"yt{it % 2}"])
                    kb.dma("sp", x1t[:], X1[t0:t0 + 512, :].rearrange("(j p) d -> p j d", p=128), reads=["X1scr"],
                           writes=["x1t"])
                    for j in range(4):
                        i = xi[0] % 2
                        xi[0] += 1
                        for half in range(2):
                            hs = slice(half * 512, (half + 1) * 512)
                            ps, pk = kb.pf()
                            for c in range(8):
                                mm(ps[:], y_[:, c, j * 128:(j + 1) * 128], outo[:, c, hs], c == 0, c == 7,
                                   [f"yt{it % 2}", "outo"], [pk])
                            op("dve", lambda e: e.tensor_tensor(out=t2[half][:], in0=ps[:], in1=G1[:, hs], op=ALU.mult),
                               reads=[pk, "mod1G"], writes=[f"t2p{half}"])
                            op("dve", lambda e: e.tensor_tensor(out=x2[i][:, hs], in0=t2[half][:], in1=x1t[:, j, hs],
                                                                op=ALU.add),
                               reads=[f"t2p{half}", "x1t"], writes=[f"x2_{i}"])
                        op("act", lambda e: e.activation(out=cm["junk"][:], in_=x2[i][:], func=AF.Square,
                                                         accum_out=ssp[:, i:i + 1]),
                           reads=[f"x2_{i}"], writes=["junk", f"ssp{i}"])
                        op("dve", lambda e: e.tensor_scalar(out=rsp[:, i:i + 1], in0=ssp[:, i:i + 1], scalar1=1.0 / D,
                                                            scalar2=1e-6, op0=ALU.mult, op1=ALU.add),
                           reads=[f"ssp{i}"], writes=[f"rsp{i}"])
                        op("act", lambda e: e.activation(out=rsp[:, i:i + 1], in_=rsp[:, i:i + 1], func=AF.Sqrt),
                           reads=[f"rsp{i}"], writes=[f"rsp{i}"])
                        op("dve", lambda e: e.reciprocal(out=rsp[:, i:i + 1], in_=rsp[:, i:i + 1]), reads=[f"rsp{i}"],
                           writes=[f"rsp{i}"])
                        op("dve", lambda e: e.scalar_tensor_tensor(out=ot[i][:], in0=x2[i][:], scalar=rsp[:, i:i + 1],
                                                                   in1=fg[:, 0, :], op0=ALU.mult, op1=ALU.mult),
                           reads=[f"x2_{i}", f"rsp{i}", "fg"], writes=[f"ot{i}"])
                        kb.dma("pool", out[t0 + j * 128:t0 + (j + 1) * 128, :], ot[i][:], reads=[f"ot{i}"],
                               writes=["OUT"], key=f"otst{i}")

                conf_A(0)
                for it in range(NT):
                    if it + 1 < NT:
                        conf_A(it + 1)
                    conf_B(it)
                    if fuse_post and it >= 1:
                        post_tile(it - 1)
                if fuse_post:
                    post_tile(NT - 1)
                kb.barrier()

            if "STOP_CONF" in dump:
                kb.finish(["Sy", "Syc0", "Syc1"])
                return nc
        kb.finish(["OUT"])
    return nc


def scan_phase(nc, kb, SC, SEQ, NT, cp, cpc, identf, identb, masks_in, blk, onesf,
               S_r, S_k, S_v, S_sw, S_a, S_g, S_zc, S_of, S_y, mm, dump=()):
    op = kb.op
    masks = kb.sb(SC, "masks", [128, 2, 6, 128], F32)
    kb.dma("sp", masks[:], masks_in.rearrange("p (d m t) -> p d m t", d=2, m=6), writes=["masks"])
    blkb = kb.sb(SC, "blkb", [128, 128], BF16)
    op("dve", lambda e: e.tensor_copy(out=blkb[:], in_=blk[:]), reads=["blk"], writes=["blkb"])
    sqb = kb.sb(SC, "sqb", [128, 512], BF16)
    omka = kb.sb(SC, "omka", [128, 4], F32)
    kac = cp[:, CP["k_a"]:CP["k_a"] + 4]
    op("dve", lambda e: e.tensor_scalar(out=omka[:], in0=kac, scalar1=-1.0, scalar2=1.0, op0=ALU.mult, op1=ALU.add),
       reads=["cp"], writes=["omka"])
    epsb = kb.sb(SC, "epsb", [128, 2], F32)
    op("dve", lambda e: e.memset(epsb[:], 1e-18), writes=["epsb"])
    ones128 = kb.sb(SC, "ones128", [128, 128], F32)
    op("dve", lambda e: e.memset(ones128[:], 1.0), writes=["ones128"])
    NSLOT = 6

    class NS:
        pass

    TMP = NS()
    for n in ("rt", "kt", "vt"):
        setattr(TMP, n, kb.sb(SC, f"{n}T", [128, 514], BF16))
    for n in ("atd", "at0", "vb16", "Af"):
        setattr(TMP, n, kb.sb(SC, f"{n}T", [128, 512], BF16))
    for n in ("swt", "rp", "kp", "vp", "kkr", "kkn", "kd", "bb", "lw", "cs", "csr", "ig", "gp", "tA", "tB", "ksum"):
        setattr(TMP, n, kb.sb(SC, f"{n}T", [128, 512], F32))
    TMPN = set(vars(TMP).keys())
    PB = []
    for b in range(2):
        P = NS()
        P.b = b
        P.k = lambda n, b=b: (f"{n}_pT" if n in TMPN else f"{n}_p{b}")
        for n in TMPN:
            setattr(P, n, getattr(TMP, n))
        for n in ("gt_", "zct", "Bf", "Kf", "yo"):
            setattr(P, n, kb.sb(SC, f"{n}{b}", [128, 512], BF16))
        for n in ("gam", "Rf32", "bonus"):
            setattr(P, n, kb.sb(SC, f"{n}{b}", [128, 512], F32))
        P.AR = kb.sb(SC, f"AR{b}", [128, 4, 2, 128], BF16)
        for n in ("AT", "BT", "KT", "VT"):
            setattr(P, n, kb.sb(SC, f"{n}{b}", [128, 4, 128], BF16))
        PB.append(P)
    SL = []
    for s_ in range(NSLOT):
        L = NS()
        L.s = s_
        L.k = lambda n, s_=s_: f"{n}_s{s_}"
        L.smx = [kb.sb(SC, f"smx{s_}_{h}", [128, 3, 128], BF16) for h in range(2)]
        L.smo = [kb.sb(SC, f"smo{s_}_{h}", [128, 3, 128], BF16) for h in range(2)]
        L.smxT = kb.sb(SC, f"smxT{s_}", [128, 2, 3, 128], BF16)
        L.XX = [kb.sb(SC, f"XX{s_}_{i}", [128, 2, 2, 128], BF16) for i in range(2)]
        L.PPb = kb.sb(SC, f"PPb{s_}", [128, 2, 2, 128], BF16)
        L.Yb = kb.sb(SC, f"Yb{s_}", [128, 2, 2, 128], BF16)
        L.YBb = kb.sb(SC, f"YBb{s_}", [128, 2, 128], BF16)
        L.T128b = kb.sb(SC, f"T128b{s_}", [128, 2, 128], BF16)
        L.mkv = kb.sb(SC, f"mkv{s_}", [128, 128], BF16)
        L.WTc = kb.sb(SC, f"WTc{s_}", [128, 128], BF16)
        L.UlTc = kb.sb(SC, f"UlTc{s_}", [128, 128], BF16)
        L.P0b = kb.sb(SC, f"P0b{s_}", [128, 128], F32)
        L.Gt = kb.sb(SC, f"Gt{s_}", [128, 128], F32)
        L.ofs = kb.sb(SC, f"ofs{s_}", [128, 128], F32)
        L.ofl = kb.sb(SC, f"ofl{s_}", [128, 128], F32)
        L.osum = kb.sb(SC, f"osum{s_}", [128, 128], F32)
        L.onrm = kb.sb(SC, f"onrm{s_}", [128, 128], F32)
        L.st6 = kb.sb(SC, f"st6{s_}", [128, 2, 6], F32)
        L.mv = kb.sb(SC, f"mv{s_}", [128, 2, 2], F32)
        L.rsd = kb.sb(SC, f"rsd{s_}", [128, 2], F32)
        L.yl = kb.sb(SC, f"yl{s_}", [128, 128], F32)
        L.yl2 = kb.sb(SC, f"yl2{s_}", [128, 128], F32)
        SL.append(L)
    ST = [[kb.sb(SC, f"ST{p}_{i}", [128, 128], F32) for i in range(2)] for p in range(2)]
    shared_i = [0]

    def shared_bank():
        return kb.pfx(7)

    def prep_bank():
        return kb.pfx(6)

    coef = kb.sb(SC, "coef", [128, 24], F32)
    coefh = kb.sb(SC, "coefh", [128, 24], BF16)
    coefl = kb.sb(SC, "coefl", [128, 24], F32)
    mu12 = cp[:, CP["mu"]:CP["mu"] + 12]
    op("dve", lambda e: e.tensor_scalar(out=coef[:, 0:12], in0=mu12, scalar1=0.5, scalar2=None, op0=ALU.mult),
       reads=["cp"], writes=["coef"])
    op("dve", lambda e: e.tensor_scalar(out=coef[:, 12:24], in0=mu12, scalar1=-1.0, scalar2=1.0, op0=ALU.mult,
                                        op1=ALU.add), reads=["cp"], writes=["coef"])
    op("dve", lambda e: e.tensor_copy(out=coefh[:], in_=coef[:]), reads=["coef"], writes=["coefh"])
    op("dve", lambda e: e.tensor_copy(out=coefl[:], in_=coefh[:]), reads=["coefh"], writes=["coefl"])
    op("dve", lambda e: e.tensor_tensor(out=coefl[:], in0=coef[:], in1=coefl[:], op=ALU.subtract),
       reads=["coef", "coefl"], writes=["coefl"])
    DGh = kb.sb(SC, "DGh", [128, 24, 128], BF16)
    DGl = kb.sb(SC, "DGl", [128, 24, 128], BF16)
    for i_ in range(24):
        op("dve", lambda e: e.tensor_scalar(out=DGh[:, i_, :], in0=identf[:], scalar1=coef[:, i_:i_ + 1], scalar2=None,
                                            op0=ALU.mult), reads=["identf", "coef"], writes=["DGh"])
        op("dve", lambda e: e.tensor_scalar(out=DGl[:, i_, :], in0=identf[:], scalar1=coefl[:, i_:i_ + 1], scalar2=None,
                                            op0=ALU.mult), reads=["identf", "coefl"], writes=["DGl"])

    def prep_loads(P, hp, d, tau0, n, col0, lat0, bwd):
        k = P.k
        chs = slice(hp * 128, (hp + 1) * 128)
        kb.dma("sp", P.rt[:, :n + 2], S_r[chs, col0 - 1:col0 + n + 1], reads=["Sstreams", "Shalo"], writes=[k("rt")])
        kb.dma("sp", P.kt[:, :n + 2], S_k[chs, col0 - 1:col0 + n + 1], reads=["Sstreams", "Shalo"], writes=[k("kt")])
        kb.dma("sp", P.vt[:, :n + 2], S_v[chs, col0 - 1:col0 + n + 1], reads=["Sstreams", "Shalo"], writes=[k("vt")])
        kb.dma("sp", P.swt[:, :n], S_sw[d][chs, tau0:tau0 + n], reads=["Slora"], writes=[k("swt")])
        kb.dma("sp", P.atd[:, :n], S_a[d][chs, tau0:tau0 + n], reads=["Slora"], writes=[k("atd")])
        if bwd and lat0 is not None:
            kb.dma("sp", P.at0[:, :n], S_a[0][chs, tau0:tau0 + n], reads=["Slora"], writes=[k("at0")])

    def prep_gen(P, hp, d, tau0, n, col0, lat0, bwd):
        k = P.k
        chs = slice(hp * 128, (hp + 1) * 128)
        nch = n // 128
        fin = bwd and lat0 is not None
        if fin:
            kb.dma("sp", P.gt_[:, :n], S_g[chs, tau0:tau0 + n], reads=["Slora"], writes=[k("gt_")])
            kb.dma("sp", P.zct[:, :n], S_zc[chs, lat0:lat0 + n], reads=["Sstreams"], writes=[k("zct")])
        yield
        tA, tB = P.tA, P.tB
        for (xt_, xk, mi) in ((P.rt, "rt", 0), (P.kt, "kt", 1), (P.vt, "vt", 2)):
            ps, pk = prep_bank()
            ih, io = mi * 4 + hp, 12 + mi * 4 + hp
            seq = [(DGh, ih, 0), (DGh, io, 1), (DGh, ih, 2)]
            for qi, (dg, ii, sh) in enumerate(seq):
                mm(ps[:, :n], dg[:, ii, :], xt_[:, sh:sh + n], qi == 0, qi == len(seq) - 1, ["DGh", "DGl", k(xk)], [pk])
            if mi == 0:
                op("act", lambda e: e.activation(out=P.rp[:, :n], in_=ps[:, :n], func=AF.Copy), reads=[pk], writes=[k("rp")])
            elif mi == 1:
                op("act", lambda e: e.activation(out=P.kp[:, :n], in_=ps[:, :n], func=AF.Copy), reads=[pk], writes=[k("kp")])
                op("act", lambda e: e.activation(out=P.kkr[:, :n], in_=ps[:, :n], func=AF.Copy, scale=cpc("k_k", hp)),
                   reads=[pk, "cp"], writes=[k("kkr")])
                op("act", lambda e: e.activation(out=sqb[:, :n], in_=ps[:, :n], func=AF.Square, scale=cpc("k_k", hp)),
                   reads=[pk, "cp"], writes=["sqb"])
            else:
                op("act", lambda e: e.activation(out=P.vp[:, :n], in_=ps[:, :n], func=AF.Copy), reads=[pk], writes=[k("vp")])
                op("act", lambda e: e.activation(out=P.vb16[:, :n], in_=ps[:, :n], func=AF.Copy), reads=[pk],
                   writes=[k("vb16")])
            yield
        ps, pk = prep_bank()
        mm(ps[:, :n], blkb[:], sqb[:, :n], True, True, ["blkb", "sqb"], [pk])
        op("act", lambda e: e.activation(out=tA[:, :n], in_=ps[:, :n], func=AF.Ln, bias=epsb[:, 0:1], scale=1.0),
           reads=[pk, "epsb"], writes=[k("tA")])
        op("act", lambda e: e.activation(out=tA[:, :n], in_=tA[:, :n], func=AF.Exp, scale=-0.5), reads=[k("tA")],
           writes=[k("tA")])
        yield
        op("dve", lambda e: e.tensor_tensor(out=P.kkn[:, :n], in0=P.kkr[:, :n], in1=tA[:, :n], op=ALU.mult),
           reads=[k("kkr"), k("tA")], writes=[k("kkn")])
        yield
        op("act", lambda e: e.activation(out=tB[:, :n], in_=P.atd[:, :n], func=AF.Identity, scale=cpc("k_a", hp),
                                         bias=omka[:, hp:hp + 1]), reads=[k("atd"), "cp", "omka"], writes=[k("tB")])
        op("dve", lambda e: e.tensor_tensor(out=P.kd[:, :n], in0=tB[:, :n], in1=P.kp[:, :n], op=ALU.mult),
           reads=[k("tB"), k("kp")], writes=[k("kd")])
        op("dve", lambda e: e.tensor_tensor(out=P.bb[:, :n], in0=P.kkn[:, :n], in1=P.atd[:, :n], op=ALU.mult),
           reads=[k("kkn"), k("atd")], writes=[k("bb")])
        op("act", lambda e: e.activation(out=P.lw[:, :n], in_=P.swt[:, :n], func=AF.Copy, scale=-EXPM05),
           reads=[k("swt")], writes=[k("lw")])
        yield
        for j in range(nch):
            js = slice(j * 128, (j + 1) * 128)
            op("dve", lambda e: e.tensor_tensor_scan(out=P.cs[:, js], data0=ones128[:], data1=P.lw[:, js], initial=0.0,
                                                     op0=ALU.mult, op1=ALU.add), reads=["ones128", k("lw")],
               writes=[k("cs")])
        cse, csk = P.cs, k("cs")
        if bwd:
            op("dve", lambda e: e.tensor_tensor(out=tB[:, :n], in0=P.lw[:, :n], in1=P.cs[:, :n], op=ALU.subtract),
               reads=[k("lw"), k("cs")], writes=[k("tB")])
            for j in range(nch):
                js = slice(j * 128, (j + 1) * 128)
                op("dve", lambda e: e.tensor_scalar(out=P.csr[:, js], in0=tB[:, js],
                                                    scalar1=P.cs[:, j * 128 + 127:j * 128 + 128], scalar2=None,
                                                    op0=ALU.add), reads=[k("tB"), k("cs")], writes=[k("csr")])
            cse, csk = P.csr, k("csr")
        yield
        op("act", lambda e: e.activation(out=P.gam[:, :n], in_=cse[:, :n], func=AF.Exp), reads=[csk], writes=[k("gam")])
        op("act", lambda e: e.activation(out=P.ig[:, :n], in_=cse[:, :n], func=AF.Exp, scale=-1.0), reads=[csk],
           writes=[k("ig")])
        g3 = lambda t_: t_[:, :n].rearrange("p (j t) -> p j t", t=128)
        if not bwd:
            op("act", lambda e: e.activation(out=g3(P.gp)[:, :, 1:128], in_=g3(P.gam)[:, :, 0:127], func=AF.Copy),
               reads=[k("gam")], writes=[k("gp")])
            op("act", lambda e: e.activation(out=g3(P.gp)[:, :, 0:1], in_=g3(ones128)[:, :nch, 0:1] if False else
                                             ones128[:, 0:nch].unsqueeze(2), func=AF.Copy),
               reads=["ones128"], writes=[k("gp")])
        else:
            op("act", lambda e: e.activation(out=g3(P.gp)[:, :, 0:127], in_=g3(P.gam)[:, :, 1:128], func=AF.Copy),
               reads=[k("gam")], writes=[k("gp")])
            op("act", lambda e: e.activation(out=g3(P.gp)[:, :, 127:128], in_=ones128[:, 0:nch].unsqueeze(2),
                                             func=AF.Copy), reads=["ones128"], writes=[k("gp")])
        yield
        v3 = lambda t_: t_[:, :n].rearrange("p (j t) -> p j t", t=128)
        op("dve", lambda e: e.scalar_tensor_tensor(out=P.Af[:, :n], in0=P.kkn[:, :n], scalar=-1.0, in1=P.gp[:, :n],
                                                   op0=ALU.mult, op1=ALU.mult), reads=[k("kkn"), k("gp")], writes=[k("Af")])
        op("act", lambda e: e.activation(out=P.AR[:, :nch, 0, :], in_=v3(P.Af), func=AF.Copy), reads=[k("Af")],
           writes=[k("AR")])
        op("dve", lambda e: e.tensor_tensor(out=P.Rf32[:, :n], in0=P.rp[:, :n], in1=P.gam[:, :n], op=ALU.mult),
           reads=[k("rp"), k("gam")], writes=[k("Rf32")])
        op("act", lambda e: e.activation(out=P.AR[:, :nch, 1, :], in_=v3(P.Rf32), func=AF.Copy), reads=[k("Rf32")],
           writes=[k("AR")])
        yield
        op("dve", lambda e: e.tensor_tensor(out=P.Bf[:, :n], in0=P.bb[:, :n], in1=P.ig[:, :n], op=ALU.mult),
           reads=[k("bb"), k("ig")], writes=[k("Bf")])
        op("dve", lambda e: e.tensor_tensor(out=P.Kf[:, :n], in0=P.kd[:, :n], in1=P.ig[:, :n], op=ALU.mult),
           reads=[k("kd"), k("ig")], writes=[k("Kf")])
        yield
        for (src, sk, dstT, dk) in ((P.Af, "Af", P.AT, "AT"), (P.Bf, "Bf", P.BT, "BT"), (P.Kf, "Kf", P.KT, "KT"),
                                    (P.vb16, "vb16", P.VT, "VT")):
            pf_, pk = prep_bank()
            pt = pf_[:].bitcast(BF16)
            for j in range(nch):
                op("pe", lambda e: e.transpose(out=pt[:, j * 128:(j + 1) * 128], in_=src[:, j * 128:(j + 1) * 128],
                                               identity=identb[:]), reads=[k(sk), "identb"], writes=[pk])
            op("act", lambda e: e.activation(out=dstT[:, :nch, :].rearrange("p j t -> p (j t)"), in_=pt[:, :n],
                                             func=AF.Copy), reads=[pk], writes=[k(dk)])
            yield
        if fin:
            op("dve", lambda e: e.tensor_tensor(out=tB[:, :n], in0=P.at0[:, :n], in1=P.atd[:, :n], op=ALU.add),
               reads=[k("at0"), k("atd")], writes=[k("tB")])
            op("dve", lambda e: e.tensor_scalar(out=tB[:, :n], in0=tB[:, :n], scalar1=-2.0, scalar2=cpc("k_a", hp),
                                                op0=ALU.add, op1=ALU.mult), reads=[k("tB"), "cp"], writes=[k("tB")])
            op("dve", lambda e: e.scalar_tensor_tensor(out=P.ksum[:, :n], in0=tB[:, :n], scalar=2.0, in1=P.kp[:, :n],
                                                       op0=ALU.add, op1=ALU.mult), reads=[k("tB"), k("kp")],
               writes=[k("ksum")])
            op("dve", lambda e: e.scalar_tensor_tensor(out=sqb[:, :n], in0=P.rp[:, :n], scalar=cpc("r_k", hp),
                                                       in1=P.ksum[:, :n], op0=ALU.mult, op1=ALU.mult),
               reads=[k("rp"), "cp", k("ksum")], writes=["sqb"])
            ps, pk = prep_bank()
            mm(ps[:, :n], blkb[:], sqb[:, :n], True, True, ["blkb", "sqb"], [pk])
            op("dve", lambda e: e.tensor_tensor(out=P.bonus[:, :n], in0=ps[:, :n], in1=P.vp[:, :n], op=ALU.mult),
               reads=[pk, k("vp")], writes=[k("bonus")])
            yield

    chain_turn = [0]

    def chunk_gen(L, P, inst, hp, d, j, lat_row, bwd, fin, want_out, stp, first_of_pass, sc_state):
        k = L.k
        pk_ = P.k
        js = slice(j * 128, (j + 1) * 128)
        bank, bk = kb.pfx(L.s)
        if fin:
            kb.dma("sp", L.ofl[:], S_of[lat_row:lat_row + 128, hp * 128:(hp + 1) * 128], reads=["Sof"], writes=[k("ofl")])
        smx, smo, smxT, PPb = L.smx, L.smo, L.smxT, L.PPb
        flat = lambda t_: t_[:].rearrange("p h a t -> p (h a t)")
        for h in range(2):
            hs = slice(64 * h, 64 * h + 64)
            arh = P.AR[hs, j, :, :].rearrange("p a t -> p (a t)")
            mm(bank[:, 0:256], P.Bf[hs, js], arh, True, False, [pk_("Bf"), pk_("AR")], [bk])
            mm(bank[:, 256:512], P.Kf[hs, js], arh, False, True, [pk_("Kf"), pk_("AR")], [bk])
            op("dve", lambda e: e.tensor_tensor(out=smx[h][:], in0=bank[:, 0:128].unsqueeze(1).broadcast_to([128, 3, 128]),
                                                in1=masks[:, d, 0:3, :], op=ALU.mult), reads=[bk, "masks"],
               writes=[k(f"smx{h}")])
            op("dve", lambda e: e.tensor_tensor(out=smo[h][:], in0=bank[:, 128:512].rearrange("p (m t) -> p m t", m=3),
                                                in1=masks[:, d, 3:6, :], op=ALU.mult), reads=[bk, "masks"],
               writes=[k(f"smo{h}")])
            yield
        pt = bank[:].bitcast(BF16)
        for h in range(2):
            for m in range(3):
                c0 = (h * 3 + m) * 128
                op("pe", lambda e: e.transpose(out=pt[:, c0:c0 + 128], in_=smx[h][:, m, :], identity=identb[:]),
                   reads=[k(f"smx{h}"), "identb"], writes=[bk])
        op("act", lambda e: e.activation(out=smxT[:].rearrange("p h m t -> p (h m t)"), in_=pt[:, 0:768], func=AF.Copy),
           reads=[bk], writes=[k("smxT")])
        yield
        Xc = [smx[0][:, 0, :], smx[1][:, 0, :]]
        XTc = [smxT[:, 0, 0, :], smxT[:, 1, 0, :]]
        xkeys = [k("smx0"), k("smx1"), k("smxT")]

        evi = [L.s]

        def evac_copy(dst_ap, src_ap, dkey):
            if L.s >= 2:
                op("act", lambda e: e.activation(out=dst_ap, in_=src_ap, func=AF.Copy), reads=[bk], writes=[dkey])
            else:
                op("dve", lambda e: e.tensor_copy(out=dst_ap, in_=src_ap), reads=[bk], writes=[dkey])

        def pp_seed(h, first):
            mm(bank[:, h * 256:(h + 1) * 256], identb[:], PPb[:, h, :, :].rearrange("p a t -> p (a t)"), first, False,
               ["identb", k("PPb")], [bk])

        def pp_update():
            evac_copy(flat(PPb), bank[:], k("PPb"))

        def pp_seed_all():
            mm(bank[:], identb[:], flat(PPb), True, False, ["identb", k("PPb")], [bk])

        for i in range(1, 5):
            last = i == 4
            xx = L.XX[i % 2]
            xk = k(f"XX{i % 2}")
            if not last:
                for h in range(2):
                    mm(bank[:, h * 256:h * 256 + 128], XTc[h], Xc[h], h == 0, False, xkeys, [bk])
                    mm(bank[:, h * 256 + 128:(h + 1) * 256], Xc[h], XTc[h], False, h == 1, xkeys, [bk])
                evac_copy(flat(xx), bank[:], xk)
            else:
                for h in range(2):
                    mm(bank[:, h * 256:h * 256 + 128], XTc[h], Xc[h], h == 0, h == 1, xkeys, [bk])
                evac_copy(xx[:, :, 0, :], bank[:].rearrange("p (h a t) -> p h a t", h=2, a=2)[:, :, 0, :], xk)
            Xc = [xx[:, 0, 0, :], xx[:, 1, 0, :]]
            XTc = [xx[:, 0, 1, :], xx[:, 1, 1, :]]
            xkeys = [xk]
            yield
            if i == 1:
                for h in range(2):
                    lo, hi = slice(h * 256, h * 256 + 128), slice(h * 256 + 128, (h + 1) * 256)
                    m0, m0t = smx[h][:, 0, :], smxT[:, h, 0, :]
                    kk_ = [k(f"smx{h}"), k("smxT"), xk, "identb"]
                    mm(bank[:, lo], identb[:], identb[:], h == 0, False, kk_, [bk])
                    mm(bank[:, lo], identb[:], m0, False, False, kk_, [bk])
                    mm(bank[:, lo], identb[:], Xc[h], False, False, kk_, [bk])
                    mm(bank[:, lo], m0t, Xc[h], False, False, kk_, [bk])
                    mm(bank[:, hi], identb[:], identb[:], False, False, kk_, [bk])
                    mm(bank[:, hi], identb[:], m0t, False, False, kk_, [bk])
                    mm(bank[:, hi], Xc[h], identb[:], False, False, kk_, [bk])
                    mm(bank[:, hi], Xc[h], m0t, False, h == 1, kk_, [bk])
            else:
                pp_seed_all()
                for h in range(2):
                    mm(bank[:, h * 256:h * 256 + 128], PPb[:, h, 1, :], Xc[h], False, False, [k("PPb"), xk], [bk])
                    mm(bank[:, h * 256 + 128:(h + 1) * 256], Xc[h], PPb[:, h, 1, :], False, h == 1, [k("PPb"), xk], [bk])
            pp_update()
            yield
        for h in range(2):
            mm(bank[:, h * 256:h * 256 + 128], smxT[:, h, 1, :], PPb[:, h, 0, :], h == 0, False, [k("smxT"), k("PPb")], [bk])
            mm(bank[:, h * 256 + 128:(h + 1) * 256], smx[h][:, 1, :], PPb[:, h, 1, :], False, h == 1,
               [k(f"smx{h}"), k("PPb")], [bk])
        evac_copy(flat(L.Yb), bank[:], k("Yb"))
        yield
        pp_seed_all()
        for h in range(2):
            mm(bank[:, h * 256:h * 256 + 128], PPb[:, h, 1, :], L.Yb[:, h, 0, :], False, False, [k("PPb"), k("Yb")], [bk])
            mm(bank[:, h * 256 + 128:(h + 1) * 256], PPb[:, h, 0, :], L.Yb[:, h, 1, :], False, h == 1,
               [k("PPb"), k("Yb")], [bk])
        pp_update()
        yield
        for h in range(2):
            mm(bank[:, h * 128:(h + 1) * 128], smxT[:, h, 2, :], PPb[:, h, 0, :], h == 0, False, [k("smxT"), k("PPb")], [bk])
        for h in range(2):
            hs = slice(64 * h, 64 * h + 64)
            mm(bank[:, 256 + 64 * h:256 + 64 * h + 64], smo[h][:, 1, :], P.VT[:, j, hs], False, h == 1,
               [k(f"smo{h}"), pk_("VT")], [bk])
        evac_copy(L.YBb[:].rearrange("p h t -> p (h t)"), bank[:, 0:256], k("YBb"))
        evac_copy(L.mkv[:], bank[:, 256:384], k("mkv"))
        yield
        for h in range(2):
            mm(bank[:, h * 128:(h + 1) * 128], identb[:], PPb[:, h, 0, :], h == 0, False, ["identb", k("PPb")], [bk])
            mm(bank[:, h * 128:(h + 1) * 128], PPb[:, h, 1, :], L.YBb[:, h, :], False, h == 1, [k("PPb"), k("YBb")], [bk])
        evac_copy(L.T128b[:].rearrange("p h t -> p (h t)"), bank[:, 0:256], k("T128b"))
        yield
        for h in range(2):
            hs = slice(64 * h, 64 * h + 64)
            mm(bank[:, hs], L.T128b[:, h, :], P.AT[:, j, hs], h == 0, False, [k("T128b"), pk_("AT")], [bk])
            mm(bank[:, 128 + 64 * h:128 + 64 * h + 64], L.T128b[:, h, :], L.mkv[:, hs], False, h == 1,
               [k("T128b"), k("mkv")], [bk])
        op("act", lambda e: e.activation(out=L.WTc[:], in_=bank[:, 0:128], func=AF.Copy), reads=[bk], writes=[k("WTc")])
        op("act", lambda e: e.activation(out=L.UlTc[:], in_=bank[:, 128:256], func=AF.Copy), reads=[bk],
           writes=[k("UlTc")])
        yield
        mm(bank[:, 0:128], L.WTc[:], P.BT[:, j, :], True, not want_out, [k("WTc"), pk_("BT")], [bk])
        if want_out:
            for h in range(2):
                mm(bank[64 * h:64 * h + 64, 128:256], L.WTc[:, 64 * h:64 * h + 64], smo[h][:, 0, :], False, True,
                   [k("WTc"), k(f"smo{h}")], [bk])
        op("dve", lambda e: e.tensor_tensor(out=L.P0b[:], in0=bank[:, 0:128], in1=identf[:], op=ALU.add),
           reads=[bk, "identf"], writes=[k("P0b")])
        if want_out:
            op("dve", lambda e: e.tensor_tensor(out=L.Gt[:], in0=bank[:, 128:256], in1=P.Rf32[:, js], op=ALU.add),
               reads=[bk, pk_("Rf32")], writes=[k("Gt")])
        yield
        while chain_turn[0] != inst:
            yield
        if first_of_pass:
            op("dve", lambda e: e.memset(ST[stp][0][:], 0.0), writes=[f"ST{stp}_0"])
            sc_state[0] = 0
        cur = sc_state[0]
        stc, stck = ST[stp][cur], f"ST{stp}_{cur}"
        stn, stnk = ST[stp][1 - cur], f"ST{stp}_{1 - cur}"
        if want_out:
            po, pok = shared_bank()
            mm(po[:, 0:128], L.Gt[:], stc[:], True, False, [k("Gt"), stck], [pok])
            for h in range(2):
                hs = slice(64 * h, 64 * h + 64)
                mm(po[:, hs], smo[h][:, 0, :], L.UlTc[:, hs], False, False, [k(f"smo{h}"), k("UlTc")], [pok])
                mm(po[:, hs], smo[h][:, 2, :], P.VT[:, j, hs], False, h == 1, [k(f"smo{h}"), pk_("VT")], [pok])
        mm(bank[:, 0:128], P.BT[:, j, :], L.UlTc[:], True, False, [pk_("BT"), k("UlTc")], [bk])
        mm(bank[:, 0:128], P.KT[:, j, :], P.VT[:, j, :], False, False, [pk_("KT"), pk_("VT")], [bk])
        mm(bank[:, 0:128], L.P0b[:], stc[:], False, True, [k("P0b"), stck], [bk])
        gcol = j * 128 if bwd else j * 128 + 127
        op("dve", lambda e: e.scalar_tensor_tensor(out=stn[:], in0=bank[:, 0:128], scalar=P.gam[:, gcol:gcol + 1],
                                                   in1=blk[:], op0=ALU.mult, op1=ALU.mult),
           reads=[bk, pk_("gam"), "blk"], writes=[stnk])
        sc_state[0] = 1 - cur
        chain_turn[0] += 1
        if not want_out:
            return
        if not fin:
            op("act", lambda e: e.activation(out=L.ofs[:], in_=po[:, 0:128], func=AF.Copy), reads=[pok],
               writes=[k("ofs")])
            kb.dma("sp", S_of[lat_row:lat_row + 128, hp * 128:(hp + 1) * 128], L.ofs[:], reads=[k("ofs")],
                   writes=["Sof"], key="ofst")
            return
        op("dve", lambda e: e.tensor_tensor(out=L.osum[:], in0=po[:, 0:128], in1=L.ofl[:], op=ALU.add),
           reads=[pok, k("ofl")], writes=[k("osum")])
        yield
        for h in range(2):
            hs = slice(64 * h, 64 * h + 64)
            op("dve", lambda e: e.bn_stats(out=L.st6[:, h, :], in_=L.osum[:, hs]), reads=[k("osum")], writes=[k("st6")])
            op("dve", lambda e: e.bn_aggr(out=L.mv[:, h, :], in_=L.st6[:, h, :]), reads=[k("st6")], writes=[k("mv")])
        op("dve", lambda e: e.tensor_scalar(out=L.rsd[:], in0=L.mv[:, :, 1], scalar1=64e-5, scalar2=None, op0=ALU.add),
           reads=[k("mv")], writes=[k("rsd")])
        op("act", lambda e: e.activation(out=L.rsd[:], in_=L.rsd[:], func=AF.Sqrt), reads=[k("rsd")], writes=[k("rsd")])
        yield
        op("dve", lambda e: e.reciprocal(out=L.rsd[:], in_=L.rsd[:]), reads=[k("rsd")], writes=[k("rsd")])
        for h in range(2):
            hs = slice(64 * h, 64 * h + 64)
            op("dve", lambda e: e.tensor_scalar(out=L.onrm[:, hs], in0=L.osum[:, hs], scalar1=L.mv[:, h, 0:1],
                                                scalar2=L.rsd[:, h:h + 1], op0=ALU.subtract, op1=ALU.mult),
               reads=[k("osum"), k("mv"), k("rsd")], writes=[k("onrm")])
        yield
        op("pe", lambda e: e.transpose(out=bank[:, 0:128], in_=L.onrm[:], identity=identf[:]),
           reads=[k("onrm"), "identf"], writes=[bk])
        op("dve", lambda e: e.tensor_scalar(out=L.yl[:], in0=bank[:, 0:128], scalar1=cpc("lnx_g", hp),
                                            scalar2=cpc("lnx_b", hp), op0=ALU.mult, op1=ALU.add),
           reads=[bk, "cp"], writes=[k("yl")])
        yield
        op("dve", lambda e: e.tensor_tensor(out=L.yl2[:], in0=L.yl[:], in1=P.bonus[:, js], op=ALU.add),
           reads=[k("yl"), pk_("bonus")], writes=[k("yl2")])
        op("dve", lambda e: e.tensor_tensor(out=L.yl[:], in0=L.yl2[:], in1=P.gt_[:, js], op=ALU.mult),
           reads=[k("yl2"), pk_("gt_")], writes=[k("yl")])
        op("dve", lambda e: e.tensor_tensor(out=P.yo[:, js], in0=L.yl[:], in1=P.zct[:, js], op=ALU.mult),
           reads=[k("yl"), pk_("zct")], writes=[pk_("yo")])

    sups = []
    npass = 0
    for hp in range(4):
        for d in range(2):
            bwd = d == 1
            lst = [(0, CTX, CTX0, None)] + [(CTX + it * 512, 512, LAT0 + it * 512, it * 512) for it in range(NT)]
            if bwd:
                lst = [lst[0]] + lst[1:][::-1]
            for qi, (tau0, n, col0, lat0) in enumerate(lst):
                sups.append((hp, d, tau0, n, col0, lat0, bwd, qi == 0, npass % 2))
            npass += 1
    lim = [int(x[8:]) for x in dump if x.startswith("SCANLIM_")]
    if lim:
        sups = sups[:lim[0]]
    nsup = len(sups)
    pass_state = {}
    prep_done = [False] * nsup
    prep_started = [False] * nsup
    chunks_left = [0] * nsup
    inst_list = []
    for q, (hp, d, tau0, n, col0, lat0, bwd, first, stp) in enumerate(sups):
        nch = n // 128
        order = list(range(nch))[::-1] if bwd else list(range(nch))
        chunks_left[q] = nch
        for oi, j in enumerate(order):
            inst_list.append((q, j, first and oi == 0))
    loads_issued = [False] * nsup

    def issue_loads(q):
        if q < nsup and not loads_issued[q]:
            (hp, d, tau0, n, col0, lat0, bwd, first, stp) = sups[q]
            prep_loads(PB[q % 2], hp, d, tau0, n, col0, lat0, bwd)
            loads_issued[q] = True

    active = []
    free_slots = list(range(NSLOT))
    next_inst = 0
    sc_states = {}
    while next_inst < len(inst_list) or active:
        for q in range(nsup):
            if prep_started[q]:
                continue
            if q >= 2 and chunks_left[q - 2] > 0:
                break
            if q >= 1 and not prep_done[q - 1]:
                break
            (hp, d, tau0, n, col0, lat0, bwd, first, stp) = sups[q]
            issue_loads(q)
            active.append([prep_gen(PB[q % 2], hp, d, tau0, n, col0, lat0, bwd), "prep", q, None])
            prep_started[q] = True
            break
        if next_inst < len(inst_list) and free_slots:
            q, j, first = inst_list[next_inst]
            if prep_done[q]:
                (hp, d, tau0, n, col0, lat0, bwd, firstq, stp) = sups[q]
                slot = free_slots.pop(0)
                fin = bwd and lat0 is not None
                lat_row = None if lat0 is None else lat0 + j * 128
                key = (hp, d)
                if key not in sc_states:
                    sc_states[key] = [0]
                g = chunk_gen(SL[slot], PB[q % 2], next_inst, hp, d, j, lat_row, bwd, fin, lat0 is not None, stp, first,
                              sc_states[key])
                active.append([g, "chunk", q, slot])
                next_inst += 1
        still = []
        for item in active:
            g, kind, q, slot = item
            try:
                next(g)
                still.append(item)
            except StopIteration:
                if kind == "prep":
                    prep_done[q] = True
                    issue_loads(q + 1)
                else:
                    free_slots.append(slot)
                    chunks_left[q] -= 1
                    if chunks_left[q] == 0:
                        (hp, d, tau0, n, col0, lat0, bwd, firstq, stp) = sups[q]
                        if bwd and lat0 is not None:
                            P = PB[q % 2]
                            kb.dma("sp", S_y[hp * 128:(hp + 1) * 128, lat0:lat0 + 512], P.yo[:], reads=[P.k("yo")],
                                   writes=["Sy"], key="yost")
        active = still


def prep_inputs(inp, SEQ):
    f = lambda a: np.ascontiguousarray(np.asarray(a, np.float32))
    shared = {
        "ada_w_e": f(inp["ada_w_e"][0]), "ada_w_o": f(inp["ada_w_o"][0]),
        "ada_b_e": f(inp["ada_b_e"][0][None]), "ada_b_o": f(inp["ada_b_o"][0][None]),
        "norm_e": f(inp["norm_e"][0][None]), "norm_o": f(inp["norm_o"][0][None]),
        "in_e": f(inp["in_e"][0]), "out_e": f(inp["out_e"][0]), "in_o": f(inp["in_o"][0]), "out_o": f(inp["out_o"][0]),
        "pool_w": f(np.asarray(inp["pool_w"][0]).transpose(1, 0, 2).reshape(128, 512)),
        "final_g": f(np.asarray(inp["final_g"])[None]),
    }
    cols = [_cols(inp["pool_scale"][0][None]), _cols(inp["sconv_w"][0]), _cols(inp["rwkv_mu"][0]),
            _cols(inp["k_k"][0][None]), _cols(inp["k_a"][0][None]), _cols(np.asarray(inp["r_k"][0]).reshape(1, 512)),
            _cols(inp["lnx_g"][0][None]), _cols(inp["lnx_b"][0][None]), _cols(inp["conf_dw_b"][0][None]),
            _cols(inp["conf_ln_g"][0][None]), _cols(inp["conf_ln_b"][0][None]), _cols(inp["w0"][0]),
            _cols(inp["a0"][0]), _cols(inp["conf_dw_w"][0])]
    shared["cp"] = np.ascontiguousarray(np.concatenate(cols, axis=1))
    assert shared["cp"].shape == (128, NCP)

    def l1(w):
        w = np.asarray(w, np.float32).transpose(1, 0, 2).reshape(4, 128, -1).transpose(1, 0, 2)
        return np.ascontiguousarray(w.reshape(128, -1))

    shared["w1l"] = l1(inp["w1"][0])
    shared["a1l"] = l1(inp["a1"][0])
    shared["g1l"] = np.ascontiguousarray(
        np.asarray(inp["g1"][0], np.float32).reshape(4, 128, 96).transpose(1, 0, 2).reshape(128, 384))
    shared["w2l"] = f(np.asarray(inp["w2"][0]).reshape(64, 512))
    shared["a2l"] = f(np.asarray(inp["a2"][0]).reshape(64, 512))
    shared["g2l"] = f(inp["g2"][0])
    shared.update(host_consts())
    maps = []
    for b in range(2):
        m = dict(shared)
        m["x"] = f(inp["x"][b][:SEQ])
        m["ctx"] = f(inp["ctx"][b])
        m["ccol"] = np.ascontiguousarray(np.concatenate(
            [np.asarray(inp["c"][b], np.float32).reshape(8, 128).T, np.asarray(inp["c_ctx"], np.float32).reshape(8, 128).T],
            axis=1))
        maps.append(m)
    return maps


_NC_CACHE = {}


def kernel(**inputs):
    SEQ = 8192
    if SEQ not in _NC_CACHE:
        _NC_CACHE[SEQ] = build(SEQ)
    nc = _NC_CACHE[SEQ]
    maps = prep_inputs(inputs, SEQ)
    in_maps = [maps[c // 4] for c in range(8)]
    res = run_bass_kernel_spmd(nc, in_maps, core_ids=list(range(8)))
    return np.stack([res.results[0]["out"], res.results[4]["out"]], axis=0).astype(np.float32)
```
